# Optimizing a Trainium2 kernel written in Bass

```python
import jax
import jax.numpy as jnp
from jax import lax
import numpy as np


D_MODEL = 1024
BATCH = 2
SEQ = 16384
DEPTH = 4

N_HEADS = 8
HEAD_DIM = 128
GDN_WIDTH = N_HEADS * HEAD_DIM
SC_WIDTH = D_MODEL
CONV_W = 3
CHUNK = 64
D_FF = 4 * D_MODEL
LN_EPS = 1e-5
RMS_EPS = 1e-6
L2_EPS = 1e-6
DEEPNORM_ALPHA = (2 * DEPTH) ** 0.25
DEEPNORM_BETA = (8 * DEPTH) ** -0.25

IN_SPLITS = (3 * GDN_WIDTH,
             GDN_WIDTH,
             2 * N_HEADS,
             2 * N_HEADS,
             SC_WIDTH,
             SC_WIDTH,
             SC_WIDTH,
             D_MODEL,
             D_MODEL)
IN_COLS = sum(IN_SPLITS)

kernel_name = 'hybrid_gdn_shortconv_deepnorm_encoder'


def _layer_norm(x, g, b):
    xf = x.astype(jnp.float32)
    mu = jnp.mean(xf, axis=-1, keepdims=True)
    var = jnp.mean(jnp.square(xf - mu), axis=-1, keepdims=True)
    y = (xf - mu) * lax.rsqrt(var + LN_EPS) * g.astype(jnp.float32) + b.astype(jnp.float32)
    return y.astype(x.dtype)


def _l2norm(x):
    return x * lax.rsqrt(jnp.sum(jnp.square(x), axis=-1, keepdims=True) + L2_EPS)


def _centred_dwconv(x, w):
    c = x.shape[-1]
    return lax.conv_general_dilated(
        x, w[:, None, :].astype(x.dtype), window_strides=(1,),
        padding=[(CONV_W // 2, CONV_W // 2)],
        dimension_numbers=('NWC', 'WIO', 'NWC'), feature_group_count=c)


def _split_in(proj):
    idx = []
    acc = 0
    for s in IN_SPLITS[:-1]:
        acc += s
        idx.append(acc)
    return jnp.split(proj, idx, axis=-1)


def _gated_delta_chunked(q, k, v, g, beta):
    b, t, h, dk = q.shape
    dv = v.shape[-1]
    n = t // CHUNK

    def blk(a):
        return a.reshape(b, n, CHUNK, h, -1).transpose(1, 0, 3, 2, 4)

    q, k, v = blk(q), blk(k), blk(v)
    g = g.reshape(b, n, CHUNK, h).transpose(1, 0, 3, 2)
    beta = beta.reshape(b, n, CHUNK, h).transpose(1, 0, 3, 2)
    gc = jnp.cumsum(g, axis=-1)
    incl = jnp.tril(jnp.ones((CHUNK, CHUNK), dtype=bool))
    strict = jnp.tril(jnp.ones((CHUNK, CHUNK), dtype=bool), -1)
    diff = gc[..., :, None] - gc[..., None, :]
    decay = jnp.where(incl, jnp.exp(jnp.where(incl, diff, 0.0)), 0.0)
    kb = k * beta[..., None]
    lmat = jnp.where(strict, jnp.einsum('nbhcd,nbhsd->nbhcs', kb, k) * decay, 0.0)
    eye = jnp.eye(CHUNK, dtype=lmat.dtype)
    tmat = lax.linalg.triangular_solve(lmat + eye, jnp.broadcast_to(eye, lmat.shape),
                                       left_side=True, lower=True, unit_diagonal=True)
    u = jnp.einsum('nbhcs,nbhsd->nbhcd', tmat, v * beta[..., None])
    w = jnp.einsum('nbhcs,nbhsd->nbhcd', tmat, kb * jnp.exp(gc)[..., None])
    attn = jnp.einsum('nbhcd,nbhsd->nbhcs', q, k) * decay
    q_dec = q * jnp.exp(gc)[..., None]
    k_dec = k * jnp.exp(gc[..., -1:] - gc)[..., None]
    g_last = jnp.exp(gc[..., -1])

    def step(state, xs):
        u_i, w_i, a_i, qd_i, kd_i, gl_i = xs
        v_new = u_i - jnp.einsum('bhcd,bhde->bhce', w_i, state)
        o_i = jnp.einsum('bhcd,bhde->bhce', qd_i, state) + jnp.einsum('bhcs,bhse->bhce', a_i, v_new)
        state = state * gl_i[..., None, None] + jnp.einsum('bhcd,bhce->bhde', kd_i, v_new)
        return state, o_i

    s0 = jnp.zeros((b, h, dk, dv), dtype=jnp.float32)
    _, o = lax.scan(step, s0, (u, w, attn, q_dec, k_dec, g_last))
    return o.transpose(1, 0, 3, 2, 4).reshape(b, t, h, dv)


def _gdn_branch(qkv, z, a, bb, conv_w, a_log, dt_bias, norm_w, w_o):
    bsz, t, _ = qkv.shape
    out_dtype = qkv.dtype
    qkv = jax.nn.silu(_centred_dwconv(qkv, conv_w)).astype(jnp.float32)
    q, k, v = jnp.split(qkv, 3, axis=-1)
    q = _l2norm(q.reshape(bsz, t, N_HEADS, HEAD_DIM)) * (HEAD_DIM ** -0.5)
    k = _l2norm(k.reshape(bsz, t, N_HEADS, HEAD_DIM))
    v = v.reshape(bsz, t, N_HEADS, HEAD_DIM)
    a = a.astype(jnp.float32).reshape(bsz, t, 2, N_HEADS)
    bb = bb.astype(jnp.float32).reshape(bsz, t, 2, N_HEADS)
    g = -jnp.exp(a_log.astype(jnp.float32)) * jax.nn.softplus(a + dt_bias.astype(jnp.float32))
    beta = jax.nn.sigmoid(bb)
    o_fwd = _gated_delta_chunked(q, k, v, g[:, :, 0], beta[:, :, 0])
    fl = lambda arr: jnp.flip(arr, axis=1)
    o_bwd = fl(_gated_delta_chunked(fl(q), fl(k), fl(v), fl(g[:, :, 1]), fl(beta[:, :, 1])))
    o = o_fwd + o_bwd
    zf = z.astype(jnp.float32).reshape(bsz, t, N_HEADS, HEAD_DIM)
    o = (o * lax.rsqrt(jnp.mean(jnp.square(o), axis=-1, keepdims=True) + RMS_EPS)
         * norm_w.astype(jnp.float32) * jax.nn.silu(zf))
    return o.reshape(bsz, t, GDN_WIDTH).astype(out_dtype) @ w_o


def _shortconv_branch(sc_b, sc_c, sc_x, conv_w, w_o):
    return (sc_b * _centred_dwconv(sc_c * sc_x, conv_w)) @ w_o


def setup_inputs(seed: int = 0) -> dict:
    key = jax.random.key(seed)
    ks = jax.random.split(key, 20)
    f32 = jnp.float32
    nrm = lambda k, shape, s: jax.random.normal(k, shape, f32) * s
    x = jax.random.normal(ks[0], (BATCH, SEQ, D_MODEL), f32)
    w_in = nrm(ks[1], (DEPTH, D_MODEL, IN_COLS), D_MODEL ** -0.5)
    conv_qkv = nrm(ks[2], (DEPTH, CONV_W, 3 * GDN_WIDTH), CONV_W ** -0.5)
    a_log = jnp.log(jax.random.uniform(ks[3], (DEPTH, 2, N_HEADS), f32, 1.0, 16.0))
    dt = jnp.exp(jax.random.uniform(ks[4], (DEPTH, 2, N_HEADS), f32,
                                    math_log(1e-3), math_log(1e-1)))
    dt_bias = dt + jnp.log(-jnp.expm1(-dt))
    gdn_norm_w = 1.0 + nrm(ks[5], (DEPTH, HEAD_DIM), 0.02)
    w_o_gdn = nrm(ks[6], (DEPTH, GDN_WIDTH, D_MODEL), GDN_WIDTH ** -0.5)
    conv_sc = nrm(ks[7], (DEPTH, CONV_W, SC_WIDTH), CONV_W ** -0.5)
    w_o_sc = nrm(ks[8], (DEPTH, SC_WIDTH, D_MODEL), SC_WIDTH ** -0.5)
    w_out = nrm(ks[9], (DEPTH, D_MODEL, D_MODEL), DEEPNORM_BETA * D_MODEL ** -0.5)
    ln1_g = 1.0 + nrm(ks[10], (DEPTH, D_MODEL), 0.02)
    ln1_b = nrm(ks[11], (DEPTH, D_MODEL), 0.02)
    w_up = nrm(ks[12], (DEPTH, D_MODEL, D_FF), D_MODEL ** -0.5)
    b_up = nrm(ks[13], (DEPTH, D_FF), 0.02)
    w_down = nrm(ks[14], (DEPTH, D_FF, D_MODEL), DEEPNORM_BETA * D_FF ** -0.5)
    b_down = nrm(ks[15], (DEPTH, D_MODEL), 0.02)
    ln2_g = 1.0 + nrm(ks[16], (DEPTH, D_MODEL), 0.02)
    ln2_b = nrm(ks[17], (DEPTH, D_MODEL), 0.02)
    return {'x': x, 'w_in': w_in, 'conv_qkv': conv_qkv, 'a_log': a_log, 'dt_bias': dt_bias,
            'gdn_norm_w': gdn_norm_w, 'w_o_gdn': w_o_gdn, 'conv_sc': conv_sc, 'w_o_sc': w_o_sc,
            'w_out': w_out, 'ln1_g': ln1_g, 'ln1_b': ln1_b, 'w_up': w_up, 'b_up': b_up,
            'w_down': w_down, 'b_down': b_down, 'ln2_g': ln2_g, 'ln2_b': ln2_b}


def math_log(v):
    return float(np.log(v))


def reference(x, w_in, conv_qkv, a_log, dt_bias, gdn_norm_w, w_o_gdn, conv_sc, w_o_sc,
              w_out, ln1_g, ln1_b, w_up, b_up, w_down, b_down, ln2_g, ln2_b):
    for l in range(DEPTH):
        proj = x @ w_in[l]
        qkv, z, a, bb, sc_b, sc_c, sc_x, gate_a, gate_b = _split_in(proj)
        y_a = _gdn_branch(qkv, z, a, bb, conv_qkv[l], a_log[l], dt_bias[l],
                          gdn_norm_w[l], w_o_gdn[l])
        y_b = _shortconv_branch(sc_b, sc_c, sc_x, conv_sc[l], w_o_sc[l])
        mixed = jax.nn.sigmoid(gate_a) * y_a + jax.nn.sigmoid(gate_b) * y_b
        x = _layer_norm(DEEPNORM_ALPHA * x + mixed @ w_out[l], ln1_g[l], ln1_b[l])
        h = jnp.square(jax.nn.relu(x @ w_up[l] + b_up[l]))
        x = _layer_norm(DEEPNORM_ALPHA * x + h @ w_down[l] + b_down[l], ln2_g[l], ln2_b[l])
    return x
```

```python
import contextlib
import numpy as np
import ml_dtypes
import concourse.bass as bass
import concourse.mybir as mybir
from concourse.bass_utils import run_bass_kernel_spmd

F32 = mybir.dt.float32
BF16 = mybir.dt.bfloat16
ALU = mybir.AluOpType
AF = mybir.ActivationFunctionType

D_MODEL = 1024
N_HEADS = 8
HD = 128
DEPTH = 4
D_FF = 4096
IN_COLS = 9248
ALPHA = (2 * DEPTH) ** 0.25
LN_EPS = 1e-5
RMS_EPS = 1e-6
L2_EPS = 1e-6
NEG = -30000.0

EPOCH = 12000
NBC = 4
SAME_ENGINE_SYNC = True


class Slot:
    __slots__ = ("w", "r", "dsem", "dcnt", "name")

    def __init__(self, name=""):
        self.w = None
        self.r = []
        self.dsem = None
        self.dcnt = 0
        self.name = name


class Em:
    ENG = ("pe", "dve", "act", "pool", "sp")

    def __init__(self, nc, stack):
        self.nc = nc
        self.stack = stack
        self.q = {e: [] for e in self.ENG}
        self.cnt = {e: 0 for e in self.ENG}
        self.epoch = {e: 0 for e in self.ENG}
        self.waited = {e: {} for e in self.ENG}
        self.sems = {}
        self.nsem = 0
        self.ninst = {e: 0 for e in self.ENG}

    def sem(self, key):
        if key not in self.sems:
            self.nsem += 1
            self.sems[key] = self.stack.enter_context(self.nc.semaphore(f"s{self.nsem}"))
        return self.sems[key]

    def _waits(self, eng, deps):
        w = []
        for d in deps:
            if d is None:
                continue
            key, val = d
            if key[0] == eng:
                if not SAME_ENGINE_SYNC:
                    continue
                if key[1] == self.epoch[eng] and val > self.cnt[eng]:
                    continue
            if self.waited[eng].get(key, 0) >= val:
                continue
            self.waited[eng][key] = val
            w.append((self.sem(key), val))
        return w

    def op(self, eng, meth, kw, reads=(), writes=(), deps=(), sig=True):
        fn = (meth, kw)
        alld = list(deps)
        for s in reads:
            alld.append(s.w)
        for s in writes:
            alld.append(s.w)
            alld.extend(s.r)
        waits = self._waits(eng, alld)
        if sig:
            self.cnt[eng] += 1
            key = (eng, self.epoch[eng])
            tok = (key, self.cnt[eng])
            semh = self.sem(key)
            if self.cnt[eng] >= EPOCH:
                self.epoch[eng] += 1
                self.cnt[eng] = 0
        else:
            key = (eng, self.epoch[eng])
            tok = (key, self.cnt[eng] + 1)
            semh = None
        self.q[eng].append((waits, fn, semh, 1))
        self.ninst[eng] += 1
        for s in reads:
            s.r.append(tok)
        for s in writes:
            s.w = tok
            s.r = []
        return tok

    def dma(self, eng, out, in_, slot, reads=(), writes=(), deps=()):
        alld = list(deps)
        for s in reads:
            alld.append(s.w)
        for s in writes:
            alld.append(s.w)
            alld.extend(s.r)
        waits = self._waits(eng, alld)
        if slot.dsem is None:
            slot.dsem = ("d", id(slot))
        slot.dcnt += 16
        tok = (slot.dsem, slot.dcnt)
        semh = self.sem(slot.dsem)
        self.q[eng].append((waits, ("dma_start", dict(out=out, in_=in_)), semh, 16))
        self.ninst[eng] += 1
        for s in reads:
            s.r.append(tok)
        for s in writes:
            s.w = tok
            s.r = []
        return tok

    def wait_all(self, eng, toks):
        waits = self._waits(eng, toks)
        self.q[eng].append((waits, None, None, 0))

    def finalize(self):
        nc = self.nc
        block = self.stack.enter_context(nc.Block())
        qs = self.q

        def run(e, lst):
            for waits, fn, semh, inc in lst:
                for (s, v) in waits:
                    e.wait_ge(s, v)
                if fn is None:
                    continue
                ins = getattr(e, fn[0])(**fn[1])
                if semh is not None:
                    ins.then_inc(semh, inc)

        @block.tensor
        def _(e):
            run(e, qs["pe"])

        @block.vector
        def _(e):
            run(e, qs["dve"])

        @block.scalar
        def _(e):
            run(e, qs["act"])

        @block.gpsimd
        def _(e):
            run(e, qs["pool"])

        @block.sync
        def _(e):
            run(e, qs["sp"])


def sb(em, name, shape, dt):
    return em.stack.enter_context(em.nc.sbuf_tensor(name, shape, dt))


def pst(em, name, shape, dt):
    return em.stack.enter_context(em.nc.psum_tensor(name, shape, dt))


def make_consts(em):
    C = {}
    C["idf"] = sb(em, "c_idf", [128, 128], F32)
    C["idb"] = sb(em, "c_idb", [128, 128], BF16)
    C["ones"] = sb(em, "c_ones", [64, 128], F32)
    C["nones"] = sb(em, "c_nones", [64, 128], F32)
    C["tri"] = [sb(em, f"c_tri{d}", [64, 64], F32) for d in range(2)]
    for nm in ("SL", "SU", "IU", "IL"):
        C[nm] = sb(em, f"c_{nm}", [64, NBC * 64], F32)
    s = Slot("consts")
    C["slot"] = s
    idf, idb = C["idf"], C["idb"]
    C["ones128"] = sb(em, "c_ones128", [128, 128], F32)
    em.op("pool", "memset", dict(ap=C["ones128"][:], constant=1.0), writes=[s])
    em.op("pool", "memset", dict(ap=idf[:], constant=0.0), writes=[s])
    em.op("pool", "affine_select", dict(out=idf[:], in_=idf[:], pattern=[[-1, 128]], compare_op=ALU.not_equal,
                                        fill=1.0, base=0, channel_multiplier=1), writes=[s])
    em.op("pool", "tensor_copy", dict(out=idb[:], in_=idf[:]), writes=[s])
    em.op("pool", "memset", dict(ap=C["ones"][:], constant=1.0), writes=[s])
    em.op("pool", "memset", dict(ap=C["nones"][:], constant=-1.0), writes=[s])
    for d in range(2):
        em.op("pool", "memset", dict(ap=C["tri"][d][:], constant=1.0), writes=[s])
    em.op("pool", "affine_select", dict(out=C["tri"][0][:], in_=C["tri"][0][:], pattern=[[1, 64]],
                                        compare_op=ALU.is_ge, fill=0.0, base=0, channel_multiplier=-1), writes=[s])
    em.op("pool", "affine_select", dict(out=C["tri"][1][:], in_=C["tri"][1][:], pattern=[[-1, 64]],
                                        compare_op=ALU.is_ge, fill=0.0, base=0, channel_multiplier=1), writes=[s])
    specs = {"SL": (1, -1, ALU.is_gt), "SU": (-1, 1, ALU.is_gt), "IU": (-1, 1, ALU.is_ge), "IL": (1, -1, ALU.is_ge)}
    for nm, (cm, jm, cmpop) in specs.items():
        m = C[nm]
        em.op("pool", "memset", dict(ap=m[:], constant=0.0), writes=[s])
        em.op("pool", "affine_select", dict(
            out=m[:].rearrange("p (n j) -> p n j", j=64), in_=m[:].rearrange("p (n j) -> p n j", j=64),
            pattern=[[0, NBC], [jm, 64]], compare_op=cmpop, fill=NEG, base=0, channel_multiplier=cm), writes=[s])
    return C


class Ring:
    def __init__(self, em, name, shape, dt, n, psum=False):
        mk = pst if psum else sb
        self.tiles = [mk(em, f"{name}{i}", shape, dt) for i in range(n)]
        self.slots = [Slot(f"{name}{i}") for i in range(n)]
        self.i = 0
        self.n = n

    def next(self):
        t, s = self.tiles[self.i], self.slots[self.i]
        self.i = (self.i + 1) % self.n
        return t, s


def r3(ap, inner):
    return ap.rearrange("p (n j) -> p n j", j=inner)


def conv3(em, engs, Y, ys, X, xs, w, ws, PB):
    em.op(engs[0], "tensor_scalar_mul", dict(out=Y[:, 0:PB], in0=X[:, 1:PB + 1], scalar1=w[:, 1:2]),
          reads=[xs, ws], writes=[ys])
    em.op("dve", "scalar_tensor_tensor", dict(out=Y[:, 0:PB], in0=X[:, 0:PB], scalar=w[:, 0:1], in1=Y[:, 0:PB],
                                                op0=ALU.mult, op1=ALU.add), reads=[xs, ws], writes=[ys])
    em.op("dve", "scalar_tensor_tensor", dict(out=Y[:, 0:PB], in0=X[:, 2:PB + 2], scalar=w[:, 2:3], in1=Y[:, 0:PB],
                                                op0=ALU.mult, op1=ALU.add), reads=[xs, ws], writes=[ys])


def load_halo(em, X, xs, src, t0, PB, T):
    lo, hi, c0, c1 = t0 - 1, t0 + PB + 1, 0, PB + 2
    if lo < 0:
        em.op("pool", "memset", dict(ap=X[:, 0:1], constant=0.0), writes=[xs])
        lo, c0 = 0, 1
    if hi > T:
        em.op("pool", "memset", dict(ap=X[:, PB + 1:PB + 2], constant=0.0), writes=[xs])
        hi, c1 = T, PB + 1
    em.dma("sp", X[:, c0:c1], src(lo, hi), xs, writes=[xs])


def build_p2(em, C, io, B, T):
    NB = NBC
    PB = NB * 64
    W = PB
    NBLK = T // PB
    NCH = T // 64
    par = sb(em, "p2par", [128, 16], F32)
    par2 = sb(em, "p2par2", [64, 8], F32)
    ps_ = Slot("par")
    em.dma("sp", par[:, 0:9], io["cw"], ps_, writes=[ps_])
    em.dma("sp", par[:, 9:12], io["csc"], ps_, writes=[ps_])
    em.dma("sp", par[:, 12:13], io["nw"], ps_, writes=[ps_])
    em.dma("sp", par2[:, 0:2], io["alog"], ps_, writes=[ps_])
    em.dma("sp", par2[:, 2:4], io["dtb"], ps_, writes=[ps_])
    em.op("act", "activation", dict(out=par2[:, 4:6], in_=par2[:, 0:2], func=AF.Exp), writes=[ps_])
    em.op("dve", "tensor_scalar_mul", dict(out=par2[:, 4:6], in0=par2[:, 4:6], scalar1=-1.0), writes=[ps_])
    cs = C["slot"]
    ones128 = C["ones128"]
    idf = C["idf"]
    rPc = Ring(em, "p2P", [128, 512], F32, 5, psum=True)

    rX = Ring(em, "p2aX", [128, PB + 2], F32, 6)
    rY = Ring(em, "p2aY", [128, PB], F32, 4)
    rQ = Ring(em, "p2aQ", [128, PB], F32, 3)
    rO = Ring(em, "p2aO", [128, PB], BF16, 4)
    for b in range(B):
        for blk in range(NBLK):
            t0 = blk * PB
            g0 = b * T + t0
            Xs = []
            for kind in range(3):
                X, xs = rX.next()
                load_halo(em, X, xs, lambda a, c, kind=kind: io["xin"](kind, b, a, c), t0, PB, T)
                Xs.append((X, xs))
            engsets = [("dve", "dve", "dve"), ("pool", "pool", "pool"), ("dve", "pool", "dve")]
            for kind in range(3):
                X, xs = Xs[kind]
                Y, ys = rY.next()
                conv3(em, engsets[kind], Y, ys, X, xs, par[:, kind * 3:kind * 3 + 3], ps_, PB)
                em.op("act", "activation", dict(out=Y[:], in_=Y[:], func=AF.Silu), writes=[ys])
                if kind < 2:
                    Q, qs = rQ.next()
                    em.op("pool", "tensor_tensor", dict(out=Q[:], in0=Y[:], in1=Y[:], op=ALU.mult), reads=[ys], writes=[qs])
                    P, pss = rPc.next()
                    em.op("pe", "matmul", dict(out=P[:, 0:W], lhsT=ones128[:], rhs=Q[:], start=True, stop=True),
                          reads=[qs, cs], writes=[pss])
                    if kind == 0:
                        em.op("act", "activation", dict(out=Q[:], in_=P[:, 0:W], func=AF.Sqrt, scale=float(HD),
                                                        bias=float(HD) * L2_EPS), reads=[pss], writes=[qs])
                    else:
                        em.op("act", "activation", dict(out=Q[:], in_=P[:, 0:W], func=AF.Sqrt, bias=L2_EPS),
                              reads=[pss], writes=[qs])
                    em.op("dve", "reciprocal", dict(out=Q[:], in_=Q[:]), writes=[qs])
                    O, os_ = rO.next()
                    em.op("dve" if kind == 0 else "pool", "tensor_tensor", dict(out=O[:], in0=Y[:], in1=Q[:], op=ALU.mult),
                          reads=[ys, qs], writes=[os_])
                    dst = io["qn"] if kind == 0 else io["kn"]
                    em.dma("sp", dst[:, g0:g0 + PB], O[:], os_, reads=[os_])
                else:
                    em.dma("sp", io["vv"][:, g0:g0 + PB], Y[:], ys, reads=[ys])
            Xb, xbs = rX.next()
            em.dma("sp", Xb[:, 0:PB], io["xin"](4, b, t0, t0 + PB), xbs, writes=[xbs])
            Xc, xcs = rX.next()
            load_halo(em, Xc, xcs, lambda a, c: io["xin"](5, b, a, c), t0, PB, T)
            Xx, xxs = rX.next()
            load_halo(em, Xx, xxs, lambda a, c: io["xin"](6, b, a, c), t0, PB, T)
            em.op("pool", "tensor_tensor", dict(out=Xc[:], in0=Xc[:], in1=Xx[:], op=ALU.mult), reads=[xxs], writes=[xcs])
            Y, ys = rY.next()
            conv3(em, ("dve", "dve", "pool"), Y, ys, Xc, xcs, par[:, 9:12], ps_, PB)
            O, os_ = rO.next()
            em.op("dve", "tensor_tensor", dict(out=O[:], in0=Y[:], in1=Xb[:, 0:PB], op=ALU.mult),
                  reads=[ys, xbs], writes=[os_])
            em.dma("sp", io["uscT"][:, g0:g0 + PB], O[:], os_, reads=[os_])
    prep_done = []
    for r in (rO, rY):
        for s in r.slots:
            prep_done.extend(s.r)
    em.wait_all("sp", prep_done)

    scans = [(b, d) for b in range(B) for d in range(2)]
    NS = len(scans)
    abt = sb(em, "p2abt", [64, B, NCH, 4], F32)
    abs_ = Slot("abt")
    for b in range(B):
        em.dma("sp", abt[:, b, :, :], io["abc"][b * T:(b + 1) * T, :].rearrange("(n i) q -> i n q", i=64), abs_,
               writes=[abs_])
    colnames = ("ngc", "col1", "bege", "kds", "beta")
    col = [{k: sb(em, f"p2c_{k}{si}", [64, NCH], F32) for k in colnames} for si in range(NS)]
    cols = [Slot(f"col{si}") for si in range(NS)]
    tmpA = sb(em, "p2c_tA", [64, NCH], F32)
    tmpG = sb(em, "p2c_tG", [64, NCH], F32)
    ts_ = Slot("coltmp")
    assert NCH <= 512
    for si, (b, d) in enumerate(scans):
        c = col[si]
        s_ = cols[si]
        a_in = abt[:, b, :, d]
        b_in = abt[:, b, :, 2 + d]
        em.op("act", "activation", dict(out=tmpG[:], in_=a_in, func=AF.Exp, bias=par2[:, 2 + d:3 + d]),
              reads=[abs_, ps_], writes=[ts_])
        em.op("act", "activation", dict(out=tmpG[:], in_=tmpG[:], func=AF.Ln, bias=1.0), writes=[ts_])
        em.op("dve", "tensor_scalar_mul", dict(out=tmpG[:], in0=tmpG[:], scalar1=par2[:, 4 + d:5 + d]), writes=[ts_])
        P, pss = rPc.next()
        em.op("pe", "matmul", dict(out=P[0:64, 0:NCH], lhsT=C["tri"][d][:], rhs=tmpG[:], start=True, stop=True),
              reads=[ts_, cs], writes=[pss])
        P2, pss2 = rPc.next()
        em.op("pe", "matmul", dict(out=P2[0:64, 0:NCH], lhsT=C["ones"][:, 0:64], rhs=tmpG[:], start=True, stop=True),
              reads=[ts_, cs], writes=[pss2])
        em.op("dve", "tensor_scalar_mul", dict(out=c["ngc"][:], in0=P[0:64, 0:NCH], scalar1=-1.0), reads=[pss], writes=[s_])
        em.op("dve", "tensor_tensor", dict(out=c["kds"][:], in0=P2[0:64, 0:NCH], in1=c["ngc"][:], op=ALU.add),
              reads=[pss2], writes=[s_])
        em.op("act", "activation", dict(out=c["kds"][:], in_=c["kds"][:], func=AF.Exp), writes=[s_])
        em.op("act", "activation", dict(out=tmpA[:], in_=b_in, func=AF.Exp, scale=-1.0), reads=[abs_], writes=[ts_])
        em.op("act", "activation", dict(out=tmpA[:], in_=tmpA[:], func=AF.Ln, bias=1.0), writes=[ts_])
        em.op("dve", "scalar_tensor_tensor", dict(out=c["col1"][:], in0=tmpA[:], scalar=-1.0, in1=c["ngc"][:],
                                                  op0=ALU.mult, op1=ALU.subtract), reads=[ts_], writes=[s_])
        em.op("act", "activation", dict(out=c["beta"][:], in_=tmpA[:], func=AF.Exp, scale=-1.0), reads=[ts_], writes=[s_])
        em.op("act", "activation", dict(out=c["bege"][:], in_=c["col1"][:], func=AF.Exp), writes=[s_])

    rIn = Ring(em, "p2bIn", [128, 2, W], BF16, 2)
    rInV = Ring(em, "p2bInV", [128, W], F32, 2)
    rDX = Ring(em, "p2bDX", [64, 2, W], F32, 2)
    rT3 = Ring(em, "p2bT3", [64, 3, W], F32, 2)
    rEG = Ring(em, "p2bEG", [128, W], F32, 2)
    rM = Ring(em, "p2bM", [64, 2, W], F32, 3)
    rPm = Ring(em, "p2bPm", [64, W], F32, 3)
    rTt = Ring(em, "p2bTt", [64, W], BF16, 2)
    rKV = Ring(em, "p2bKV", [64, 2, NB * 128], BF16, 2)
    res = [[dict(u=sb(em, f"p2r_u{si}{p}", [64, NB * 128], F32), wT=sb(em, f"p2r_w{si}{p}", [128, W], F32),
                 attnT=sb(em, f"p2r_a{si}{p}", [64, W], BF16), qdT=sb(em, f"p2r_q{si}{p}", [128, W], F32),
                 kd=sb(em, f"p2r_k{si}{p}", [64, NB * 128], BF16), gl=sb(em, f"p2r_g{si}{p}", [128, NB], F32),
                 slot=Slot(f"res{si}{p}")) for p in range(2)] for si in range(NS)]
    S = [sb(em, f"p2S{si}", [128, 128], F32) for si in range(NS)]
    Ss = [Slot(f"S{si}") for si in range(NS)]
    vnew = [sb(em, f"p2vn{si}", [64, 128], BF16) for si in range(NS)]
    vns = [Slot(f"vn{si}") for si in range(NS)]
    ps_vn = pst(em, "p2ps_vn", [64, NS, 128], F32)
    ps_ds = pst(em, "p2ps_ds", [128, NS, 128], F32)
    ps_o = pst(em, "p2ps_o", [128, NS, 2, 64], F32)
    pvs = [Slot() for _ in range(NS)]
    pds = [Slot() for _ in range(NS)]
    pos = [[Slot(), Slot()] for _ in range(NS)]
    rOb = Ring(em, "p2bOb", [128, W], F32, NS + 2)
    for si in range(NS):
        em.op("pool", "memset", dict(ap=S[si][:], constant=0.0), writes=[Ss[si]])
    idb3 = idf[0:64, 0:64].unsqueeze(1).broadcast_to([64, NB, 64])

    def bc(colap, n0, n, inner):
        return colap[:, n0:n0 + n].unsqueeze(2).broadcast_to([64, n, inner])

    def precompute(si, blk, R):
        b, d = scans[si]
        c = col[si]
        cslot = cols[si]
        n0 = blk * NB
        g0 = b * T + blk * PB
        rs = R["slot"]
        mask1, mask2, mask3 = (C["SL"], C["SU"], C["IU"]) if d == 0 else (C["SU"], C["SL"], C["IL"])
        last = 63 if d == 0 else 0
        KQ, kqs = rIn.next()
        em.dma("sp", KQ[:, 0, :], io["kn"][:, g0:g0 + PB], kqs, writes=[kqs])
        em.dma("sp", KQ[:, 1, :], io["qn"][:, g0:g0 + PB], kqs, writes=[kqs])
        V, vs_ = rInV.next()
        em.dma("sp", V[:], io["vv"][:, g0:g0 + PB], vs_, writes=[vs_])
        DX, dxs = rDX.next()
        em.op("dve", "tensor_tensor", dict(out=r3(DX[:, 0, :], 64), in0=idb3, in1=bc(c["ngc"], n0, NB, 64), op=ALU.mult),
              reads=[cslot, cs], writes=[dxs])
        em.op("pool", "tensor_tensor", dict(out=r3(DX[:, 1, :], 64), in0=idb3, in1=bc(c["col1"], n0, NB, 64), op=ALU.mult),
              reads=[cslot, cs], writes=[dxs])
        yield
        T3, t3s = rT3.next()
        specs = [(C["ones"], 0, mask1, c["col1"]), (C["ones"], 1, mask2, c["ngc"]), (C["nones"], 0, mask3, c["ngc"])]
        for k, (lh, dxi, msk, cadd) in enumerate(specs):
            PA, pas = rPc.next()
            em.op("pe", "matmul", dict(out=PA[0:64, 0:W], lhsT=lh[:, 0:64], rhs=DX[:, dxi, :], start=True, stop=False),
                  reads=[dxs, cs], writes=[pas], sig=False)
            em.op("pe", "matmul", dict(out=PA[0:64, 0:W], lhsT=idf[0:64, 0:64], rhs=msk[:], start=False, stop=True),
                  reads=[cs], writes=[pas])
            em.op("dve", "tensor_tensor", dict(out=r3(T3[:, k, :], 64), in0=r3(PA[0:64, 0:W], 64), in1=bc(cadd, n0, NB, 64),
                                               op=ALU.add), reads=[pas, cslot], writes=[t3s])
        yield
        PE_, pes = rPc.next()
        em.op("pe", "matmul", dict(out=PE_[:, 0:W], lhsT=C["nones"][:, :], rhs=DX[:, 0, :], start=True, stop=True),
              reads=[dxs, cs], writes=[pes])
        EG, egs = rEG.next()
        em.op("act", "activation", dict(out=EG[:], in_=PE_[:, 0:W], func=AF.Exp), reads=[pes], writes=[egs])
        em.op("act", "activation", dict(out=T3[:], in_=T3[:], func=AF.Exp), writes=[t3s])
        yield
        Pkk, pkks = rPc.next()
        for n in range(NB):
            sl = slice(n * 64, (n + 1) * 64)
            em.op("pe", "matmul", dict(out=Pkk[0:64, sl], lhsT=KQ[:, 0, sl], rhs=KQ[:, 0, sl], start=True, stop=True),
                  reads=[kqs], writes=[pkks], sig=(n == NB - 1))
        Pqk, pqks = rPc.next()
        for n in range(NB):
            sl = slice(n * 64, (n + 1) * 64)
            em.op("pe", "matmul", dict(out=Pqk[0:64, sl], lhsT=KQ[:, 0, sl], rhs=KQ[:, 1, sl], start=True, stop=True),
                  reads=[kqs], writes=[pqks], sig=(n == NB - 1))
        M, ms = rM.next()
        em.op("dve", "tensor_tensor", dict(out=M[:, 1, :], in0=Pkk[0:64, 0:W], in1=T3[:, 0, :], op=ALU.mult),
              reads=[pkks, t3s], writes=[ms])
        em.op("dve", "tensor_tensor", dict(out=M[:, 0, :], in0=Pkk[0:64, 0:W], in1=T3[:, 1, :], op=ALU.mult),
              reads=[pkks, t3s], writes=[ms])
        em.op("dve", "tensor_tensor", dict(out=R["attnT"][:], in0=Pqk[0:64, 0:W], in1=T3[:, 2, :], op=ALU.mult),
              reads=[pqks, t3s], writes=[rs])
        em.op("pool", "tensor_copy", dict(out=R["gl"][:], in_=r3(EG[:], 64)[:, :, last]), reads=[egs], writes=[rs])
        em.op("pool", "tensor_tensor", dict(out=R["qdT"][:], in0=KQ[:, 1, :], in1=EG[:], op=ALU.mult),
              reads=[egs, kqs], writes=[rs])
        Pm, pms = rPm.next()
        em.op("pool", "tensor_tensor", dict(out=r3(Pm[:], 64), in0=idb3, in1=r3(M[:, 0, :], 64), op=ALU.subtract),
              reads=[ms, cs], writes=[pms])
        yield
        KV, kvs = rKV.next()
        Pt, pts = rPc.next()
        for n in range(NB):
            em.op("pe", "matmul", dict(out=Pt[0:64, n * 128:(n + 1) * 128], lhsT=KQ[:, 0, n * 64:(n + 1) * 64],
                                       rhs=C["idb"][:], start=True, stop=True), reads=[kqs, cs], writes=[pts], sig=(n == NB - 1))
        em.op("dve", "tensor_tensor", dict(out=r3(KV[:, 0, :], 128), in0=r3(Pt[0:64, 0:NB * 128], 128),
                                           in1=bc(c["bege"], n0, NB, 128), op=ALU.mult), reads=[pts, cslot], writes=[kvs])
        em.op("dve", "tensor_tensor", dict(out=r3(R["kd"][:], 128), in0=r3(Pt[0:64, 0:NB * 128], 128),
                                           in1=bc(c["kds"], n0, NB, 128), op=ALU.mult), reads=[pts, cslot], writes=[rs])
        Pt2, pts2 = rPc.next()
        for n in range(NB):
            em.op("pe", "matmul", dict(out=Pt2[0:64, n * 128:(n + 1) * 128], lhsT=V[:, n * 64:(n + 1) * 64],
                                       rhs=idf[:], start=True, stop=True), reads=[vs_, cs], writes=[pts2], sig=(n == NB - 1))
        em.op("dve", "tensor_tensor", dict(out=r3(KV[:, 1, :], 128), in0=r3(Pt2[0:64, 0:NB * 128], 128),
                                           in1=bc(c["beta"], n0, NB, 128), op=ALU.mult), reads=[pts2, cslot], writes=[kvs])
        yield
        for r in range(5):
            lastr = (r == 4)
            M2, m2s = rM.next()
            if not lastr:
                Pa, pas = rPc.next()
                for n in range(NB):
                    sl = slice(n * 64, (n + 1) * 64)
                    em.op("pe", "matmul", dict(out=Pa[0:64, sl], lhsT=M[:, 1, sl], rhs=M[:, 0, sl], start=True, stop=True),
                          reads=[ms], writes=[pas], sig=(n == NB - 1))
                em.op("act", "copy", dict(out=M2[:, 0, :], in_=Pa[0:64, 0:W]), reads=[pas], writes=[m2s])
            Pb, pbs = rPc.next()
            for n in range(NB):
                sl = slice(n * 64, (n + 1) * 64)
                em.op("pe", "matmul", dict(out=Pb[0:64, sl], lhsT=M[:, 0, sl], rhs=M[:, 1, sl], start=True, stop=True),
                      reads=[ms], writes=[pbs], sig=(n == NB - 1))
            em.op("dve", "tensor_copy", dict(out=M2[:, 1, :], in_=Pb[0:64, 0:W]), reads=[pbs], writes=[m2s])
            Pp, pps = rPc.next()
            for n in range(NB):
                sl = slice(n * 64, (n + 1) * 64)
                em.op("pe", "matmul", dict(out=Pp[0:64, sl], lhsT=M2[:, 1, sl], rhs=Pm[:, sl], start=True, stop=True),
                      reads=[m2s, pms], writes=[pps], sig=(n == NB - 1))
            Pm2, pms2 = rPm.next()
            em.op("dve", "tensor_tensor", dict(out=Pm2[:], in0=Pp[0:64, 0:W], in1=Pm[:], op=ALU.add),
                  reads=[pps, pms], writes=[pms2])
            M, ms = M2, m2s
            Pm, pms = Pm2, pms2
            yield
        Tt, tts = rTt.next()
        em.op("act", "copy", dict(out=Tt[:], in_=Pm[:]), reads=[pms], writes=[tts])
        Pu, pus = rPc.next()
        for n in range(NB):
            em.op("pe", "matmul", dict(out=Pu[0:64, n * 128:(n + 1) * 128], lhsT=Tt[:, n * 64:(n + 1) * 64],
                                       rhs=KV[:, 1, n * 128:(n + 1) * 128], start=True, stop=True),
                  reads=[tts, kvs], writes=[pus], sig=(n == NB - 1))
        em.op("act", "copy", dict(out=R["u"][:], in_=Pu[0:64, 0:NB * 128]), reads=[pus], writes=[rs])
        Pw, pws = rPc.next()
        for n in range(NB):
            em.op("pe", "matmul", dict(out=Pw[:, n * 64:(n + 1) * 64], lhsT=KV[:, 0, n * 128:(n + 1) * 128],
                                       rhs=Tt[:, n * 64:(n + 1) * 64], start=True, stop=True),
                  reads=[tts, kvs], writes=[pws], sig=(n == NB - 1))
        em.op("dve", "tensor_copy", dict(out=R["wT"][:], in_=Pw[:, 0:W]), reads=[pws], writes=[rs])
        yield

    def scan_step(si, n, R, Ob, obs):
        b, d = scans[si]
        rs = R["slot"]
        nn = n if d == 0 else NB - 1 - n
        sl = slice(nn * 64, (nn + 1) * 64)
        sle = slice(nn * 128, (nn + 1) * 128)
        pr = n % 2
        em.op("pe", "matmul", dict(out=ps_vn[:, si, :], lhsT=R["wT"][:, sl], rhs=S[si][:], start=True, stop=True),
              reads=[rs, Ss[si]], writes=[pvs[si]])
        em.op("dve", "tensor_tensor", dict(out=vnew[si][:], in0=R["u"][:, sle], in1=ps_vn[:, si, :], op=ALU.subtract),
              reads=[rs, pvs[si]], writes=[vns[si]])
        em.op("pe", "matmul", dict(out=ps_o[:, si, pr, :], lhsT=S[si][:], rhs=R["qdT"][:, sl], start=True, stop=False),
              reads=[rs, Ss[si]], writes=[pos[si][pr]], sig=False)
        em.op("pe", "matmul", dict(out=ps_o[:, si, pr, :], lhsT=vnew[si][:], rhs=R["attnT"][:, sl], start=False, stop=True),
              reads=[rs, vns[si]], writes=[pos[si][pr]], sig=False)
        em.op("pe", "matmul", dict(out=ps_ds[:, si, :], lhsT=R["kd"][:, sle], rhs=vnew[si][:], start=True, stop=True),
              reads=[rs, vns[si]], writes=[pds[si]])
        em.op("dve", "scalar_tensor_tensor", dict(out=S[si][:], in0=S[si][:], scalar=R["gl"][:, nn:nn + 1], in1=ps_ds[:, si, :],
                                                  op0=ALU.mult, op1=ALU.add), reads=[rs, pds[si]], writes=[Ss[si]])
        em.op("act", "copy", dict(out=Ob[:, sl], in_=ps_o[:, si, pr, :]), reads=[pos[si][pr]], writes=[obs])

    NSTAGE = 10
    for blk in range(NBLK + 1):
        gens = []
        if blk < NBLK:
            for si, (b, d) in enumerate(scans):
                bb = blk if d == 0 else NBLK - 1 - blk
                gens.append(precompute(si, bb, res[si][blk % 2]))
        if blk == 0:
            for g in gens:
                for _ in g:
                    pass
            continue
        obufs = [rOb.next() for _ in range(NS)]
        steps = [(si, n) for n in range(NB) for si in range(NS)]
        per = -(-(NSTAGE * len(gens)) // len(steps)) if gens else 0
        gi = 0
        for (si, n) in steps:
            scan_step(si, n, res[si][(blk - 1) % 2], obufs[si][0], obufs[si][1])
            k = 0
            while k < per and gi < len(gens):
                try:
                    next(gens[gi])
                    k += 1
                except StopIteration:
                    gi += 1
        while gi < len(gens):
            for _ in gens[gi]:
                pass
            gi += 1
        for si, (b, d) in enumerate(scans):
            bb = (blk - 1) if d == 0 else NBLK - 1 - (blk - 1)
            g0 = b * T + bb * PB
            em.dma("sp", io["oT"][d, :, g0:g0 + PB], obufs[si][0][:], obufs[si][1], reads=[obufs[si][1]])
    fin = []
    for s in rOb.slots:
        fin.extend(s.r)
    em.wait_all("sp", fin)

    rF = Ring(em, "p2cF", [128, 3, PB], F32, 2)
    rFo = Ring(em, "p2cFo", [128, PB], BF16, 2)
    for b in range(B):
        for blk in range(NBLK):
            g0 = b * T + blk * PB
            F, fs = rF.next()
            em.dma("sp", F[:, 0, :], io["oT"][0, :, g0:g0 + PB], fs, writes=[fs])
            em.dma("sp", F[:, 1, :], io["oT"][1, :, g0:g0 + PB], fs, writes=[fs])
            em.dma("sp", F[:, 2, :], io["xin"](3, b, blk * PB, blk * PB + PB), fs, writes=[fs])
            em.op("dve", "tensor_tensor", dict(out=F[:, 0, :], in0=F[:, 0, :], in1=F[:, 1, :], op=ALU.add), writes=[fs])
            em.op("pool", "tensor_tensor", dict(out=F[:, 1, :], in0=F[:, 0, :], in1=F[:, 0, :], op=ALU.mult), writes=[fs])
            P, pss = rPc.next()
            em.op("pe", "matmul", dict(out=P[:, 0:W], lhsT=ones128[:], rhs=F[:, 1, :], start=True, stop=True),
                  reads=[fs, cs], writes=[pss])
            em.op("act", "activation", dict(out=F[:, 1, :], in_=P[:, 0:W], func=AF.Sqrt, scale=1.0 / HD, bias=RMS_EPS),
                  reads=[pss], writes=[fs])
            em.op("dve", "reciprocal", dict(out=F[:, 1, :], in_=F[:, 1, :]), writes=[fs])
            em.op("dve", "tensor_tensor", dict(out=F[:, 0, :], in0=F[:, 0, :], in1=F[:, 1, :], op=ALU.mult), writes=[fs])
            em.op("act", "activation", dict(out=F[:, 2, :], in_=F[:, 2, :], func=AF.Silu), writes=[fs])
            O, os_ = rFo.next()
            em.op("dve", "scalar_tensor_tensor", dict(out=O[:], in0=F[:, 0, :], scalar=par[:, 12:13], in1=F[:, 2, :],
                                                      op0=ALU.mult, op1=ALU.mult), reads=[fs, ps_], writes=[os_])
            em.dma("sp", io["gdnT"][:, g0:g0 + PB], O[:], os_, reads=[os_])
    fin = []
    for s in rFo.slots:
        fin.extend(s.r)
    em.wait_all("sp", fin)


def run_p2_program(xin, abc, cw, csc, nw, alog, dtb, B, T):
    NTOK = B * T
    nc = bass.Bass("TRN2", target_bir_lowering=False)
    with contextlib.ExitStack() as st:
        em = Em(nc, st)
        d_x = nc.dram_tensor("xin", [7, 128, NTOK], F32, kind="ExternalInput").ap()
        d_abc = nc.dram_tensor("abc", [NTOK, 4], F32, kind="ExternalInput").ap()
        d_cw = nc.dram_tensor("cw", [128, 9], F32, kind="ExternalInput").ap()
        d_csc = nc.dram_tensor("csc", [128, 3], F32, kind="ExternalInput").ap()
        d_nw = nc.dram_tensor("nw", [128, 1], F32, kind="ExternalInput").ap()
        d_al = nc.dram_tensor("alog", [64, 2], F32, kind="ExternalInput").ap()
        d_dt = nc.dram_tensor("dtb", [64, 2], F32, kind="ExternalInput").ap()
        d_g = nc.dram_tensor("gdnT", [128, NTOK], BF16, kind="ExternalOutput").ap()
        d_u = nc.dram_tensor("uscT", [128, NTOK], BF16, kind="ExternalOutput").ap()
        io = dict(
            xin=lambda kind, b, a, c: d_x[kind, :, b * T + a:b * T + c], abc=d_abc, cw=d_cw, csc=d_csc, nw=d_nw,
            alog=d_al, dtb=d_dt, gdnT=d_g, uscT=d_u,
            qn=nc.dram_tensor("s_qn", [128, NTOK], BF16, kind="Internal").ap(),
            kn=nc.dram_tensor("s_kn", [128, NTOK], BF16, kind="Internal").ap(),
            vv=nc.dram_tensor("s_vv", [128, NTOK], F32, kind="Internal").ap(),
            oT=nc.dram_tensor("s_oT", [2, 128, NTOK], F32, kind="Internal").ap())
        C = make_consts(em)
        build_p2(em, C, io, B, T)
        em.finalize()
    print("P2 insts", em.ninst, "sems", em.nsem, flush=True)
    n = len(xin)
    in_maps = [dict(xin=xin[c], abc=abc[c], cw=cw[c], csc=csc[c], nw=nw[c], alog=alog[c], dtb=dtb[c]) for c in range(n)]
    res = run_bass_kernel_spmd(nc, in_maps, core_ids=list(range(n)))
    return [r["gdnT"] for r in res.results], [r["uscT"] for r in res.results]


def p1_dest(io, g):
    if g < 4096:
        return io["send"][(g % 1024) // 128, g // 1024, :, :]
    g2 = g - 4128
    kind, h = 4 + g2 // 1024, (g2 % 1024) // 128
    if kind <= 6:
        return io["send"][h, kind, :, :]
    return io["gateT"][kind - 7, h * 128:(h + 1) * 128, :]


def build_p1(em, C, io, TOK):
    cs = C["slot"]
    NT = TOK // 128
    xT = sb(em, "p1xT", [128, 8, TOK], BF16)
    xTs = Slot("xT")
    rXl = Ring(em, "p1x", [128, 1024], F32, 2)
    rXb = Ring(em, "p1xb", [128, 1024], BF16, 2)
    rP = Ring(em, "p1P", [128, 512], F32, 6, psum=True)
    for t in range(NT):
        X, xs = rXl.next()
        em.dma("sp", X[:], io["x"][t * 128:(t + 1) * 128, :], xs, writes=[xs])
        Xb, xbs = rXb.next()
        em.op("act", "copy", dict(out=Xb[:], in_=X[:]), reads=[xs], writes=[xbs])
        for hf in range(2):
            P, pss = rP.next()
            for k in range(4):
                kk = hf * 4 + k
                em.op("pe", "matmul", dict(out=P[:, k * 128:(k + 1) * 128], lhsT=Xb[:, kk * 128:(kk + 1) * 128],
                                           rhs=C["idb"][:], start=True, stop=True), reads=[xbs, cs], writes=[pss], sig=(k == 3))
            em.op("dve", "tensor_copy", dict(out=xT[:, hf * 4:(hf + 1) * 4, t * 128:(t + 1) * 128], in_=r3(P[:, :], 128)),
                  reads=[pss], writes=[xTs])
    CW = 512
    rWf = Ring(em, "p1wf", [128, 8, CW], F32, 2)
    rWb = Ring(em, "p1wb", [128, 8, CW], BF16, 2)
    rSt = Ring(em, "p1st", [128, 512], F32, 4)
    wv = io["w_in"].rearrange("(kc p) c -> p kc c", p=128)
    slabs = [(c0, 512) for c0 in range(0, 4096, 512)] + [(4096, 32)] + [(c0, 512) for c0 in range(4128, 9248, 512)]
    TB = min(512, TOK)
    NTB = TOK // TB
    ev = 0
    for (c0, cw_) in slabs:
        Wf, wfs = rWf.next()
        em.dma("sp", Wf[:, :, 0:cw_], wv[:, :, c0:c0 + cw_], wfs, writes=[wfs])
        Wb, wbs = rWb.next()
        em.op("pool", "tensor_copy", dict(out=Wb[:, :, 0:cw_], in_=Wf[:, :, 0:cw_]), reads=[wfs], writes=[wbs])
        if cw_ == 32:
            for t in range(NT):
                P, pss = rP.next()
                for k in range(8):
                    em.op("pe", "matmul", dict(out=P[:, 0:32], lhsT=xT[:, k, t * 128:(t + 1) * 128], rhs=Wb[:, k, 0:32],
                                               start=(k == 0), stop=(k == 7)), reads=[xTs, wbs], writes=[pss], sig=(k == 7))
                St, sts = rSt.next()
                em.op("act", "copy", dict(out=St[:, 0:32], in_=P[:, 0:32]), reads=[pss], writes=[sts])
                em.dma("sp", io["abT"][t * 128:(t + 1) * 128, :], St[:, 0:32], sts, reads=[sts])
            continue
        for j in range(cw_ // 128):
            dst = p1_dest(io, c0 + j * 128)
            for tb in range(NTB):
                P, pss = rP.next()
                for k in range(8):
                    em.op("pe", "matmul", dict(out=P[:, 0:TB], lhsT=Wb[:, k, j * 128:(j + 1) * 128], rhs=xT[:, k, tb * TB:(tb + 1) * TB],
                                               start=(k == 0), stop=(k == 7)), reads=[xTs, wbs], writes=[pss], sig=(k == 7))
                St, sts = rSt.next()
                if ev % 2 == 0:
                    em.op("act", "copy", dict(out=St[:, 0:TB], in_=P[:, 0:TB]), reads=[pss], writes=[sts])
                else:
                    em.op("dve", "tensor_copy", dict(out=St[:, 0:TB], in_=P[:, 0:TB]), reads=[pss], writes=[sts])
                ev += 1
                em.dma("sp", dst[:, tb * TB:(tb + 1) * TB], St[:, 0:TB], sts, reads=[sts])
    fin = []
    for s in rSt.slots:
        fin.extend(s.r)
    em.wait_all("sp", fin)


def load_weight_bf16(em, Wb, wbs, src3, nk, ncols, rSt, slabc, eng="pool"):
    for c0 in range(0, ncols, slabc):
        Wf, wfs = rSt.next()
        em.dma("sp", Wf[:, 0:nk, 0:slabc], src3[:, :, c0:c0 + slabc], wfs, writes=[wfs])
        em.op(eng, "tensor_copy", dict(out=Wb[:, :, c0:c0 + slabc], in_=Wf[:, 0:nk, 0:slabc]), reads=[wfs], writes=[wbs])


def layer_norm(em, Rr, rs, gbc, bbc, cslot, rSm):
    Sm, sms = rSm.next()
    em.op("dve", "bn_stats", dict(out=Sm[:, 0:6], in_=Rr[:, 0:512]), reads=[rs], writes=[sms])
    em.op("dve", "bn_stats", dict(out=Sm[:, 6:12], in_=Rr[:, 512:1024]), reads=[rs], writes=[sms])
    em.op("dve", "bn_aggr", dict(out=Sm[:, 12:14], in_=Sm[:, 0:12]), writes=[sms])
    em.op("act", "activation", dict(out=Sm[:, 14:15], in_=Sm[:, 13:14], func=AF.Sqrt, bias=LN_EPS), writes=[sms])
    em.op("dve", "reciprocal", dict(out=Sm[:, 14:15], in_=Sm[:, 14:15]), writes=[sms])
    em.op("dve", "tensor_scalar", dict(out=Rr[:], in0=Rr[:], scalar1=Sm[:, 12:13], scalar2=Sm[:, 14:15],
                                       op0=ALU.subtract, op1=ALU.mult), reads=[sms], writes=[rs])
    em.op("pool", "tensor_tensor", dict(out=Rr[:], in0=Rr[:], in1=gbc[:], op=ALU.mult), reads=[cslot], writes=[rs])
    em.op("pool", "tensor_tensor", dict(out=Rr[:], in0=Rr[:], in1=bbc[:], op=ALU.add), reads=[cslot], writes=[rs])


def build_p3a(em, C, io, TOK):
    cs = C["slot"]
    rSt = Ring(em, "p3awf", [128, 8, 256], F32, 2)
    W = {}
    for nm in ("wg", "ws", "wo"):
        W[nm] = (sb(em, f"p3a_{nm}", [128, 8, 1024], BF16), Slot(nm))
        load_weight_bf16(em, W[nm][0], W[nm][1], io[nm].rearrange("(kc p) c -> p kc c", p=128), 8, 1024, rSt, 256)
    lnc = sb(em, "p3a_ln", [128, 2, 1024], F32)
    lns = Slot("ln")
    em.dma("sp", lnc[:, 0, :], io["lng"], lns, writes=[lns])
    em.dma("sp", lnc[:, 1, :], io["lnb"], lns, writes=[lns])
    TBA = min(256, TOK)
    rG = Ring(em, "p3aG", [128, 2, 8, TBA], BF16, 2)
    rGT = Ring(em, "p3aGT", [128, 2, 8, TBA], F32, 2)
    rT = Ring(em, "p3aT", [128, 2, TBA], F32, 2)
    rMx = Ring(em, "p3aMx", [128, 8, TBA], BF16, 2)
    rX = Ring(em, "p3aX", [128, 1024], F32, 3)
    rSm = Ring(em, "p3aSm", [128, 16], F32, 2)
    rP = Ring(em, "p3aP", [128, 512], F32, 6, psum=True)
    gv = io["gdnT"].rearrange("(j p) t -> p j t", p=128)
    uv = io["uscT"].rearrange("(j p) t -> p j t", p=128)
    for tb in range(TOK // TBA):
        tsl = slice(tb * TBA, (tb + 1) * TBA)
        G, gs = rG.next()
        em.dma("sp", G[:, 0, :, :], gv[:, :, tsl], gs, writes=[gs])
        em.dma("sp", G[:, 1, :, :], uv[:, :, tsl], gs, writes=[gs])
        GT, gts = rGT.next()
        for gi in range(2):
            em.dma("sp", GT[:, gi, :, :], io["gateT"][gi].rearrange("(j p) t -> p j t", p=128)[:, :, tsl], gts, writes=[gts])
        em.op("act", "activation", dict(out=GT[:], in_=GT[:], func=AF.Sigmoid), writes=[gts])
        Mx, mxs = rMx.next()
        for m in range(8):
            Ps = []
            for gi, nm in enumerate(("wg", "ws")):
                P, pss = rP.next()
                for j in range(8):
                    em.op("pe", "matmul", dict(out=P[:, 0:TBA], lhsT=W[nm][0][:, j, m * 128:(m + 1) * 128], rhs=G[:, gi, j, :],
                                               start=(j == 0), stop=(j == 7)), reads=[W[nm][1], gs], writes=[pss], sig=(j == 7))
                Ps.append((P, pss))
            Tt, tts = rT.next()
            for gi in range(2):
                em.op("dve", "tensor_tensor", dict(out=Tt[:, gi, :], in0=GT[:, gi, m, :], in1=Ps[gi][0][:, 0:TBA], op=ALU.mult),
                      reads=[gts, Ps[gi][1]], writes=[tts])
            em.op("pool", "tensor_tensor", dict(out=Mx[:, m, :], in0=Tt[:, 0, :], in1=Tt[:, 1, :], op=ALU.add),
                  reads=[tts], writes=[mxs])
        for tile in range(TBA // 128):
            t0 = tb * TBA + tile * 128
            X, xs = rX.next()
            em.dma("sp", X[:], io["x"][t0:t0 + 128, :], xs, writes=[xs])
            for hf in range(2):
                P, pss = rP.next()
                for j in range(8):
                    em.op("pe", "matmul", dict(out=P[:, :], lhsT=Mx[:, j, tile * 128:(tile + 1) * 128],
                                               rhs=W["wo"][0][:, j, hf * 512:(hf + 1) * 512], start=(j == 0), stop=(j == 7)),
                          reads=[W["wo"][1], mxs], writes=[pss], sig=(j == 7))
                em.op("dve", "scalar_tensor_tensor", dict(out=X[:, hf * 512:(hf + 1) * 512], in0=X[:, hf * 512:(hf + 1) * 512],
                                                          scalar=float(ALPHA), in1=P[:, :], op0=ALU.mult, op1=ALU.add),
                      reads=[pss], writes=[xs])
            layer_norm(em, X, xs, lnc[:, 0, :], lnc[:, 1, :], lns, rSm)
            em.dma("sp", io["x1"][t0:t0 + 128, :], X[:], xs, reads=[xs])
    fin = []
    for s in rX.slots:
        fin.extend(s.r)
    em.wait_all("sp", fin)


def build_p3b(em, C, io, TOK):
    cs = C["slot"]
    rSt = Ring(em, "p3bwf", [128, 8, 128], F32, 2)
    Wup = sb(em, "p3b_wup", [128, 8, 4096], BF16)
    wus = Slot("wup")
    Wdn = sb(em, "p3b_wdn", [128, 32, 1024], BF16)
    wds = Slot("wdn")
    load_weight_bf16(em, Wup, wus, io["wup"].rearrange("(kc p) c -> p kc c", p=128), 8, 4096, rSt, 128)
    dv = io["wdn"].rearrange("(fc p) c -> p fc c", p=128)
    for fc in range(32):
        Wf, wfs = rSt.next()
        Wf2 = Wf[:].rearrange("p a b -> p (a b)")
        em.dma("sp", Wf2, dv[:, fc, :], wfs, writes=[wfs])
        em.op("pool", "tensor_copy", dict(out=Wdn[:, fc, :], in_=Wf2), reads=[wfs], writes=[wds])
    cst = sb(em, "p3b_c", [128, 3, 1024], F32)
    bup = sb(em, "p3b_bup", [128, 32], F32)
    cs2 = Slot("p3bc")
    em.dma("sp", cst[:, 0, :], io["lng"], cs2, writes=[cs2])
    em.dma("sp", cst[:, 1, :], io["lnb"], cs2, writes=[cs2])
    em.dma("sp", cst[:, 2, :], io["bdn"], cs2, writes=[cs2])
    em.dma("sp", bup[:], io["bup"], cs2, writes=[cs2])
    TBB = min(256, TOK)
    NTL = TBB // 128
    rX = Ring(em, "p3bX", [128, 1024], F32, 2 * NTL)
    rXb = Ring(em, "p3bXb", [128, 1024], BF16, 1)
    rXT = Ring(em, "p3bXT", [128, 8, TBB], BF16, 1)
    rH = Ring(em, "p3bH", [128, TBB], F32, 2)
    rHT = Ring(em, "p3bHT", [128, 32, TBB], BF16, 1)
    rSm = Ring(em, "p3bSm", [128, 16], F32, 2)
    rP = Ring(em, "p3bP", [128, 512], F32, 6, psum=True)
    for tb in range(TOK // TBB):
        XT, xts = rXT.next()
        tiles = []
        for tile in range(NTL):
            t0 = tb * TBB + tile * 128
            X, xs = rX.next()
            em.dma("sp", X[:], io["x1"][t0:t0 + 128, :], xs, writes=[xs])
            tiles.append((X, xs, t0))
            Xb, xbs = rXb.next()
            em.op("act", "copy", dict(out=Xb[:], in_=X[:]), reads=[xs], writes=[xbs])
            for hf in range(2):
                P, pss = rP.next()
                for k in range(4):
                    kk = hf * 4 + k
                    em.op("pe", "matmul", dict(out=P[:, k * 128:(k + 1) * 128], lhsT=Xb[:, kk * 128:(kk + 1) * 128],
                                               rhs=C["idb"][:], start=True, stop=True), reads=[xbs, cs], writes=[pss], sig=(k == 3))
                em.op("dve", "tensor_copy", dict(out=XT[:, hf * 4:(hf + 1) * 4, tile * 128:(tile + 1) * 128], in_=r3(P[:, :], 128)),
                      reads=[pss], writes=[xts])
        HT, hts = rHT.next()
        for f in range(32):
            P, pss = rP.next()
            for k in range(8):
                em.op("pe", "matmul", dict(out=P[:, 0:TBB], lhsT=Wup[:, k, f * 128:(f + 1) * 128], rhs=XT[:, k, :],
                                           start=(k == 0), stop=(k == 7)), reads=[wus, xts], writes=[pss], sig=(k == 7))
            H, hs = rH.next()
            em.op("act", "activation", dict(out=H[:], in_=P[:, 0:TBB], func=AF.Relu, bias=bup[:, f:f + 1]),
                  reads=[pss, cs2], writes=[hs])
            em.op("pool", "tensor_tensor", dict(out=HT[:, f, :], in0=H[:], in1=H[:], op=ALU.mult), reads=[hs], writes=[hts])
        for tile in range(NTL):
            X, xs, t0 = tiles[tile]
            for hf in range(2):
                P, pss = rP.next()
                for f in range(32):
                    em.op("pe", "matmul", dict(out=P[:, :], lhsT=HT[:, f, tile * 128:(tile + 1) * 128],
                                               rhs=Wdn[:, f, hf * 512:(hf + 1) * 512], start=(f == 0), stop=(f == 31)),
                          reads=[wds, hts], writes=[pss], sig=(f == 31))
                em.op("dve", "scalar_tensor_tensor", dict(out=X[:, hf * 512:(hf + 1) * 512], in0=X[:, hf * 512:(hf + 1) * 512],
                                                          scalar=float(ALPHA), in1=P[:, :], op0=ALU.mult, op1=ALU.add),
                      reads=[pss], writes=[xs])
            em.op("pool", "tensor_tensor", dict(out=X[:], in0=X[:], in1=cst[:, 2, :], op=ALU.add), reads=[cs2], writes=[xs])
            layer_norm(em, X, xs, cst[:, 0, :], cst[:, 1, :], cs2, rSm)
            em.dma("sp", io["x2"][t0:t0 + 128, :], X[:], xs, reads=[xs])
    fin = []
    for s in rX.slots:
        fin.extend(s.r)
    em.wait_all("sp", fin)


def _launch(build, ins, outs, internals, arrays, ncores=8):
    nc = bass.Bass("TRN2", target_bir_lowering=False)
    with contextlib.ExitStack() as st:
        em = Em(nc, st)
        io = {}
        for nm, (shape, dt) in ins.items():
            io[nm] = nc.dram_tensor(nm, list(shape), dt, kind="ExternalInput").ap()
        for nm, (shape, dt) in outs.items():
            io[nm] = nc.dram_tensor(nm, list(shape), dt, kind="ExternalOutput").ap()
        for nm, (shape, dt) in internals.items():
            io[nm] = nc.dram_tensor(nm, list(shape), dt, kind="Internal").ap()
        C = make_consts(em)
        build(em, C, io)
        em.finalize()
    res = run_bass_kernel_spmd(nc, arrays, core_ids=list(range(ncores)))
    return [{nm: r[nm] for nm in outs} for r in res.results]


def run_p1(x_sh, w_in_l, TOK):
    ins = dict(x=((TOK, 1024), F32), w_in=((1024, IN_COLS), F32))
    outs = dict(send=((8, 7, 128, TOK), F32), abT=((TOK, 32), F32), gateT=((2, 1024, TOK), F32))
    return _launch(lambda em, C, io: build_p1(em, C, io, TOK), ins, outs, {}, [dict(x=x_sh[c], w_in=w_in_l) for c in range(8)])


def run_p2(xin, abc, cw, csc, nw, alog, dtb, B, T):
    NTOK = B * T
    ins = dict(xin=((7, 128, NTOK), F32), abc=((NTOK, 4), F32), cw=((128, 9), F32), csc=((128, 3), F32),
               nw=((128, 1), F32), alog=((64, 2), F32), dtb=((64, 2), F32))
    outs = dict(gdnT=((128, NTOK), BF16), uscT=((128, NTOK), BF16))
    internals = dict(qn=((128, NTOK), BF16), kn=((128, NTOK), BF16), vv=((128, NTOK), F32), oT=((2, 128, NTOK), F32))

    def build(em, C, io):
        d_x = io["xin"]
        io["xin"] = lambda kind, b, a, c: d_x[kind, :, b * T + a:b * T + c]
        build_p2(em, C, io, B, T)
    arrays = [dict(xin=xin[c], abc=abc[c], cw=cw[c], csc=csc[c], nw=nw[c], alog=alog[c], dtb=dtb[c]) for c in range(8)]
    return _launch(build, ins, outs, internals, arrays)


def run_p3a(gdnT, uscT, gateT, x_sh, wg, ws, wo, lng, lnb, TOK):
    ins = dict(gdnT=((1024, TOK), BF16), uscT=((1024, TOK), BF16), gateT=((2, 1024, TOK), F32), x=((TOK, 1024), F32),
               wg=((1024, 1024), F32), ws=((1024, 1024), F32), wo=((1024, 1024), F32), lng=((128, 1024), F32), lnb=((128, 1024), F32))
    outs = dict(x1=((TOK, 1024), F32))
    arrays = [dict(gdnT=gdnT[c], uscT=uscT[c], gateT=gateT[c], x=x_sh[c], wg=wg, ws=ws, wo=wo, lng=lng, lnb=lnb) for c in range(8)]
    return _launch(lambda em, C, io: build_p3a(em, C, io, TOK), ins, outs, {}, arrays)


def run_p3b(x1, wup, bup, wdn, bdn, lng, lnb, TOK):
    ins = dict(x1=((TOK, 1024), F32), wup=((1024, 4096), F32), bup=((128, 32), F32), wdn=((4096, 1024), F32),
               bdn=((128, 1024), F32), lng=((128, 1024), F32), lnb=((128, 1024), F32))
    outs = dict(x2=((TOK, 1024), F32))
    arrays = [dict(x1=x1[c], wup=wup, bup=bup, wdn=wdn, bdn=bdn, lng=lng, lnb=lnb) for c in range(8)]
    return _launch(lambda em, C, io: build_p3b(em, C, io, TOK), ins, outs, {}, arrays)


def bcast128(v):
    return np.ascontiguousarray(np.broadcast_to(np.asarray(v, np.float32)[None, :], (128, v.shape[0])))


def forward_unfused(inp, B, T, depth):
    TOK = B * T // 8
    x = np.ascontiguousarray(inp["x"], dtype=np.float32).reshape(B * T, 1024)
    x_sh = [np.ascontiguousarray(x[c * TOK:(c + 1) * TOK]) for c in range(8)]
    for l in range(depth):
        o1 = run_p1(x_sh, np.ascontiguousarray(inp["w_in"][l]), TOK)
        xin = [np.ascontiguousarray(np.concatenate([o1[c]["send"][h] for c in range(8)], axis=2)) for h in range(8)]
        abT = np.concatenate([o1[c]["abT"] for c in range(8)], axis=0)
        abc = [np.ascontiguousarray(abT[:, [h, 8 + h, 16 + h, 24 + h]]) for h in range(8)]
        cq = inp["conv_qkv"][l]
        cw = [np.ascontiguousarray(np.concatenate([cq[:, k * 1024 + h * 128:k * 1024 + (h + 1) * 128].T for k in range(3)], axis=1))
              for h in range(8)]
        csc = [np.ascontiguousarray(inp["conv_sc"][l][:, h * 128:(h + 1) * 128].T) for h in range(8)]
        nw = [np.ascontiguousarray(inp["gdn_norm_w"][l].reshape(128, 1))] * 8
        alog = [np.ascontiguousarray(np.broadcast_to(inp["a_log"][l][:, h][None, :], (64, 2))) for h in range(8)]
        dtb = [np.ascontiguousarray(np.broadcast_to(inp["dt_bias"][l][:, h][None, :], (64, 2))) for h in range(8)]
        o2 = run_p2(xin, abc, cw, csc, nw, alog, dtb, B, T)
        gd = np.concatenate([o2[h]["gdnT"] for h in range(8)], axis=0)
        us = np.concatenate([o2[h]["uscT"] for h in range(8)], axis=0)
        gdn_sh = [np.ascontiguousarray(gd[:, c * TOK:(c + 1) * TOK]) for c in range(8)]
        usc_sh = [np.ascontiguousarray(us[:, c * TOK:(c + 1) * TOK]) for c in range(8)]
        gate_sh = [o1[c]["gateT"] for c in range(8)]
        o3 = run_p3a(gdn_sh, usc_sh, gate_sh, x_sh, np.ascontiguousarray(inp["w_o_gdn"][l]), np.ascontiguousarray(inp["w_o_sc"][l]),
                     np.ascontiguousarray(inp["w_out"][l]), bcast128(inp["ln1_g"][l]), bcast128(inp["ln1_b"][l]), TOK)
        x1 = [o3[c]["x1"] for c in range(8)]
        o4 = run_p3b(x1, np.ascontiguousarray(inp["w_up"][l]), np.ascontiguousarray(inp["b_up"][l].reshape(32, 128).T),
                     np.ascontiguousarray(inp["w_down"][l]), bcast128(inp["b_down"][l]), bcast128(inp["ln2_g"][l]),
                     bcast128(inp["ln2_b"][l]), TOK)
        x_sh = [o4[c]["x2"] for c in range(8)]
    return np.concatenate(x_sh, axis=0).reshape(B, T, 1024).astype(np.float32)


def kernel(**inputs):
    inp = {k: np.asarray(v) for k, v in inputs.items()}
    B, T = inp["x"].shape[0], inp["x"].shape[1]
    return forward_unfused(inp, B, T, inp["w_in"].shape[0])
```

```python
import contextlib
import numpy as np
import ml_dtypes
import concourse.bass as bass
import concourse.mybir as mybir
from concourse.bass_utils import run_bass_kernel_spmd

F32 = mybir.dt.float32
BF16 = mybir.dt.bfloat16
F32R = mybir.dt.float32r


def fr(ap):
    return ap
ALU = mybir.AluOpType
AF = mybir.ActivationFunctionType

D_MODEL = 1024
N_HEADS = 8
HD = 128
DEPTH = 4
D_FF = 4096
IN_COLS = 9248
ALPHA = (2 * DEPTH) ** 0.25
LN_EPS = 1e-5
RMS_EPS = 1e-6
L2_EPS = 1e-6
NEG = -30000.0

EPOCH = 12000
NBC = 4
SAME_ENGINE_SYNC = True
_DBG = {}


class Slot:
    __slots__ = ("w", "r", "dsem", "dcnt", "name")

    def __init__(self, name=""):
        self.w = None
        self.r = []
        self.dsem = None
        self.dcnt = 0
        self.name = name


class Em:
    ENG = ("pe", "dve", "act", "pool", "sp")

    def __init__(self, nc, stack):
        self.nc = nc
        self.stack = stack
        self.q = {e: [] for e in self.ENG}
        self.cnt = {e: 0 for e in self.ENG}
        self.epoch = {e: 0 for e in self.ENG}
        self.waited = {e: {} for e in self.ENG}
        self.sems = {}
        self.nsem = 0
        self.ninst = {e: 0 for e in self.ENG}

    def sem(self, key):
        if key not in self.sems:
            self.nsem += 1
            self.sems[key] = self.stack.enter_context(self.nc.semaphore(f"s{self.nsem}"))
        return self.sems[key]

    def _waits(self, eng, deps):
        w = []
        for d in deps:
            if d is None:
                continue
            key, val = d
            if key[0] == eng:
                if not SAME_ENGINE_SYNC:
                    continue
                if key[1] == self.epoch[eng] and val > self.cnt[eng]:
                    continue
            if self.waited[eng].get(key, 0) >= val:
                continue
            self.waited[eng][key] = val
            w.append((self.sem(key), val))
        return w

    def op(self, eng, meth, kw, reads=(), writes=(), deps=(), sig=True):
        fn = (meth, kw)
        alld = list(deps)
        for s in reads:
            alld.append(s.w)
        for s in writes:
            alld.append(s.w)
            alld.extend(s.r)
        waits = self._waits(eng, alld)
        if sig:
            self.cnt[eng] += 1
            key = (eng, self.epoch[eng])
            tok = (key, self.cnt[eng])
            semh = self.sem(key)
            if self.cnt[eng] >= EPOCH:
                self.epoch[eng] += 1
                self.cnt[eng] = 0
        else:
            key = (eng, self.epoch[eng])
            tok = (key, self.cnt[eng] + 1)
            semh = None
        self.q[eng].append((waits, fn, semh, 1))
        self.ninst[eng] += 1
        for s in reads:
            s.r.append(tok)
        for s in writes:
            s.w = tok
            s.r = []
        return tok

    def dma(self, eng, out, in_, slot, reads=(), writes=(), deps=()):
        alld = list(deps)
        for s in reads:
            alld.append(s.w)
        for s in writes:
            alld.append(s.w)
            alld.extend(s.r)
        waits = self._waits(eng, alld)
        if slot.dsem is None:
            slot.dsem = ("d", id(slot))
        slot.dcnt += 16
        tok = (slot.dsem, slot.dcnt)
        semh = self.sem(slot.dsem)
        self.q[eng].append((waits, ("dma_start", dict(out=out, in_=in_)), semh, 16))
        self.ninst[eng] += 1
        for s in reads:
            s.r.append(tok)
        for s in writes:
            s.w = tok
            s.r = []
        return tok

    def wait_all(self, eng, toks):
        waits = self._waits(eng, toks)
        self.q[eng].append((waits, None, None, 0))

    def finalize(self):
        nc = self.nc
        block = self.stack.enter_context(nc.Block())
        qs = self.q

        def run(e, lst):
            for waits, fn, semh, inc in lst:
                for (s, v) in waits:
                    e.wait_ge(s, v)
                if fn is None:
                    continue
                ins = getattr(e, fn[0])(**fn[1])
                if semh is not None:
                    ins.then_inc(semh, inc)

        @block.tensor
        def _(e):
            run(e, qs["pe"])

        @block.vector
        def _(e):
            run(e, qs["dve"])

        @block.scalar
        def _(e):
            run(e, qs["act"])

        @block.gpsimd
        def _(e):
            run(e, qs["pool"])

        @block.sync
        def _(e):
            run(e, qs["sp"])


def sb(em, name, shape, dt):
    return em.stack.enter_context(em.nc.sbuf_tensor(name, shape, dt))


def pst(em, name, shape, dt):
    return em.stack.enter_context(em.nc.psum_tensor(name, shape, dt))


def make_consts(em):
    C = {}
    C["idf"] = sb(em, "c_idf", [128, 128], F32)
    C["idb"] = sb(em, "c_idb", [128, 128], BF16)
    C["ones"] = sb(em, "c_ones", [64, 128], F32)
    C["nones"] = sb(em, "c_nones", [64, 128], F32)
    C["tri"] = [sb(em, f"c_tri{d}", [64, 64], F32) for d in range(2)]
    for nm in ("SL", "SU", "IU", "IL"):
        C[nm] = sb(em, f"c_{nm}", [64, NBC * 64], F32)
    s = Slot("consts")
    C["slot"] = s
    idf, idb = C["idf"], C["idb"]
    C["ones128"] = sb(em, "c_ones128", [128, 128], F32)
    em.op("pool", "memset", dict(ap=C["ones128"][:], constant=1.0), writes=[s])
    em.op("pool", "memset", dict(ap=idf[:], constant=0.0), writes=[s])
    em.op("pool", "affine_select", dict(out=idf[:], in_=idf[:], pattern=[[-1, 128]], compare_op=ALU.not_equal,
                                        fill=1.0, base=0, channel_multiplier=1), writes=[s])
    em.op("pool", "tensor_copy", dict(out=idb[:], in_=idf[:]), writes=[s])
    em.op("pool", "memset", dict(ap=C["ones"][:], constant=1.0), writes=[s])
    em.op("pool", "memset", dict(ap=C["nones"][:], constant=-1.0), writes=[s])
    for d in range(2):
        em.op("pool", "memset", dict(ap=C["tri"][d][:], constant=1.0), writes=[s])
    em.op("pool", "affine_select", dict(out=C["tri"][0][:], in_=C["tri"][0][:], pattern=[[1, 64]],
                                        compare_op=ALU.is_ge, fill=0.0, base=0, channel_multiplier=-1), writes=[s])
    em.op("pool", "affine_select", dict(out=C["tri"][1][:], in_=C["tri"][1][:], pattern=[[-1, 64]],
                                        compare_op=ALU.is_ge, fill=0.0, base=0, channel_multiplier=1), writes=[s])
    specs = {"SL": (1, -1, ALU.is_gt), "SU": (-1, 1, ALU.is_gt), "IU": (-1, 1, ALU.is_ge), "IL": (1, -1, ALU.is_ge)}
    for nm, (cm, jm, cmpop) in specs.items():
        m = C[nm]
        em.op("pool", "memset", dict(ap=m[:], constant=0.0), writes=[s])
        em.op("pool", "affine_select", dict(
            out=m[:].rearrange("p (n j) -> p n j", j=64), in_=m[:].rearrange("p (n j) -> p n j", j=64),
            pattern=[[0, NBC], [jm, 64]], compare_op=cmpop, fill=NEG, base=0, channel_multiplier=cm), writes=[s])
    return C


class Ring:
    def __init__(self, em, name, shape, dt, n, psum=False):
        mk = pst if psum else sb
        self.tiles = [mk(em, f"{name}{i}", shape, dt) for i in range(n)]
        self.slots = [Slot(f"{name}{i}") for i in range(n)]
        self.i = 0
        self.n = n

    def next(self):
        t, s = self.tiles[self.i], self.slots[self.i]
        self.i = (self.i + 1) % self.n
        return t, s


def r3(ap, inner):
    return ap.rearrange("p (n j) -> p n j", j=inner)


def conv3(em, engs, Y, ys, X, xs, w, ws, PB):
    if engs[0] == "act":
        em.op("act", "activation", dict(out=Y[:, 0:PB], in_=X[:, 1:PB + 1], func=AF.Copy, scale=w[:, 1:2]),
              reads=[xs, ws], writes=[ys])
    else:
        em.op(engs[0], "tensor_scalar_mul", dict(out=Y[:, 0:PB], in0=X[:, 1:PB + 1], scalar1=w[:, 1:2]),
              reads=[xs, ws], writes=[ys])
    em.op("dve", "scalar_tensor_tensor", dict(out=Y[:, 0:PB], in0=X[:, 0:PB], scalar=w[:, 0:1], in1=Y[:, 0:PB],
                                                op0=ALU.mult, op1=ALU.add), reads=[xs, ws], writes=[ys])
    em.op("dve", "scalar_tensor_tensor", dict(out=Y[:, 0:PB], in0=X[:, 2:PB + 2], scalar=w[:, 2:3], in1=Y[:, 0:PB],
                                                op0=ALU.mult, op1=ALU.add), reads=[xs, ws], writes=[ys])


def load_halo(em, X, xs, src, t0, PB, T):
    lo, hi, c0, c1 = t0 - 1, t0 + PB + 1, 0, PB + 2
    if lo < 0:
        em.op("pool", "memset", dict(ap=X[:, 0:1], constant=0.0), writes=[xs])
        lo, c0 = 0, 1
    if hi > T:
        em.op("pool", "memset", dict(ap=X[:, PB + 1:PB + 2], constant=0.0), writes=[xs])
        hi, c1 = T, PB + 1
    em.dma("sp", X[:, c0:c1], src(lo, hi), xs, writes=[xs])


def build_p2(em, C, io, B, T):
    NB = NBC
    PB = NB * 64
    W = PB
    NBLK = T // PB
    NCH = T // 64
    par = sb(em, "p2par", [128, 16], F32)
    par2 = sb(em, "p2par2", [64, 8], F32)
    ps_ = Slot("par")
    em.dma("sp", par[:, 0:9], io["cw"], ps_, writes=[ps_])
    em.dma("sp", par[:, 9:12], io["csc"], ps_, writes=[ps_])
    em.dma("sp", par[:, 12:13], io["nw"], ps_, writes=[ps_])
    em.dma("sp", par2[:, 0:2], io["alog"], ps_, writes=[ps_])
    em.dma("sp", par2[:, 2:4], io["dtb"], ps_, writes=[ps_])
    em.op("act", "activation", dict(out=par2[:, 4:6], in_=par2[:, 0:2], func=AF.Exp), writes=[ps_])
    em.op("dve", "tensor_scalar_mul", dict(out=par2[:, 4:6], in0=par2[:, 4:6], scalar1=-1.0), writes=[ps_])
    cs = C["slot"]
    ones128 = C["ones128"]
    idf = C["idf"]
    rPc = Ring(em, "p2P", [128, 512], F32, 5, psum=True)

    rX = Ring(em, "p2aX", [128, PB + 2], F32, 6)
    rXb = Ring(em, "p2aXb", [128, PB], F32, 3)
    rY = Ring(em, "p2aY", [128, PB], F32, 8)
    rQ = Ring(em, "p2aQ", [128, PB], F32, 4)
    rO = Ring(em, "p2aO", [128, PB], BF16, 4)

    def p2a_A(b, blk):
        t0 = blk * PB
        g0 = b * T + t0
        ctx = dict(g0=g0, qk=[])
        Xs = []
        for kind in range(3):
            X, xs = rX.next()
            load_halo(em, X, xs, lambda a, c, kind=kind: io["xin"](kind, b, a, c), t0, PB, T)
            Xs.append((X, xs))
        Xb, xbs = rXb.next()
        em.dma("sp", Xb[:, 0:PB], io["xin"](4, b, t0, t0 + PB), xbs, writes=[xbs])
        Xc, xcs = rX.next()
        load_halo(em, Xc, xcs, lambda a, c: io["xin"](5, b, a, c), t0, PB, T)
        Xx, xxs = rX.next()
        load_halo(em, Xx, xxs, lambda a, c: io["xin"](6, b, a, c), t0, PB, T)
        em.op("pool", "tensor_tensor", dict(out=Xc[:], in0=Xc[:], in1=Xx[:], op=ALU.mult), reads=[xxs], writes=[xcs])
        streams = []
        for kind in range(3):
            Y, ys = rY.next()
            streams.append((Xs[kind][0], Xs[kind][1], Y, ys, par[:, kind * 3:kind * 3 + 3]))
        Ysc, yscs = rY.next()
        streams.append((Xc, xcs, Ysc, yscs, par[:, 9:12]))
        for (X, xs, Y, ys, w) in streams:
            em.op("act", "activation", dict(out=Y[:, 0:PB], in_=X[:, 1:PB + 1], func=AF.Copy, scale=w[:, 1:2]),
                  reads=[xs, ps_], writes=[ys])
        for (X, xs, Y, ys, w) in streams:
            em.op("dve", "scalar_tensor_tensor", dict(out=Y[:, 0:PB], in0=X[:, 0:PB], scalar=w[:, 0:1], in1=Y[:, 0:PB],
                                                      op0=ALU.mult, op1=ALU.add), reads=[xs, ps_], writes=[ys])
        for (X, xs, Y, ys, w) in streams:
            em.op("dve", "scalar_tensor_tensor", dict(out=Y[:, 0:PB], in0=X[:, 2:PB + 2], scalar=w[:, 2:3], in1=Y[:, 0:PB],
                                                      op0=ALU.mult, op1=ALU.add), reads=[xs, ps_], writes=[ys])
        for kind in range(3):
            Y, ys = streams[kind][2], streams[kind][3]
            em.op("act", "activation", dict(out=Y[:], in_=Y[:], func=AF.Silu), writes=[ys])
        for kind in range(2):
            Y, ys = streams[kind][2], streams[kind][3]
            Q, qs = rQ.next()
            em.op("pool", "tensor_tensor", dict(out=Q[:], in0=Y[:], in1=Y[:], op=ALU.mult), reads=[ys], writes=[qs])
            P, pss = rPc.next()
            em.op("pe", "matmul", dict(out=P[:, 0:W], lhsT=ones128[:], rhs=Q[:], start=True, stop=True),
                  reads=[qs, cs], writes=[pss])
            ctx["qk"].append((Y, ys, Q, qs, P, pss))
        em.dma("sp", io["vv"][:, g0:g0 + PB], streams[2][2][:], streams[2][3], reads=[streams[2][3]])
        Y, ys = Ysc, yscs
        ctx["sc"] = (Y, ys, Xb, xbs)
        return ctx

    def p2a_B(ctx):
        g0 = ctx["g0"]
        (Y0, ys0, Q0, qs0, P0, pss0), (Y1, ys1, Q1, qs1, P1, pss1) = ctx["qk"]
        em.op("act", "activation", dict(out=Q0[:], in_=P0[:, 0:W], func=AF.Sqrt, scale=float(HD), bias=float(HD) * L2_EPS),
              reads=[pss0], writes=[qs0])
        em.op("act", "activation", dict(out=Q1[:], in_=P1[:, 0:W], func=AF.Sqrt, bias=L2_EPS), reads=[pss1], writes=[qs1])
        em.op("dve", "reciprocal", dict(out=Q0[:], in_=Q0[:]), writes=[qs0])
        em.op("dve", "reciprocal", dict(out=Q1[:], in_=Q1[:]), writes=[qs1])
        O0, os0 = rO.next()
        O1, os1 = rO.next()
        em.op("pool", "tensor_tensor", dict(out=O0[:], in0=Y0[:], in1=Q0[:], op=ALU.mult), reads=[ys0, qs0], writes=[os0])
        em.op("pool", "tensor_tensor", dict(out=O1[:], in0=Y1[:], in1=Q1[:], op=ALU.mult), reads=[ys1, qs1], writes=[os1])
        em.dma("sp", io["qn"][:, g0:g0 + PB], O0[:], os0, reads=[os0])
        em.dma("sp", io["kn"][:, g0:g0 + PB], O1[:], os1, reads=[os1])
        Y, ys, Xb, xbs = ctx["sc"]
        O, os_ = rO.next()
        em.op("pool", "tensor_tensor", dict(out=O[:], in0=Y[:], in1=Xb[:, 0:PB], op=ALU.mult), reads=[ys, xbs], writes=[os_])
        em.dma("sp", io["uscT"][:, g0:g0 + PB], O[:], os_, reads=[os_])

    prev = None
    for b in range(B):
        for blk in range(NBLK):
            ctx = p2a_A(b, blk)
            if prev is not None:
                p2a_B(prev)
            prev = ctx
    p2a_B(prev)
    prep_done = []
    for r in (rO, rY):
        for s in r.slots:
            prep_done.extend(s.r)
    em.wait_all("sp", prep_done)

    scans = [(b, d) for b in range(B) for d in range(2)]
    NS = len(scans)
    abt = sb(em, "p2abt", [64, B, NCH, 4], F32)
    abs_ = Slot("abt")
    for b in range(B):
        em.dma("sp", abt[:, b, :, :], io["abc"][b * T:(b + 1) * T, :].rearrange("(n i) q -> i n q", i=64), abs_,
               writes=[abs_])
    colnames = ("ngc", "col1", "bege", "kds", "beta")
    col = [{k: sb(em, f"p2c_{k}{si}", [64, NCH], F32) for k in colnames} for si in range(NS)]
    cols = [Slot(f"col{si}") for si in range(NS)]
    tmpA = sb(em, "p2c_tA", [64, NCH], F32)
    tmpG = sb(em, "p2c_tG", [64, NCH], F32)
    ts_ = Slot("coltmp")
    assert NCH <= 512
    for si, (b, d) in enumerate(scans):
        c = col[si]
        s_ = cols[si]
        a_in = abt[:, b, :, d]
        b_in = abt[:, b, :, 2 + d]
        em.op("act", "activation", dict(out=tmpG[:], in_=a_in, func=AF.Exp, bias=par2[:, 2 + d:3 + d]),
              reads=[abs_, ps_], writes=[ts_])
        em.op("act", "activation", dict(out=tmpG[:], in_=tmpG[:], func=AF.Ln, bias=1.0), writes=[ts_])
        em.op("dve", "tensor_scalar_mul", dict(out=tmpG[:], in0=tmpG[:], scalar1=par2[:, 4 + d:5 + d]), writes=[ts_])
        P, pss = rPc.next()
        em.op("pe", "matmul", dict(out=P[0:64, 0:NCH], lhsT=C["tri"][d][:], rhs=tmpG[:], start=True, stop=True),
              reads=[ts_, cs], writes=[pss])
        P2, pss2 = rPc.next()
        em.op("pe", "matmul", dict(out=P2[0:64, 0:NCH], lhsT=C["ones"][:, 0:64], rhs=tmpG[:], start=True, stop=True),
              reads=[ts_, cs], writes=[pss2])
        em.op("dve", "tensor_scalar_mul", dict(out=c["ngc"][:], in0=P[0:64, 0:NCH], scalar1=-1.0), reads=[pss], writes=[s_])
        em.op("dve", "tensor_tensor", dict(out=c["kds"][:], in0=P2[0:64, 0:NCH], in1=c["ngc"][:], op=ALU.add),
              reads=[pss2], writes=[s_])
        em.op("act", "activation", dict(out=c["kds"][:], in_=c["kds"][:], func=AF.Exp), writes=[s_])
        em.op("act", "activation", dict(out=tmpA[:], in_=b_in, func=AF.Exp, scale=-1.0), reads=[abs_], writes=[ts_])
        em.op("act", "activation", dict(out=tmpA[:], in_=tmpA[:], func=AF.Ln, bias=1.0), writes=[ts_])
        em.op("dve", "scalar_tensor_tensor", dict(out=c["col1"][:], in0=tmpA[:], scalar=-1.0, in1=c["ngc"][:],
                                                  op0=ALU.mult, op1=ALU.subtract), reads=[ts_], writes=[s_])
        em.op("act", "activation", dict(out=c["beta"][:], in_=tmpA[:], func=AF.Exp, scale=-1.0), reads=[ts_], writes=[s_])
        em.op("act", "activation", dict(out=c["bege"][:], in_=c["col1"][:], func=AF.Exp), writes=[s_])

    tmp = []
    for si in range(NS):
        tmp.append(dict(
            rIn=Ring(em, f"p2bIn{si}_", [128, 2, W], BF16, 1), rInV=Ring(em, f"p2bInV{si}_", [128, W], F32, 1), rVb=Ring(em, f"p2bVb{si}_", [128, W], BF16, 1),
            rDX=Ring(em, f"p2bDX{si}_", [64, 2, W], F32, 1), rT3=Ring(em, f"p2bT3{si}_", [64, 3, W], F32, 1),
            rEG=Ring(em, f"p2bEG{si}_", [128, W], F32, 1), rM=Ring(em, f"p2bM{si}_", [64, 2, W], F32, 2),
            rPm=Ring(em, f"p2bPm{si}_", [64, W], F32, 2), rTt=Ring(em, f"p2bTt{si}_", [64, W], BF16, 1),
            rKV=Ring(em, f"p2bKV{si}_", [64, 2, NB * 128], BF16, 1)))
    res = [[dict(u=sb(em, f"p2r_u{si}{p}", [64, NB * 128], F32), wT=sb(em, f"p2r_w{si}{p}", [128, W], F32),
                 attnT=sb(em, f"p2r_a{si}{p}", [64, W], BF16), qdT=sb(em, f"p2r_q{si}{p}", [128, W], F32),
                 kd=sb(em, f"p2r_k{si}{p}", [64, NB * 128], BF16), gl=sb(em, f"p2r_g{si}{p}", [128, NB], F32),
                 slot=Slot(f"res{si}{p}")) for p in range(2)] for si in range(NS)]
    S = [sb(em, f"p2S{si}", [128, 128], F32) for si in range(NS)]
    Ss = [Slot(f"S{si}") for si in range(NS)]
    vnew = [sb(em, f"p2vn{si}", [64, 128], BF16) for si in range(NS)]
    vns = [Slot(f"vn{si}") for si in range(NS)]
    ps_vn = pst(em, "p2ps_vn", [64, NS, 128], F32)
    ps_ds = pst(em, "p2ps_ds", [128, NS, 128], F32)
    ps_o = pst(em, "p2ps_o", [128, NS, 2, 64], F32)
    _bvn, _bds, _bo = Slot("bank_vn"), Slot("bank_ds"), Slot("bank_o")
    pvs = [_bvn for _ in range(NS)]
    pds = [_bds for _ in range(NS)]
    pos = [[_bo, _bo] for _ in range(NS)]
    rOb = Ring(em, "p2bOb", [128, W], F32, NS + 2)
    Sb = [sb(em, f"p2Sb{si}", [128, 128], BF16) for si in range(NS)]
    Sbs = [Slot(f"Sb{si}") for si in range(NS)]
    for si in range(NS):
        em.op("pool", "memset", dict(ap=S[si][:], constant=0.0), writes=[Ss[si]])
        em.op("pool", "memset", dict(ap=Sb[si][:], constant=0.0), writes=[Sbs[si]])
    idb3 = idf[0:64, 0:64].unsqueeze(1).broadcast_to([64, NB, 64])

    def bc(colap, n0, n, inner):
        return colap[:, n0:n0 + n].unsqueeze(2).broadcast_to([64, n, inner])

    def precompute(si, blk, R):
        b, d = scans[si]
        c = col[si]
        cslot = cols[si]
        n0 = blk * NB
        g0 = b * T + blk * PB
        rs = R["slot"]
        mask1, mask2, mask3 = (C["SL"], C["SU"], C["IU"]) if d == 0 else (C["SU"], C["SL"], C["IL"])
        last = 63 if d == 0 else 0
        tm = tmp[si]
        rIn, rInV, rDX, rT3, rEG, rM, rPm, rTt, rKV = (tm[k] for k in ("rIn", "rInV", "rDX", "rT3", "rEG", "rM", "rPm", "rTt", "rKV"))
        KQ, kqs = rIn.next()
        em.dma("sp", KQ[:, 0, :], io["kn"][:, g0:g0 + PB], kqs, writes=[kqs])
        em.dma("sp", KQ[:, 1, :], io["qn"][:, g0:g0 + PB], kqs, writes=[kqs])
        V, vs_ = rInV.next()
        em.dma("sp", V[:], io["vv"][:, g0:g0 + PB], vs_, writes=[vs_])
        DX, dxs = rDX.next()
        em.op("dve", "tensor_tensor", dict(out=r3(DX[:, 0, :], 64), in0=idb3, in1=bc(c["ngc"], n0, NB, 64), op=ALU.mult),
              reads=[cslot, cs], writes=[dxs])
        em.op("pool", "tensor_tensor", dict(out=r3(DX[:, 1, :], 64), in0=idb3, in1=bc(c["col1"], n0, NB, 64), op=ALU.mult),
              reads=[cslot, cs], writes=[dxs])
        yield
        T3, t3s = rT3.next()
        PE_, pes = rPc.next()
        em.op("pe", "matmul", dict(out=PE_[:, 0:W], lhsT=C["nones"][:, :], rhs=DX[:, 0, :], start=True, stop=True),
              reads=[dxs, cs], writes=[pes])
        PA, pas = rPc.next()
        em.op("pe", "matmul", dict(out=PA[0:64, 0:W], lhsT=C["ones"][:, 0:64], rhs=DX[:, 1, :], start=True, stop=True),
              reads=[dxs, cs], writes=[pas])
        EG, egs = rEG.next()
        em.op("act", "activation", dict(out=EG[:], in_=PE_[:, 0:W], func=AF.Exp), reads=[pes], writes=[egs])
        em.op("act", "copy", dict(out=T3[:, 0, :], in_=PE_[0:64, 0:W]), reads=[pes], writes=[t3s])
        em.op("dve", "tensor_tensor", dict(out=r3(T3[:, 2, :], 64), in0=r3(T3[:, 0, :], 64), in1=bc(c["ngc"], n0, NB, 64),
                                           op=ALU.add), reads=[cslot], writes=[t3s])
        em.op("dve", "tensor_tensor", dict(out=r3(T3[:, 0, :], 64), in0=r3(T3[:, 0, :], 64), in1=bc(c["col1"], n0, NB, 64),
                                           op=ALU.subtract), reads=[cslot], writes=[t3s])
        em.op("dve", "tensor_tensor", dict(out=r3(T3[:, 1, :], 64), in0=r3(PA[0:64, 0:W], 64), in1=bc(c["ngc"], n0, NB, 64),
                                           op=ALU.add), reads=[pas, cslot], writes=[t3s])
        yield
        em.op("pool", "tensor_tensor", dict(out=T3[:, 0, :], in0=mask1[:], in1=T3[:, 0, :], op=ALU.subtract), reads=[cs], writes=[t3s])
        em.op("pool", "tensor_tensor", dict(out=T3[:, 1, :], in0=T3[:, 1, :], in1=mask2[:], op=ALU.add), reads=[cs], writes=[t3s])
        em.op("pool", "tensor_tensor", dict(out=T3[:, 2, :], in0=T3[:, 2, :], in1=mask3[:], op=ALU.add), reads=[cs], writes=[t3s])
        em.op("act", "activation", dict(out=T3[:], in_=T3[:], func=AF.Exp), writes=[t3s])
        yield
        Pkk, pkks = rPc.next()
        for n in range(NB):
            sl = slice(n * 64, (n + 1) * 64)
            em.op("pe", "matmul", dict(out=Pkk[0:64, sl], lhsT=KQ[:, 0, sl], rhs=KQ[:, 0, sl], start=True, stop=True),
                  reads=[kqs], writes=[pkks], sig=(n == NB - 1))
        Pqk, pqks = rPc.next()
        for n in range(NB):
            sl = slice(n * 64, (n + 1) * 64)
            em.op("pe", "matmul", dict(out=Pqk[0:64, sl], lhsT=KQ[:, 0, sl], rhs=KQ[:, 1, sl], start=True, stop=True),
                  reads=[kqs], writes=[pqks], sig=(n == NB - 1))
        M, ms = rM.next()
        em.op("dve", "tensor_tensor", dict(out=fr(M[:, 1, :]), in0=Pkk[0:64, 0:W], in1=T3[:, 0, :], op=ALU.mult),
              reads=[pkks, t3s], writes=[ms])
        em.op("dve", "tensor_tensor", dict(out=fr(M[:, 0, :]), in0=Pkk[0:64, 0:W], in1=T3[:, 1, :], op=ALU.mult),
              reads=[pkks, t3s], writes=[ms])
        em.op("dve", "tensor_tensor", dict(out=R["attnT"][:], in0=Pqk[0:64, 0:W], in1=T3[:, 2, :], op=ALU.mult),
              reads=[pqks, t3s], writes=[rs])
        em.op("pool", "tensor_copy", dict(out=R["gl"][:], in_=r3(EG[:], 64)[:, :, last]), reads=[egs], writes=[rs])
        em.op("pool", "tensor_tensor", dict(out=R["qdT"][:], in0=KQ[:, 1, :], in1=EG[:], op=ALU.mult),
              reads=[egs, kqs], writes=[rs])
        Pm, pms = rPm.next()
        em.op("pool", "tensor_tensor", dict(out=fr(r3(Pm[:], 64)), in0=idb3, in1=r3(M[:, 0, :], 64), op=ALU.subtract),
              reads=[ms, cs], writes=[pms])
        yield
        KV, kvs = rKV.next()
        Pt, pts = rPc.next()
        for n in range(NB):
            em.op("pe", "matmul", dict(out=Pt[0:64, n * 128:(n + 1) * 128], lhsT=KQ[:, 0, n * 64:(n + 1) * 64],
                                       rhs=C["idb"][:], start=True, stop=True), reads=[kqs, cs], writes=[pts], sig=(n == NB - 1))
        em.op("dve", "tensor_tensor", dict(out=r3(KV[:, 0, :], 128), in0=r3(Pt[0:64, 0:NB * 128], 128),
                                           in1=bc(c["bege"], n0, NB, 128), op=ALU.mult), reads=[pts, cslot], writes=[kvs])
        em.op("dve", "tensor_tensor", dict(out=r3(R["kd"][:], 128), in0=r3(Pt[0:64, 0:NB * 128], 128),
                                           in1=bc(c["kds"], n0, NB, 128), op=ALU.mult), reads=[pts, cslot], writes=[rs])
        Pt2, pts2 = rPc.next()
        for n in range(NB):
            em.op("pe", "matmul", dict(out=Pt2[0:64, n * 128:(n + 1) * 128], lhsT=V[:, n * 64:(n + 1) * 64],
                                       rhs=idf[:], start=True, stop=True), reads=[vs_, cs], writes=[pts2], sig=(n == NB - 1))
        em.op("dve", "tensor_tensor", dict(out=r3(KV[:, 1, :], 128), in0=r3(Pt2[0:64, 0:NB * 128], 128),
                                           in1=bc(c["beta"], n0, NB, 128), op=ALU.mult), reads=[pts2, cslot], writes=[kvs])
        yield
        for r in range(5):
            lastr = (r == 4)
            M2, m2s = rM.next()
            if not lastr:
                Pa, pas = rPc.next()
                for n in range(NB):
                    sl = slice(n * 64, (n + 1) * 64)
                    em.op("pe", "matmul", dict(out=Pa[0:64, sl], lhsT=fr(M[:, 1, sl]), rhs=fr(M[:, 0, sl]), start=True, stop=True),
                          reads=[ms], writes=[pas], sig=(n == NB - 1))
                em.op("act", "copy", dict(out=fr(M2[:, 0, :]), in_=Pa[0:64, 0:W]), reads=[pas], writes=[m2s])
            Pb, pbs = rPc.next()
            for n in range(NB):
                sl = slice(n * 64, (n + 1) * 64)
                em.op("pe", "matmul", dict(out=Pb[0:64, sl], lhsT=fr(M[:, 0, sl]), rhs=fr(M[:, 1, sl]), start=True, stop=True),
                      reads=[ms], writes=[pbs], sig=(n == NB - 1))
            em.op("dve", "tensor_copy", dict(out=fr(M2[:, 1, :]), in_=Pb[0:64, 0:W]), reads=[pbs], writes=[m2s])
            Pp, pps = rPc.next()
            for n in range(NB):
                sl = slice(n * 64, (n + 1) * 64)
                em.op("pe", "matmul", dict(out=Pp[0:64, sl], lhsT=fr(M2[:, 1, sl]), rhs=fr(Pm[:, sl]), start=True, stop=True),
                      reads=[m2s, pms], writes=[pps], sig=(n == NB - 1))
            Pm2, pms2 = rPm.next()
            em.op("dve", "tensor_tensor", dict(out=fr(Pm2[:]), in0=Pp[0:64, 0:W], in1=Pm[:], op=ALU.add),
                  reads=[pps, pms], writes=[pms2])
            M, ms = M2, m2s
            Pm, pms = Pm2, pms2
            yield
        Tt, tts = rTt.next()
        em.op("act", "copy", dict(out=Tt[:], in_=Pm[:]), reads=[pms], writes=[tts])
        Pu, pus = rPc.next()
        for n in range(NB):
            em.op("pe", "matmul", dict(out=Pu[0:64, n * 128:(n + 1) * 128], lhsT=Tt[:, n * 64:(n + 1) * 64],
                                       rhs=KV[:, 1, n * 128:(n + 1) * 128], start=True, stop=True),
                  reads=[tts, kvs], writes=[pus], sig=(n == NB - 1))
        em.op("act", "copy", dict(out=R["u"][:], in_=Pu[0:64, 0:NB * 128]), reads=[pus], writes=[rs])
        Pw, pws = rPc.next()
        for n in range(NB):
            em.op("pe", "matmul", dict(out=Pw[:, n * 64:(n + 1) * 64], lhsT=KV[:, 0, n * 128:(n + 1) * 128],
                                       rhs=Tt[:, n * 64:(n + 1) * 64], start=True, stop=True),
                  reads=[tts, kvs], writes=[pws], sig=(n == NB - 1))
        em.op("dve", "tensor_copy", dict(out=R["wT"][:], in_=Pw[:, 0:W]), reads=[pws], writes=[rs])
        yield

    def scan_step(si, n, R, Ob, obs):
        b, d = scans[si]
        rs = R["slot"]
        nn = n if d == 0 else NB - 1 - n
        sl = slice(nn * 64, (nn + 1) * 64)
        sle = slice(nn * 128, (nn + 1) * 128)
        pr = n % 2
        em.op("pe", "matmul", dict(out=ps_vn[:, si, :], lhsT=R["wT"][:, sl], rhs=S[si][:], start=True, stop=True),
              reads=[rs, Ss[si]], writes=[pvs[si]])
        em.op("dve", "tensor_tensor", dict(out=vnew[si][:], in0=R["u"][:, sle], in1=ps_vn[:, si, :], op=ALU.subtract),
              reads=[rs, pvs[si]], writes=[vns[si]])
        em.op("pe", "matmul", dict(out=ps_o[:, si, pr, :], lhsT=S[si][:], rhs=R["qdT"][:, sl], start=True, stop=False),
              reads=[rs, Ss[si]], writes=[pos[si][pr]], sig=False)
        em.op("pe", "matmul", dict(out=ps_o[:, si, pr, :], lhsT=vnew[si][:], rhs=R["attnT"][:, sl], start=False, stop=True),
              reads=[rs, vns[si]], writes=[pos[si][pr]], sig=False)
        em.op("pe", "matmul", dict(out=ps_ds[:, si, :], lhsT=R["kd"][:, sle], rhs=vnew[si][:], start=True, stop=True),
              reads=[rs, vns[si]], writes=[pds[si]])
        em.op("dve", "scalar_tensor_tensor", dict(out=S[si][:], in0=S[si][:], scalar=R["gl"][:, nn:nn + 1], in1=ps_ds[:, si, :],
                                                  op0=ALU.mult, op1=ALU.add), reads=[rs, pds[si]], writes=[Ss[si]])
        em.op("act", "copy", dict(out=Ob[:, sl], in_=ps_o[:, si, pr, :]), reads=[pos[si][pr]], writes=[obs])

    NSTAGE = 10
    for blk in range(NBLK + 1):
        gens = []
        if blk < NBLK:
            for si, (b, d) in enumerate(scans):
                bb = blk if d == 0 else NBLK - 1 - blk
                gens.append(precompute(si, bb, res[si][blk % 2]))
        def advance(k):
            for _ in range(k):
                for gi in range(len(gens)):
                    if gens[gi] is None:
                        continue
                    try:
                        next(gens[gi])
                    except StopIteration:
                        gens[gi] = None
        if blk == 0:
            advance(NSTAGE + 2)
            continue
        obufs = [rOb.next() for _ in range(NS)]
        per = -(-(NSTAGE + 1) // NB)
        for n in range(NB):
            for si in range(NS):
                scan_step(si, n, res[si][(blk - 1) % 2], obufs[si][0], obufs[si][1])
            advance(per)
        advance(NSTAGE + 2)
        for si, (b, d) in enumerate(scans):
            bb = (blk - 1) if d == 0 else NBLK - 1 - (blk - 1)
            g0 = b * T + bb * PB
            em.dma("sp", io["oT"][d, :, g0:g0 + PB], obufs[si][0][:], obufs[si][1], reads=[obufs[si][1]])
    fin = []
    for s in rOb.slots:
        fin.extend(s.r)
    em.wait_all("sp", fin)

    def p2c_A(b, blk):
        g0 = b * T + blk * PB
        F0, f0s = rY.next()
        F1, f1s = rY.next()
        F2, f2s = rY.next()
        em.dma("sp", F0[:], io["oT"][0, :, g0:g0 + PB], f0s, writes=[f0s])
        em.dma("sp", F1[:], io["oT"][1, :, g0:g0 + PB], f1s, writes=[f1s])
        em.dma("sp", F2[:], io["xin"](3, b, blk * PB, blk * PB + PB), f2s, writes=[f2s])
        em.op("dve", "tensor_tensor", dict(out=F0[:], in0=F0[:], in1=F1[:], op=ALU.add), reads=[f1s], writes=[f0s])
        em.op("pool", "tensor_tensor", dict(out=F1[:], in0=F0[:], in1=F0[:], op=ALU.mult), reads=[f0s], writes=[f1s])
        P, pss = rPc.next()
        em.op("pe", "matmul", dict(out=P[:, 0:W], lhsT=ones128[:], rhs=F1[:], start=True, stop=True), reads=[f1s, cs], writes=[pss])
        em.op("act", "activation", dict(out=F2[:], in_=F2[:], func=AF.Silu), writes=[f2s])
        return (g0, F0, f0s, F1, f1s, F2, f2s, P, pss)

    def p2c_B(ctx):
        g0, F0, f0s, F1, f1s, F2, f2s, P, pss = ctx
        em.op("act", "activation", dict(out=F1[:], in_=P[:, 0:W], func=AF.Sqrt, scale=1.0 / HD, bias=RMS_EPS), reads=[pss], writes=[f1s])
        em.op("dve", "reciprocal", dict(out=F1[:], in_=F1[:]), writes=[f1s])
        em.op("pool", "tensor_tensor", dict(out=F0[:], in0=F0[:], in1=F1[:], op=ALU.mult), reads=[f1s], writes=[f0s])
        O, os_ = rO.next()
        em.op("dve", "scalar_tensor_tensor", dict(out=O[:], in0=F0[:], scalar=par[:, 12:13], in1=F2[:], op0=ALU.mult, op1=ALU.mult),
              reads=[f0s, f2s, ps_], writes=[os_])
        em.dma("sp", io["gdnT"][:, g0:g0 + PB], O[:], os_, reads=[os_])

    prev = None
    for b in range(B):
        for blk in range(NBLK):
            ctx = p2c_A(b, blk)
            if prev is not None:
                p2c_B(prev)
            prev = ctx
    p2c_B(prev)
    fin = []
    for s_ in rO.slots:
        fin.extend(s_.r)
    em.wait_all("sp", fin)


def run_p2_program(xin, abc, cw, csc, nw, alog, dtb, B, T):
    NTOK = B * T
    nc = bass.Bass("TRN2", target_bir_lowering=False)
    with contextlib.ExitStack() as st:
        em = Em(nc, st)
        d_x = nc.dram_tensor("xin", [7, 128, NTOK], F32, kind="ExternalInput").ap()
        d_abc = nc.dram_tensor("abc", [NTOK, 4], F32, kind="ExternalInput").ap()
        d_cw = nc.dram_tensor("cw", [128, 9], F32, kind="ExternalInput").ap()
        d_csc = nc.dram_tensor("csc", [128, 3], F32, kind="ExternalInput").ap()
        d_nw = nc.dram_tensor("nw", [128, 1], F32, kind="ExternalInput").ap()
        d_al = nc.dram_tensor("alog", [64, 2], F32, kind="ExternalInput").ap()
        d_dt = nc.dram_tensor("dtb", [64, 2], F32, kind="ExternalInput").ap()
        d_g = nc.dram_tensor("gdnT", [128, NTOK], BF16, kind="ExternalOutput").ap()
        d_u = nc.dram_tensor("uscT", [128, NTOK], BF16, kind="ExternalOutput").ap()
        io = dict(
            xin=lambda kind, b, a, c: d_x[kind, :, b * T + a:b * T + c], abc=d_abc, cw=d_cw, csc=d_csc, nw=d_nw,
            alog=d_al, dtb=d_dt, gdnT=d_g, uscT=d_u,
            qn=nc.dram_tensor("s_qn", [128, NTOK], BF16, kind="Internal").ap(),
            kn=nc.dram_tensor("s_kn", [128, NTOK], BF16, kind="Internal").ap(),
            vv=nc.dram_tensor("s_vv", [128, NTOK], F32, kind="Internal").ap(),
            oT=nc.dram_tensor("s_oT", [2, 128, NTOK], F32, kind="Internal").ap())
        C = make_consts(em)
        build_p2(em, C, io, B, T)
        em.finalize()
    print("P2 insts", em.ninst, "sems", em.nsem, flush=True)
    n = len(xin)
    in_maps = [dict(xin=xin[c], abc=abc[c], cw=cw[c], csc=csc[c], nw=nw[c], alog=alog[c], dtb=dtb[c]) for c in range(n)]
    res = run_bass_kernel_spmd(nc, in_maps, core_ids=list(range(n)))
    return [r["gdnT"] for r in res.results], [r["uscT"] for r in res.results]


def p1_dest(io, g):
    if g < 4096:
        return io["send"][(g % 1024) // 128, g // 1024, :, :]
    g2 = g - 4128
    kind, h = 4 + g2 // 1024, (g2 % 1024) // 128
    if kind <= 6:
        return io["send"][h, kind, :, :]
    return io["gateT"][kind - 7, h * 128:(h + 1) * 128, :]


def build_p1(em, C, io, TOK):
    cs = C["slot"]
    NT = TOK // 128
    xT = sb(em, "p1xT", [128, 8, TOK], BF16)
    xTs = Slot("xT")
    rXl = Ring(em, "p1x", [128, 1024], F32, 2)
    rXb = Ring(em, "p1xb", [128, 1024], BF16, 2)
    rP = Ring(em, "p1P", [128, 512], F32, 6, psum=True)
    for t in range(NT):
        X, xs = rXl.next()
        em.dma("sp", X[:], io["x"][t * 128:(t + 1) * 128, :], xs, writes=[xs])
        Xb, xbs = rXb.next()
        em.op("act", "copy", dict(out=Xb[:], in_=X[:]), reads=[xs], writes=[xbs])
        for hf in range(2):
            P, pss = rP.next()
            for k in range(4):
                kk = hf * 4 + k
                em.op("pe", "matmul", dict(out=P[:, k * 128:(k + 1) * 128], lhsT=Xb[:, kk * 128:(kk + 1) * 128],
                                           rhs=C["idb"][:], start=True, stop=True), reads=[xbs, cs], writes=[pss], sig=(k == 3))
            em.op("dve", "tensor_copy", dict(out=xT[:, hf * 4:(hf + 1) * 4, t * 128:(t + 1) * 128], in_=r3(P[:, :], 128)),
                  reads=[pss], writes=[xTs])
    CW = 512
    rWf = Ring(em, "p1wf", [128, 8, CW], F32, 2)
    rWb = Ring(em, "p1wb", [128, 8, CW], BF16, 2)
    rSt = Ring(em, "p1st", [128, 512], F32, 4)
    wv = io["w_in"].rearrange("(kc p) c -> p kc c", p=128)
    slabs = [(c0, 512) for c0 in range(0, 4096, 512)] + [(4096, 32)] + [(c0, 512) for c0 in range(4128, 9248, 512)]
    TB = min(512, TOK)
    NTB = TOK // TB
    ev = 0
    for (c0, cw_) in slabs:
        Wf, wfs = rWf.next()
        em.dma("sp", Wf[:, :, 0:cw_], wv[:, :, c0:c0 + cw_], wfs, writes=[wfs])
        Wb, wbs = rWb.next()
        em.op("pool", "tensor_copy", dict(out=Wb[:, :, 0:cw_], in_=Wf[:, :, 0:cw_]), reads=[wfs], writes=[wbs])
        if cw_ == 32:
            for t in range(NT):
                P, pss = rP.next()
                for k in range(8):
                    em.op("pe", "matmul", dict(out=P[:, 0:32], lhsT=xT[:, k, t * 128:(t + 1) * 128], rhs=Wb[:, k, 0:32],
                                               start=(k == 0), stop=(k == 7)), reads=[xTs, wbs], writes=[pss], sig=(k == 7))
                St, sts = rSt.next()
                em.op("act", "copy", dict(out=St[:, 0:32], in_=P[:, 0:32]), reads=[pss], writes=[sts])
                em.dma("sp", io["abT"][t * 128:(t + 1) * 128, :], St[:, 0:32], sts, reads=[sts])
            continue
        for j in range(cw_ // 128):
            dst = p1_dest(io, c0 + j * 128)
            for tb in range(NTB):
                P, pss = rP.next()
                for k in range(8):
                    em.op("pe", "matmul", dict(out=P[:, 0:TB], lhsT=Wb[:, k, j * 128:(j + 1) * 128], rhs=xT[:, k, tb * TB:(tb + 1) * TB],
                                               start=(k == 0), stop=(k == 7)), reads=[xTs, wbs], writes=[pss], sig=(k == 7))
                St, sts = rSt.next()
                if ev % 2 == 0:
                    em.op("act", "copy", dict(out=St[:, 0:TB], in_=P[:, 0:TB]), reads=[pss], writes=[sts])
                else:
                    em.op("dve", "tensor_copy", dict(out=St[:, 0:TB], in_=P[:, 0:TB]), reads=[pss], writes=[sts])
                ev += 1
                em.dma("sp", dst[:, tb * TB:(tb + 1) * TB], St[:, 0:TB], sts, reads=[sts])
    fin = []
    for s in rSt.slots:
        fin.extend(s.r)
    em.wait_all("sp", fin)


def load_weight_bf16(em, Wb, wbs, src3, nk, ncols, rSt, slabc, eng="pool"):
    for c0 in range(0, ncols, slabc):
        Wf, wfs = rSt.next()
        em.dma("sp", Wf[:, 0:nk, 0:slabc], src3[:, :, c0:c0 + slabc], wfs, writes=[wfs])
        em.op(eng, "tensor_copy", dict(out=Wb[:, :, c0:c0 + slabc], in_=Wf[:, 0:nk, 0:slabc]), reads=[wfs], writes=[wbs])


def layer_norm(em, Rr, rs, gbc, bbc, cslot, rSm):
    Sm, sms = rSm.next()
    em.op("dve", "bn_stats", dict(out=Sm[:, 0:6], in_=Rr[:, 0:512]), reads=[rs], writes=[sms])
    em.op("dve", "bn_stats", dict(out=Sm[:, 6:12], in_=Rr[:, 512:1024]), reads=[rs], writes=[sms])
    em.op("dve", "bn_aggr", dict(out=Sm[:, 12:14], in_=Sm[:, 0:12]), writes=[sms])
    em.op("act", "activation", dict(out=Sm[:, 14:15], in_=Sm[:, 13:14], func=AF.Sqrt, bias=LN_EPS), writes=[sms])
    em.op("dve", "reciprocal", dict(out=Sm[:, 14:15], in_=Sm[:, 14:15]), writes=[sms])
    em.op("dve", "tensor_scalar", dict(out=Rr[:], in0=Rr[:], scalar1=Sm[:, 12:13], scalar2=Sm[:, 14:15],
                                       op0=ALU.subtract, op1=ALU.mult), reads=[sms], writes=[rs])
    em.op("pool", "tensor_tensor", dict(out=Rr[:], in0=Rr[:], in1=gbc[:], op=ALU.mult), reads=[cslot], writes=[rs])
    em.op("pool", "tensor_tensor", dict(out=Rr[:], in0=Rr[:], in1=bbc[:], op=ALU.add), reads=[cslot], writes=[rs])


def build_p3a(em, C, io, TOK):
    cs = C["slot"]
    rSt = Ring(em, "p3awf", [128, 8, 256], F32, 2)
    W = {}
    for nm in ("wg", "ws", "wo"):
        W[nm] = (sb(em, f"p3a_{nm}", [128, 8, 1024], BF16), Slot(nm))
        load_weight_bf16(em, W[nm][0], W[nm][1], io[nm].rearrange("(kc p) c -> p kc c", p=128), 8, 1024, rSt, 256)
    lnc = sb(em, "p3a_ln", [128, 2, 1024], F32)
    lns = Slot("ln")
    em.dma("sp", lnc[:, 0, :], io["lng"], lns, writes=[lns])
    em.dma("sp", lnc[:, 1, :], io["lnb"], lns, writes=[lns])
    TBA = min(256, TOK)
    rG = Ring(em, "p3aG", [128, 2, 8, TBA], BF16, 2)
    rGT = Ring(em, "p3aGT", [128, 2, 8, TBA], F32, 2)
    rT = Ring(em, "p3aT", [128, 2, TBA], F32, 2)
    rMx = Ring(em, "p3aMx", [128, 8, TBA], BF16, 2)
    rX = Ring(em, "p3aX", [128, 1024], F32, 3)
    rSm = Ring(em, "p3aSm", [128, 16], F32, 2)
    rP = Ring(em, "p3aP", [128, 512], F32, 6, psum=True)
    gv = io["gdnT"].rearrange("(j p) t -> p j t", p=128)
    uv = io["uscT"].rearrange("(j p) t -> p j t", p=128)
    for tb in range(TOK // TBA):
        tsl = slice(tb * TBA, (tb + 1) * TBA)
        G, gs = rG.next()
        em.dma("sp", G[:, 0, :, :], gv[:, :, tsl], gs, writes=[gs])
        em.dma("sp", G[:, 1, :, :], uv[:, :, tsl], gs, writes=[gs])
        GT, gts = rGT.next()
        for gi in range(2):
            em.dma("sp", GT[:, gi, :, :], io["gateT"][gi].rearrange("(j p) t -> p j t", p=128)[:, :, tsl], gts, writes=[gts])
        em.op("act", "activation", dict(out=GT[:], in_=GT[:], func=AF.Sigmoid), writes=[gts])
        Mx, mxs = rMx.next()
        for m in range(8):
            Ps = []
            for gi, nm in enumerate(("wg", "ws")):
                P, pss = rP.next()
                for j in range(8):
                    em.op("pe", "matmul", dict(out=P[:, 0:TBA], lhsT=W[nm][0][:, j, m * 128:(m + 1) * 128], rhs=G[:, gi, j, :],
                                               start=(j == 0), stop=(j == 7)), reads=[W[nm][1], gs], writes=[pss], sig=(j == 7))
                Ps.append((P, pss))
            Tt, tts = rT.next()
            for gi in range(2):
                em.op("dve", "tensor_tensor", dict(out=Tt[:, gi, :], in0=GT[:, gi, m, :], in1=Ps[gi][0][:, 0:TBA], op=ALU.mult),
                      reads=[gts, Ps[gi][1]], writes=[tts])
            em.op("pool", "tensor_tensor", dict(out=Mx[:, m, :], in0=Tt[:, 0, :], in1=Tt[:, 1, :], op=ALU.add),
                  reads=[tts], writes=[mxs])
        for tile in range(TBA // 128):
            t0 = tb * TBA + tile * 128
            X, xs = rX.next()
            em.dma("sp", X[:], io["x"][t0:t0 + 128, :], xs, writes=[xs])
            for hf in range(2):
                P, pss = rP.next()
                for j in range(8):
                    em.op("pe", "matmul", dict(out=P[:, :], lhsT=Mx[:, j, tile * 128:(tile + 1) * 128],
                                               rhs=W["wo"][0][:, j, hf * 512:(hf + 1) * 512], start=(j == 0), stop=(j == 7)),
                          reads=[W["wo"][1], mxs], writes=[pss], sig=(j == 7))
                em.op("dve", "scalar_tensor_tensor", dict(out=X[:, hf * 512:(hf + 1) * 512], in0=X[:, hf * 512:(hf + 1) * 512],
                                                          scalar=float(ALPHA), in1=P[:, :], op0=ALU.mult, op1=ALU.add),
                      reads=[pss], writes=[xs])
            layer_norm(em, X, xs, lnc[:, 0, :], lnc[:, 1, :], lns, rSm)
            em.dma("sp", io["x1"][t0:t0 + 128, :], X[:], xs, reads=[xs])
    fin = []
    for s in rX.slots:
        fin.extend(s.r)
    em.wait_all("sp", fin)


def build_p3b(em, C, io, TOK):
    cs = C["slot"]
    rSt = Ring(em, "p3bwf", [128, 8, 128], F32, 2)
    Wup = sb(em, "p3b_wup", [128, 8, 4096], BF16)
    wus = Slot("wup")
    Wdn = sb(em, "p3b_wdn", [128, 32, 1024], BF16)
    wds = Slot("wdn")
    load_weight_bf16(em, Wup, wus, io["wup"].rearrange("(kc p) c -> p kc c", p=128), 8, 4096, rSt, 128)
    dv = io["wdn"].rearrange("(fc p) c -> p fc c", p=128)
    for fc in range(32):
        Wf, wfs = rSt.next()
        Wf2 = Wf[:].rearrange("p a b -> p (a b)")
        em.dma("sp", Wf2, dv[:, fc, :], wfs, writes=[wfs])
        em.op("pool", "tensor_copy", dict(out=Wdn[:, fc, :], in_=Wf2), reads=[wfs], writes=[wds])
    cst = sb(em, "p3b_c", [128, 3, 1024], F32)
    bup = sb(em, "p3b_bup", [128, 32], F32)
    cs2 = Slot("p3bc")
    em.dma("sp", cst[:, 0, :], io["lng"], cs2, writes=[cs2])
    em.dma("sp", cst[:, 1, :], io["lnb"], cs2, writes=[cs2])
    em.dma("sp", cst[:, 2, :], io["bdn"], cs2, writes=[cs2])
    em.dma("sp", bup[:], io["bup"], cs2, writes=[cs2])
    TBB = min(256, TOK)
    NTL = TBB // 128
    rX = Ring(em, "p3bX", [128, 1024], F32, 2 * NTL)
    rXb = Ring(em, "p3bXb", [128, 1024], BF16, 1)
    rXT = Ring(em, "p3bXT", [128, 8, TBB], BF16, 1)
    rH = Ring(em, "p3bH", [128, TBB], F32, 2)
    rHT = Ring(em, "p3bHT", [128, 32, TBB], BF16, 1)
    rSm = Ring(em, "p3bSm", [128, 16], F32, 2)
    rP = Ring(em, "p3bP", [128, 512], F32, 6, psum=True)
    for tb in range(TOK // TBB):
        XT, xts = rXT.next()
        tiles = []
        for tile in range(NTL):
            t0 = tb * TBB + tile * 128
            X, xs = rX.next()
            em.dma("sp", X[:], io["x1"][t0:t0 + 128, :], xs, writes=[xs])
            tiles.append((X, xs, t0))
            Xb, xbs = rXb.next()
            em.op("act", "copy", dict(out=Xb[:], in_=X[:]), reads=[xs], writes=[xbs])
            for hf in range(2):
                P, pss = rP.next()
                for k in range(4):
                    kk = hf * 4 + k
                    em.op("pe", "matmul", dict(out=P[:, k * 128:(k + 1) * 128], lhsT=Xb[:, kk * 128:(kk + 1) * 128],
                                               rhs=C["idb"][:], start=True, stop=True), reads=[xbs, cs], writes=[pss], sig=(k == 3))
                em.op("dve", "tensor_copy", dict(out=XT[:, hf * 4:(hf + 1) * 4, tile * 128:(tile + 1) * 128], in_=r3(P[:, :], 128)),
                      reads=[pss], writes=[xts])
        HT, hts = rHT.next()
        for f in range(32):
            P, pss = rP.next()
            for k in range(8):
                em.op("pe", "matmul", dict(out=P[:, 0:TBB], lhsT=Wup[:, k, f * 128:(f + 1) * 128], rhs=XT[:, k, :],
                                           start=(k == 0), stop=(k == 7)), reads=[wus, xts], writes=[pss], sig=(k == 7))
            H, hs = rH.next()
            em.op("act", "activation", dict(out=H[:], in_=P[:, 0:TBB], func=AF.Relu, bias=bup[:, f:f + 1]),
                  reads=[pss, cs2], writes=[hs])
            em.op("pool", "tensor_tensor", dict(out=HT[:, f, :], in0=H[:], in1=H[:], op=ALU.mult), reads=[hs], writes=[hts])
        for tile in range(NTL):
            X, xs, t0 = tiles[tile]
            for hf in range(2):
                P, pss = rP.next()
                for f in range(32):
                    em.op("pe", "matmul", dict(out=P[:, :], lhsT=HT[:, f, tile * 128:(tile + 1) * 128],
                                               rhs=Wdn[:, f, hf * 512:(hf + 1) * 512], start=(f == 0), stop=(f == 31)),
                          reads=[wds, hts], writes=[pss], sig=(f == 31))
                em.op("dve", "scalar_tensor_tensor", dict(out=X[:, hf * 512:(hf + 1) * 512], in0=X[:, hf * 512:(hf + 1) * 512],
                                                          scalar=float(ALPHA), in1=P[:, :], op0=ALU.mult, op1=ALU.add),
                      reads=[pss], writes=[xs])
            em.op("pool", "tensor_tensor", dict(out=X[:], in0=X[:], in1=cst[:, 2, :], op=ALU.add), reads=[cs2], writes=[xs])
            layer_norm(em, X, xs, cst[:, 0, :], cst[:, 1, :], cs2, rSm)
            em.dma("sp", io["x2"][t0:t0 + 128, :], X[:], xs, reads=[xs])
    fin = []
    for s in rX.slots:
        fin.extend(s.r)
    em.wait_all("sp", fin)


def _launch(build, ins, outs, internals, arrays, ncores=8):
    nc = bass.Bass("TRN2", target_bir_lowering=False)
    with contextlib.ExitStack() as st:
        em = Em(nc, st)
        io = {}
        for nm, (shape, dt) in ins.items():
            io[nm] = nc.dram_tensor(nm, list(shape), dt, kind="ExternalInput").ap()
        for nm, (shape, dt) in outs.items():
            io[nm] = nc.dram_tensor(nm, list(shape), dt, kind="ExternalOutput").ap()
        for nm, (shape, dt) in internals.items():
            io[nm] = nc.dram_tensor(nm, list(shape), dt, kind="Internal").ap()
        C = make_consts(em)
        build(em, C, io)
        em.finalize()
    if _DBG.get("trace"):
        res = run_bass_kernel_spmd(nc, arrays, core_ids=list(range(ncores)), trace=True)
        _DBG["ns"] = res.exec_time_ns
        _DBG["res"] = res
        _DBG["ninst"] = dict(em.ninst)
    else:
        res = run_bass_kernel_spmd(nc, arrays, core_ids=list(range(ncores)))
    return [{nm: r[nm] for nm in outs} for r in res.results]


def run_p1(x_sh, w_in_l, TOK):
    ins = dict(x=((TOK, 1024), F32), w_in=((1024, IN_COLS), F32))
    outs = dict(send=((8, 7, 128, TOK), F32), abT=((TOK, 32), F32), gateT=((2, 1024, TOK), F32))
    return _launch(lambda em, C, io: build_p1(em, C, io, TOK), ins, outs, {}, [dict(x=x_sh[c], w_in=w_in_l) for c in range(8)])


def run_p2(xin, abc, cw, csc, nw, alog, dtb, B, T):
    NTOK = B * T
    ins = dict(xin=((7, 128, NTOK), F32), abc=((NTOK, 4), F32), cw=((128, 9), F32), csc=((128, 3), F32),
               nw=((128, 1), F32), alog=((64, 2), F32), dtb=((64, 2), F32))
    outs = dict(gdnT=((128, NTOK), BF16), uscT=((128, NTOK), BF16))
    internals = dict(qn=((128, NTOK), BF16), kn=((128, NTOK), BF16), vv=((128, NTOK), F32), oT=((2, 128, NTOK), F32))

    def build(em, C, io):
        d_x = io["xin"]
        io["xin"] = lambda kind, b, a, c: d_x[kind, :, b * T + a:b * T + c]
        build_p2(em, C, io, B, T)
    arrays = [dict(xin=xin[c], abc=abc[c], cw=cw[c], csc=csc[c], nw=nw[c], alog=alog[c], dtb=dtb[c]) for c in range(8)]
    return _launch(build, ins, outs, internals, arrays)


def run_p3a(gdnT, uscT, gateT, x_sh, wg, ws, wo, lng, lnb, TOK):
    ins = dict(gdnT=((1024, TOK), BF16), uscT=((1024, TOK), BF16), gateT=((2, 1024, TOK), F32), x=((TOK, 1024), F32),
               wg=((1024, 1024), F32), ws=((1024, 1024), F32), wo=((1024, 1024), F32), lng=((128, 1024), F32), lnb=((128, 1024), F32))
    outs = dict(x1=((TOK, 1024), F32))
    arrays = [dict(gdnT=gdnT[c], uscT=uscT[c], gateT=gateT[c], x=x_sh[c], wg=wg, ws=ws, wo=wo, lng=lng, lnb=lnb) for c in range(8)]
    return _launch(lambda em, C, io: build_p3a(em, C, io, TOK), ins, outs, {}, arrays)


def run_p3b(x1, wup, bup, wdn, bdn, lng, lnb, TOK):
    ins = dict(x1=((TOK, 1024), F32), wup=((1024, 4096), F32), bup=((128, 32), F32), wdn=((4096, 1024), F32),
               bdn=((128, 1024), F32), lng=((128, 1024), F32), lnb=((128, 1024), F32))
    outs = dict(x2=((TOK, 1024), F32))
    arrays = [dict(x1=x1[c], wup=wup, bup=bup, wdn=wdn, bdn=bdn, lng=lng, lnb=lnb) for c in range(8)]
    return _launch(lambda em, C, io: build_p3b(em, C, io, TOK), ins, outs, {}, arrays)


def bcast128(v):
    return np.ascontiguousarray(np.broadcast_to(np.asarray(v, np.float32)[None, :], (128, v.shape[0])))


def forward_unfused(inp, B, T, depth):
    TOK = B * T // 8
    x = np.ascontiguousarray(inp["x"], dtype=np.float32).reshape(B * T, 1024)
    x_sh = [np.ascontiguousarray(x[c * TOK:(c + 1) * TOK]) for c in range(8)]
    for l in range(depth):
        o1 = run_p1(x_sh, np.ascontiguousarray(inp["w_in"][l]), TOK)
        xin = [np.ascontiguousarray(np.concatenate([o1[c]["send"][h] for c in range(8)], axis=2)) for h in range(8)]
        abT = np.concatenate([o1[c]["abT"] for c in range(8)], axis=0)
        abc = [np.ascontiguousarray(abT[:, [h, 8 + h, 16 + h, 24 + h]]) for h in range(8)]
        cq = inp["conv_qkv"][l]
        cw = [np.ascontiguousarray(np.concatenate([cq[:, k * 1024 + h * 128:k * 1024 + (h + 1) * 128].T for k in range(3)], axis=1))
              for h in range(8)]
        csc = [np.ascontiguousarray(inp["conv_sc"][l][:, h * 128:(h + 1) * 128].T) for h in range(8)]
        nw = [np.ascontiguousarray(inp["gdn_norm_w"][l].reshape(128, 1))] * 8
        alog = [np.ascontiguousarray(np.broadcast_to(inp["a_log"][l][:, h][None, :], (64, 2))) for h in range(8)]
        dtb = [np.ascontiguousarray(np.broadcast_to(inp["dt_bias"][l][:, h][None, :], (64, 2))) for h in range(8)]
        o2 = run_p2(xin, abc, cw, csc, nw, alog, dtb, B, T)
        gd = np.concatenate([o2[h]["gdnT"] for h in range(8)], axis=0)
        us = np.concatenate([o2[h]["uscT"] for h in range(8)], axis=0)
        gdn_sh = [np.ascontiguousarray(gd[:, c * TOK:(c + 1) * TOK]) for c in range(8)]
        usc_sh = [np.ascontiguousarray(us[:, c * TOK:(c + 1) * TOK]) for c in range(8)]
        gate_sh = [o1[c]["gateT"] for c in range(8)]
        o3 = run_p3a(gdn_sh, usc_sh, gate_sh, x_sh, np.ascontiguousarray(inp["w_o_gdn"][l]), np.ascontiguousarray(inp["w_o_sc"][l]),
                     np.ascontiguousarray(inp["w_out"][l]), bcast128(inp["ln1_g"][l]), bcast128(inp["ln1_b"][l]), TOK)
        x1 = [o3[c]["x1"] for c in range(8)]
        o4 = run_p3b(x1, np.ascontiguousarray(inp["w_up"][l]), np.ascontiguousarray(inp["b_up"][l].reshape(32, 128).T),
                     np.ascontiguousarray(inp["w_down"][l]), bcast128(inp["b_down"][l]), bcast128(inp["ln2_g"][l]),
                     bcast128(inp["ln2_b"][l]), TOK)
        x_sh = [o4[c]["x2"] for c in range(8)]
    return np.concatenate(x_sh, axis=0).reshape(B, T, 1024).astype(np.float32)


def kernel(**inputs):
    inp = {k: np.asarray(v) for k, v in inputs.items()}
    B, T = inp["x"].shape[0], inp["x"].shape[1]
    return forward_unfused(inp, B, T, inp["w_in"].shape[0])
```

```python
import contextlib
import numpy as np
import ml_dtypes
import concourse.bass as bass
import concourse.mybir as mybir
from concourse.bass_utils import run_bass_kernel_spmd

F32 = mybir.dt.float32
BF16 = mybir.dt.bfloat16
F32R = mybir.dt.float32r


def fr(ap):
    return ap
ALU = mybir.AluOpType
AF = mybir.ActivationFunctionType

D_MODEL = 1024
N_HEADS = 8
HD = 128
DEPTH = 4
D_FF = 4096
IN_COLS = 9248
ALPHA = (2 * DEPTH) ** 0.25
LN_EPS = 1e-5
RMS_EPS = 1e-6
L2_EPS = 1e-6
NEG = -30000.0

EPOCH = 12000
NBC = 4
SAME_ENGINE_SYNC = True
_DBG = {}


class Slot:
    __slots__ = ("w", "r", "dsem", "dcnt", "name")

    def __init__(self, name=""):
        self.w = None
        self.r = []
        self.dsem = None
        self.dcnt = 0
        self.name = name


class Em:
    ENG = ("pe", "dve", "act", "pool", "sp")

    def __init__(self, nc, stack):
        self.nc = nc
        self.stack = stack
        self.q = {e: [] for e in self.ENG}
        self.cnt = {e: 0 for e in self.ENG}
        self.epoch = {e: 0 for e in self.ENG}
        self.waited = {e: {} for e in self.ENG}
        self.sems = {}
        self.nsem = 0
        self.ninst = {e: 0 for e in self.ENG}

    def sem(self, key):
        if key not in self.sems:
            self.nsem += 1
            self.sems[key] = self.stack.enter_context(self.nc.semaphore(f"s{self.nsem}"))
        return self.sems[key]

    def _waits(self, eng, deps):
        w = []
        for d in deps:
            if d is None:
                continue
            key, val = d
            if key[0] == eng:
                if not SAME_ENGINE_SYNC:
                    continue
                if key[1] == self.epoch[eng] and val > self.cnt[eng]:
                    continue
            if self.waited[eng].get(key, 0) >= val:
                continue
            self.waited[eng][key] = val
            w.append((self.sem(key), val))
        return w

    def op(self, eng, meth, kw, reads=(), writes=(), deps=(), sig=True):
        fn = (meth, kw)
        alld = list(deps)
        for s in reads:
            alld.append(s.w)
        for s in writes:
            alld.append(s.w)
            alld.extend(s.r)
        waits = self._waits(eng, alld)
        if sig:
            self.cnt[eng] += 1
            key = (eng, self.epoch[eng])
            tok = (key, self.cnt[eng])
            semh = self.sem(key)
            if self.cnt[eng] >= EPOCH:
                self.epoch[eng] += 1
                self.cnt[eng] = 0
        else:
            key = (eng, self.epoch[eng])
            tok = (key, self.cnt[eng] + 1)
            semh = None
        self.q[eng].append((waits, fn, semh, 1))
        self.ninst[eng] += 1
        for s in reads:
            s.r.append(tok)
        for s in writes:
            s.w = tok
            s.r = []
        return tok

    def dma(self, eng, out, in_, slot, reads=(), writes=(), deps=()):
        alld = list(deps)
        for s in reads:
            alld.append(s.w)
        for s in writes:
            alld.append(s.w)
            alld.extend(s.r)
        waits = self._waits(eng, alld)
        if slot.dsem is None:
            slot.dsem = ("d", id(slot))
        slot.dcnt += 16
        tok = (slot.dsem, slot.dcnt)
        semh = self.sem(slot.dsem)
        self.q[eng].append((waits, ("dma_start", dict(out=out, in_=in_)), semh, 16))
        self.ninst[eng] += 1
        for s in reads:
            s.r.append(tok)
        for s in writes:
            s.w = tok
            s.r = []
        return tok

    def wait_all(self, eng, toks):
        waits = self._waits(eng, toks)
        self.q[eng].append((waits, None, None, 0))

    def finalize(self):
        nc = self.nc
        block = self.stack.enter_context(nc.Block())
        qs = self.q

        def run(e, lst):
            for waits, fn, semh, inc in lst:
                for (s, v) in waits:
                    e.wait_ge(s, v)
                if fn is None:
                    continue
                ins = getattr(e, fn[0])(**fn[1])
                if semh is not None:
                    ins.then_inc(semh, inc)

        @block.tensor
        def _(e):
            run(e, qs["pe"])

        @block.vector
        def _(e):
            run(e, qs["dve"])

        @block.scalar
        def _(e):
            run(e, qs["act"])

        @block.gpsimd
        def _(e):
            run(e, qs["pool"])

        @block.sync
        def _(e):
            run(e, qs["sp"])


def sb(em, name, shape, dt):
    return em.stack.enter_context(em.nc.sbuf_tensor(name, shape, dt))


def pst(em, name, shape, dt):
    return em.stack.enter_context(em.nc.psum_tensor(name, shape, dt))


def make_consts(em):
    C = {}
    C["idf"] = sb(em, "c_idf", [128, 128], F32)
    C["idb"] = sb(em, "c_idb", [128, 128], BF16)
    C["ones"] = sb(em, "c_ones", [64, 128], F32)
    C["nones"] = sb(em, "c_nones", [64, 128], F32)
    C["tri"] = [sb(em, f"c_tri{d}", [64, 64], F32) for d in range(2)]
    for nm in ("SL", "SU", "IU", "IL"):
        C[nm] = sb(em, f"c_{nm}", [64, NBC * 64], F32)
    s = Slot("consts")
    C["slot"] = s
    idf, idb = C["idf"], C["idb"]
    C["ones128"] = sb(em, "c_ones128", [128, 128], F32)
    em.op("pool", "memset", dict(ap=C["ones128"][:], constant=1.0), writes=[s])
    em.op("pool", "memset", dict(ap=idf[:], constant=0.0), writes=[s])
    em.op("pool", "affine_select", dict(out=idf[:], in_=idf[:], pattern=[[-1, 128]], compare_op=ALU.not_equal,
                                        fill=1.0, base=0, channel_multiplier=1), writes=[s])
    em.op("pool", "tensor_copy", dict(out=idb[:], in_=idf[:]), writes=[s])
    em.op("pool", "memset", dict(ap=C["ones"][:], constant=1.0), writes=[s])
    em.op("pool", "memset", dict(ap=C["nones"][:], constant=-1.0), writes=[s])
    for d in range(2):
        em.op("pool", "memset", dict(ap=C["tri"][d][:], constant=1.0), writes=[s])
    em.op("pool", "affine_select", dict(out=C["tri"][0][:], in_=C["tri"][0][:], pattern=[[1, 64]],
                                        compare_op=ALU.is_ge, fill=0.0, base=0, channel_multiplier=-1), writes=[s])
    em.op("pool", "affine_select", dict(out=C["tri"][1][:], in_=C["tri"][1][:], pattern=[[-1, 64]],
                                        compare_op=ALU.is_ge, fill=0.0, base=0, channel_multiplier=1), writes=[s])
    specs = {"SL": (1, -1, ALU.is_gt), "SU": (-1, 1, ALU.is_gt), "IU": (-1, 1, ALU.is_ge), "IL": (1, -1, ALU.is_ge)}
    for nm, (cm, jm, cmpop) in specs.items():
        m = C[nm]
        em.op("pool", "memset", dict(ap=m[:], constant=0.0), writes=[s])
        em.op("pool", "affine_select", dict(
            out=m[:].rearrange("p (n j) -> p n j", j=64), in_=m[:].rearrange("p (n j) -> p n j", j=64),
            pattern=[[0, NBC], [jm, 64]], compare_op=cmpop, fill=NEG, base=0, channel_multiplier=cm), writes=[s])
    return C


class Ring:
    def __init__(self, em, name, shape, dt, n, psum=False):
        mk = pst if psum else sb
        self.tiles = [mk(em, f"{name}{i}", shape, dt) for i in range(n)]
        self.slots = [Slot(f"{name}{i}") for i in range(n)]
        self.i = 0
        self.n = n

    def next(self):
        t, s = self.tiles[self.i], self.slots[self.i]
        self.i = (self.i + 1) % self.n
        return t, s


def r3(ap, inner):
    return ap.rearrange("p (n j) -> p n j", j=inner)


def conv3(em, engs, Y, ys, X, xs, w, ws, PB):
    if engs[0] == "act":
        em.op("act", "activation", dict(out=Y[:, 0:PB], in_=X[:, 1:PB + 1], func=AF.Copy, scale=w[:, 1:2]),
              reads=[xs, ws], writes=[ys])
    else:
        em.op(engs[0], "tensor_scalar_mul", dict(out=Y[:, 0:PB], in0=X[:, 1:PB + 1], scalar1=w[:, 1:2]),
              reads=[xs, ws], writes=[ys])
    em.op("dve", "scalar_tensor_tensor", dict(out=Y[:, 0:PB], in0=X[:, 0:PB], scalar=w[:, 0:1], in1=Y[:, 0:PB],
                                                op0=ALU.mult, op1=ALU.add), reads=[xs, ws], writes=[ys])
    em.op("dve", "scalar_tensor_tensor", dict(out=Y[:, 0:PB], in0=X[:, 2:PB + 2], scalar=w[:, 2:3], in1=Y[:, 0:PB],
                                                op0=ALU.mult, op1=ALU.add), reads=[xs, ws], writes=[ys])


def load_halo(em, X, xs, src, t0, PB, T):
    lo, hi, c0, c1 = t0 - 1, t0 + PB + 1, 0, PB + 2
    if lo < 0:
        em.op("pool", "memset", dict(ap=X[:, 0:1], constant=0.0), writes=[xs])
        lo, c0 = 0, 1
    if hi > T:
        em.op("pool", "memset", dict(ap=X[:, PB + 1:PB + 2], constant=0.0), writes=[xs])
        hi, c1 = T, PB + 1
    em.dma("sp", X[:, c0:c1], src(lo, hi), xs, writes=[xs])


def build_p2(em, C, io, B, T):
    NB = NBC
    PB = NB * 64
    W = PB
    NBLK = T // PB
    NCH = T // 64
    par = sb(em, "p2par", [128, 16], F32)
    par2 = sb(em, "p2par2", [64, 8], F32)
    ps_ = Slot("par")
    em.dma("sp", par[:, 0:9], io["cw"], ps_, writes=[ps_])
    em.dma("sp", par[:, 9:12], io["csc"], ps_, writes=[ps_])
    em.dma("sp", par[:, 12:13], io["nw"], ps_, writes=[ps_])
    em.dma("sp", par2[:, 0:2], io["alog"], ps_, writes=[ps_])
    em.dma("sp", par2[:, 2:4], io["dtb"], ps_, writes=[ps_])
    em.op("act", "activation", dict(out=par2[:, 4:6], in_=par2[:, 0:2], func=AF.Exp), writes=[ps_])
    em.op("dve", "tensor_scalar_mul", dict(out=par2[:, 4:6], in0=par2[:, 4:6], scalar1=-1.0), writes=[ps_])
    cs = C["slot"]
    ones128 = C["ones128"]
    idf = C["idf"]
    rPc = Ring(em, "p2P", [128, 512], F32, 5, psum=True)

    rX = Ring(em, "p2aX", [128, PB + 2], F32, 6)
    rXb = Ring(em, "p2aXb", [128, PB], F32, 3)
    rY = Ring(em, "p2aY", [128, PB], F32, 8)
    rQ = Ring(em, "p2aQ", [128, PB], F32, 4)
    rO = Ring(em, "p2aO", [128, PB], BF16, 4)

    def p2a_A(b, blk):
        t0 = blk * PB
        g0 = b * T + t0
        ctx = dict(g0=g0, qk=[])
        Xs = []
        for kind in range(3):
            X, xs = rX.next()
            load_halo(em, X, xs, lambda a, c, kind=kind: io["xin"](kind, b, a, c), t0, PB, T)
            Xs.append((X, xs))
        Xb, xbs = rXb.next()
        em.dma("sp", Xb[:, 0:PB], io["xin"](4, b, t0, t0 + PB), xbs, writes=[xbs])
        Xc, xcs = rX.next()
        load_halo(em, Xc, xcs, lambda a, c: io["xin"](5, b, a, c), t0, PB, T)
        Xx, xxs = rX.next()
        load_halo(em, Xx, xxs, lambda a, c: io["xin"](6, b, a, c), t0, PB, T)
        em.op("pool", "tensor_tensor", dict(out=Xc[:], in0=Xc[:], in1=Xx[:], op=ALU.mult), reads=[xxs], writes=[xcs])
        streams = []
        for kind in range(3):
            Y, ys = rY.next()
            streams.append((Xs[kind][0], Xs[kind][1], Y, ys, par[:, kind * 3:kind * 3 + 3]))
        Ysc, yscs = rY.next()
        streams.append((Xc, xcs, Ysc, yscs, par[:, 9:12]))
        for (X, xs, Y, ys, w) in streams:
            em.op("act", "activation", dict(out=Y[:, 0:PB], in_=X[:, 1:PB + 1], func=AF.Copy, scale=w[:, 1:2]),
                  reads=[xs, ps_], writes=[ys])
        for (X, xs, Y, ys, w) in streams:
            em.op("dve", "scalar_tensor_tensor", dict(out=Y[:, 0:PB], in0=X[:, 0:PB], scalar=w[:, 0:1], in1=Y[:, 0:PB],
                                                      op0=ALU.mult, op1=ALU.add), reads=[xs, ps_], writes=[ys])
        for (X, xs, Y, ys, w) in streams:
            em.op("dve", "scalar_tensor_tensor", dict(out=Y[:, 0:PB], in0=X[:, 2:PB + 2], scalar=w[:, 2:3], in1=Y[:, 0:PB],
                                                      op0=ALU.mult, op1=ALU.add), reads=[xs, ps_], writes=[ys])
        for kind in range(3):
            Y, ys = streams[kind][2], streams[kind][3]
            em.op("act", "activation", dict(out=Y[:], in_=Y[:], func=AF.Silu), writes=[ys])
        for kind in range(2):
            Y, ys = streams[kind][2], streams[kind][3]
            Q, qs = rQ.next()
            em.op("pool", "tensor_tensor", dict(out=Q[:], in0=Y[:], in1=Y[:], op=ALU.mult), reads=[ys], writes=[qs])
            P, pss = rPc.next()
            em.op("pe", "matmul", dict(out=P[:, 0:W], lhsT=ones128[:], rhs=Q[:], start=True, stop=True),
                  reads=[qs, cs], writes=[pss])
            ctx["qk"].append((Y, ys, Q, qs, P, pss))
        em.dma("sp", io["vv"][:, g0:g0 + PB], streams[2][2][:], streams[2][3], reads=[streams[2][3]])
        Y, ys = Ysc, yscs
        ctx["sc"] = (Y, ys, Xb, xbs)
        return ctx

    def p2a_B(ctx):
        g0 = ctx["g0"]
        (Y0, ys0, Q0, qs0, P0, pss0), (Y1, ys1, Q1, qs1, P1, pss1) = ctx["qk"]
        em.op("act", "activation", dict(out=Q0[:], in_=P0[:, 0:W], func=AF.Sqrt, scale=float(HD), bias=float(HD) * L2_EPS),
              reads=[pss0], writes=[qs0])
        em.op("act", "activation", dict(out=Q1[:], in_=P1[:, 0:W], func=AF.Sqrt, bias=L2_EPS), reads=[pss1], writes=[qs1])
        em.op("dve", "reciprocal", dict(out=Q0[:], in_=Q0[:]), writes=[qs0])
        em.op("dve", "reciprocal", dict(out=Q1[:], in_=Q1[:]), writes=[qs1])
        O0, os0 = rO.next()
        O1, os1 = rO.next()
        em.op("pool", "tensor_tensor", dict(out=O0[:], in0=Y0[:], in1=Q0[:], op=ALU.mult), reads=[ys0, qs0], writes=[os0])
        em.op("pool", "tensor_tensor", dict(out=O1[:], in0=Y1[:], in1=Q1[:], op=ALU.mult), reads=[ys1, qs1], writes=[os1])
        em.dma("sp", io["qn"][:, g0:g0 + PB], O0[:], os0, reads=[os0])
        em.dma("sp", io["kn"][:, g0:g0 + PB], O1[:], os1, reads=[os1])
        Y, ys, Xb, xbs = ctx["sc"]
        O, os_ = rO.next()
        em.op("pool", "tensor_tensor", dict(out=O[:], in0=Y[:], in1=Xb[:, 0:PB], op=ALU.mult), reads=[ys, xbs], writes=[os_])
        em.dma("sp", io["uscT"][:, g0:g0 + PB], O[:], os_, reads=[os_])

    prev = None
    for b in range(B):
        for blk in range(NBLK):
            ctx = p2a_A(b, blk)
            if prev is not None:
                p2a_B(prev)
            prev = ctx
    p2a_B(prev)
    prep_done = []
    for r in (rO, rY):
        for s in r.slots:
            prep_done.extend(s.r)
    em.wait_all("sp", prep_done)

    scans = [(b, d) for b in range(B) for d in range(2)]
    NS = len(scans)
    abt = sb(em, "p2abt", [64, B, NCH, 4], F32)
    abs_ = Slot("abt")
    for b in range(B):
        em.dma("sp", abt[:, b, :, :], io["abc"][b * T:(b + 1) * T, :].rearrange("(n i) q -> i n q", i=64), abs_,
               writes=[abs_])
    colnames = ("ngc", "col1", "bege", "kds", "beta")
    col = [{k: sb(em, f"p2c_{k}{si}", [64, NCH], F32) for k in colnames} for si in range(NS)]
    cols = [Slot(f"col{si}") for si in range(NS)]
    tmpA = sb(em, "p2c_tA", [64, NCH], F32)
    tmpG = sb(em, "p2c_tG", [64, NCH], F32)
    ts_ = Slot("coltmp")
    assert NCH <= 512
    for si, (b, d) in enumerate(scans):
        c = col[si]
        s_ = cols[si]
        a_in = abt[:, b, :, d]
        b_in = abt[:, b, :, 2 + d]
        em.op("act", "activation", dict(out=tmpG[:], in_=a_in, func=AF.Exp, bias=par2[:, 2 + d:3 + d]),
              reads=[abs_, ps_], writes=[ts_])
        em.op("act", "activation", dict(out=tmpG[:], in_=tmpG[:], func=AF.Ln, bias=1.0), writes=[ts_])
        em.op("dve", "tensor_scalar_mul", dict(out=tmpG[:], in0=tmpG[:], scalar1=par2[:, 4 + d:5 + d]), writes=[ts_])
        P, pss = rPc.next()
        em.op("pe", "matmul", dict(out=P[0:64, 0:NCH], lhsT=C["tri"][d][:], rhs=tmpG[:], start=True, stop=True),
              reads=[ts_, cs], writes=[pss])
        P2, pss2 = rPc.next()
        em.op("pe", "matmul", dict(out=P2[0:64, 0:NCH], lhsT=C["ones"][:, 0:64], rhs=tmpG[:], start=True, stop=True),
              reads=[ts_, cs], writes=[pss2])
        em.op("dve", "tensor_scalar_mul", dict(out=c["ngc"][:], in0=P[0:64, 0:NCH], scalar1=-1.0), reads=[pss], writes=[s_])
        em.op("dve", "tensor_tensor", dict(out=c["kds"][:], in0=P2[0:64, 0:NCH], in1=c["ngc"][:], op=ALU.add),
              reads=[pss2], writes=[s_])
        em.op("act", "activation", dict(out=c["kds"][:], in_=c["kds"][:], func=AF.Exp), writes=[s_])
        em.op("act", "activation", dict(out=tmpA[:], in_=b_in, func=AF.Exp, scale=-1.0), reads=[abs_], writes=[ts_])
        em.op("act", "activation", dict(out=tmpA[:], in_=tmpA[:], func=AF.Ln, bias=1.0), writes=[ts_])
        em.op("dve", "scalar_tensor_tensor", dict(out=c["col1"][:], in0=tmpA[:], scalar=-1.0, in1=c["ngc"][:],
                                                  op0=ALU.mult, op1=ALU.subtract), reads=[ts_], writes=[s_])
        em.op("act", "activation", dict(out=c["beta"][:], in_=tmpA[:], func=AF.Exp, scale=-1.0), reads=[ts_], writes=[s_])
        em.op("act", "activation", dict(out=c["bege"][:], in_=c["col1"][:], func=AF.Exp), writes=[s_])

    tmp = []
    for si in range(NS):
        tmp.append(dict(
            rIn=Ring(em, f"p2bIn{si}_", [128, 2, W], BF16, 1), rInV=Ring(em, f"p2bInV{si}_", [128, W], F32, 1), rVb=Ring(em, f"p2bVb{si}_", [128, W], BF16, 1),
            rDX=Ring(em, f"p2bDX{si}_", [64, 2, W], F32, 1), rT3=Ring(em, f"p2bT3{si}_", [64, 3, W], F32, 1),
            rEG=Ring(em, f"p2bEG{si}_", [128, W], F32, 1), rM=Ring(em, f"p2bM{si}_", [64, 2, W], F32, 2),
            rPm=Ring(em, f"p2bPm{si}_", [64, W], F32, 2), rTt=Ring(em, f"p2bTt{si}_", [64, W], BF16, 1),
            rKV=Ring(em, f"p2bKV{si}_", [64, 2, NB * 128], BF16, 1)))
    res = [[dict(u=sb(em, f"p2r_u{si}{p}", [64, NB * 128], F32), wT=sb(em, f"p2r_w{si}{p}", [128, W], F32),
                 attnT=sb(em, f"p2r_a{si}{p}", [64, W], BF16), qdT=sb(em, f"p2r_q{si}{p}", [128, W], F32),
                 kd=sb(em, f"p2r_k{si}{p}", [64, NB * 128], BF16), gl=sb(em, f"p2r_g{si}{p}", [128, NB], F32),
                 slot=Slot(f"res{si}{p}")) for p in range(2)] for si in range(NS)]
    S = [sb(em, f"p2S{si}", [128, 128], F32) for si in range(NS)]
    Ss = [Slot(f"S{si}") for si in range(NS)]
    vnew = [sb(em, f"p2vn{si}", [64, 128], BF16) for si in range(NS)]
    vns = [Slot(f"vn{si}") for si in range(NS)]
    ps_vn = pst(em, "p2ps_vn", [64, NS, 128], F32)
    ps_ds = pst(em, "p2ps_ds", [128, NS, 128], F32)
    ps_o = pst(em, "p2ps_o", [128, NS, 2, 64], F32)
    _bvn, _bds, _bo = Slot("bank_vn"), Slot("bank_ds"), Slot("bank_o")
    pvs = [_bvn for _ in range(NS)]
    pds = [_bds for _ in range(NS)]
    pos = [[_bo, _bo] for _ in range(NS)]
    rOb = Ring(em, "p2bOb", [128, W], F32, NS + 2)
    Sb = [sb(em, f"p2Sb{si}", [128, 128], BF16) for si in range(NS)]
    Sbs = [Slot(f"Sb{si}") for si in range(NS)]
    for si in range(NS):
        em.op("pool", "memset", dict(ap=S[si][:], constant=0.0), writes=[Ss[si]])
        em.op("pool", "memset", dict(ap=Sb[si][:], constant=0.0), writes=[Sbs[si]])
    idb3 = idf[0:64, 0:64].unsqueeze(1).broadcast_to([64, NB, 64])

    def bc(colap, n0, n, inner):
        return colap[:, n0:n0 + n].unsqueeze(2).broadcast_to([64, n, inner])

    def precompute(si, blk, R):
        b, d = scans[si]
        c = col[si]
        cslot = cols[si]
        n0 = blk * NB
        g0 = b * T + blk * PB
        rs = R["slot"]
        mask1, mask2, mask3 = (C["SL"], C["SU"], C["IU"]) if d == 0 else (C["SU"], C["SL"], C["IL"])
        last = 63 if d == 0 else 0
        tm = tmp[si]
        rIn, rInV, rDX, rT3, rEG, rM, rPm, rTt, rKV = (tm[k] for k in ("rIn", "rInV", "rDX", "rT3", "rEG", "rM", "rPm", "rTt", "rKV"))
        KQ, kqs = rIn.next()
        em.dma("sp", KQ[:, 0, :], io["kn"][:, g0:g0 + PB], kqs, writes=[kqs])
        em.dma("sp", KQ[:, 1, :], io["qn"][:, g0:g0 + PB], kqs, writes=[kqs])
        V, vs_ = rInV.next()
        em.dma("sp", V[:], io["vv"][:, g0:g0 + PB], vs_, writes=[vs_])
        DX, dxs = rDX.next()
        em.op("dve", "tensor_tensor", dict(out=r3(DX[:, 0, :], 64), in0=idb3, in1=bc(c["ngc"], n0, NB, 64), op=ALU.mult),
              reads=[cslot, cs], writes=[dxs])
        em.op("pool", "tensor_tensor", dict(out=r3(DX[:, 1, :], 64), in0=idb3, in1=bc(c["col1"], n0, NB, 64), op=ALU.mult),
              reads=[cslot, cs], writes=[dxs])
        yield
        T3, t3s = rT3.next()
        PE_, pes = rPc.next()
        em.op("pe", "matmul", dict(out=PE_[:, 0:W], lhsT=C["nones"][:, :], rhs=DX[:, 0, :], start=True, stop=True),
              reads=[dxs, cs], writes=[pes])
        PA, pas = rPc.next()
        em.op("pe", "matmul", dict(out=PA[0:64, 0:W], lhsT=C["ones"][:, 0:64], rhs=DX[:, 1, :], start=True, stop=True),
              reads=[dxs, cs], writes=[pas])
        EG, egs = rEG.next()
        em.op("act", "activation", dict(out=EG[:], in_=PE_[:, 0:W], func=AF.Exp), reads=[pes], writes=[egs])
        em.op("act", "copy", dict(out=T3[:, 0, :], in_=PE_[0:64, 0:W]), reads=[pes], writes=[t3s])
        em.op("dve", "tensor_tensor", dict(out=r3(T3[:, 2, :], 64), in0=r3(T3[:, 0, :], 64), in1=bc(c["ngc"], n0, NB, 64),
                                           op=ALU.add), reads=[cslot], writes=[t3s])
        em.op("dve", "tensor_tensor", dict(out=r3(T3[:, 0, :], 64), in0=r3(T3[:, 0, :], 64), in1=bc(c["col1"], n0, NB, 64),
                                           op=ALU.subtract), reads=[cslot], writes=[t3s])
        em.op("dve", "tensor_tensor", dict(out=r3(T3[:, 1, :], 64), in0=r3(PA[0:64, 0:W], 64), in1=bc(c["ngc"], n0, NB, 64),
                                           op=ALU.add), reads=[pas, cslot], writes=[t3s])
        yield
        em.op("pool", "tensor_tensor", dict(out=T3[:, 0, :], in0=mask1[:], in1=T3[:, 0, :], op=ALU.subtract), reads=[cs], writes=[t3s])
        em.op("pool", "tensor_tensor", dict(out=T3[:, 1, :], in0=T3[:, 1, :], in1=mask2[:], op=ALU.add), reads=[cs], writes=[t3s])
        em.op("pool", "tensor_tensor", dict(out=T3[:, 2, :], in0=T3[:, 2, :], in1=mask3[:], op=ALU.add), reads=[cs], writes=[t3s])
        em.op("act", "activation", dict(out=T3[:], in_=T3[:], func=AF.Exp), writes=[t3s])
        yield
        Pkk, pkks = rPc.next()
        for n in range(NB):
            sl = slice(n * 64, (n + 1) * 64)
            em.op("pe", "matmul", dict(out=Pkk[0:64, sl], lhsT=KQ[:, 0, sl], rhs=KQ[:, 0, sl], start=True, stop=True),
                  reads=[kqs], writes=[pkks], sig=(n == NB - 1))
        Pqk, pqks = rPc.next()
        for n in range(NB):
            sl = slice(n * 64, (n + 1) * 64)
            em.op("pe", "matmul", dict(out=Pqk[0:64, sl], lhsT=KQ[:, 0, sl], rhs=KQ[:, 1, sl], start=True, stop=True),
                  reads=[kqs], writes=[pqks], sig=(n == NB - 1))
        M, ms = rM.next()
        em.op("dve", "tensor_tensor", dict(out=fr(M[:, 1, :]), in0=Pkk[0:64, 0:W], in1=T3[:, 0, :], op=ALU.mult),
              reads=[pkks, t3s], writes=[ms])
        em.op("dve", "tensor_tensor", dict(out=fr(M[:, 0, :]), in0=Pkk[0:64, 0:W], in1=T3[:, 1, :], op=ALU.mult),
              reads=[pkks, t3s], writes=[ms])
        em.op("dve", "tensor_tensor", dict(out=R["attnT"][:], in0=Pqk[0:64, 0:W], in1=T3[:, 2, :], op=ALU.mult),
              reads=[pqks, t3s], writes=[rs])
        em.op("pool", "tensor_copy", dict(out=R["gl"][:], in_=r3(EG[:], 64)[:, :, last]), reads=[egs], writes=[rs])
        em.op("pool", "tensor_tensor", dict(out=R["qdT"][:], in0=KQ[:, 1, :], in1=EG[:], op=ALU.mult),
              reads=[egs, kqs], writes=[rs])
        Pm, pms = rPm.next()
        em.op("pool", "tensor_tensor", dict(out=fr(r3(Pm[:], 64)), in0=idb3, in1=r3(M[:, 0, :], 64), op=ALU.subtract),
              reads=[ms, cs], writes=[pms])
        yield
        KV, kvs = rKV.next()
        Pt, pts = rPc.next()
        for n in range(NB):
            em.op("pe", "matmul", dict(out=Pt[0:64, n * 128:(n + 1) * 128], lhsT=KQ[:, 0, n * 64:(n + 1) * 64],
                                       rhs=C["idb"][:], start=True, stop=True), reads=[kqs, cs], writes=[pts], sig=(n == NB - 1))
        em.op("dve", "tensor_tensor", dict(out=r3(KV[:, 0, :], 128), in0=r3(Pt[0:64, 0:NB * 128], 128),
                                           in1=bc(c["bege"], n0, NB, 128), op=ALU.mult), reads=[pts, cslot], writes=[kvs])
        em.op("dve", "tensor_tensor", dict(out=r3(R["kd"][:], 128), in0=r3(Pt[0:64, 0:NB * 128], 128),
                                           in1=bc(c["kds"], n0, NB, 128), op=ALU.mult), reads=[pts, cslot], writes=[rs])
        Pt2, pts2 = rPc.next()
        for n in range(NB):
            em.op("pe", "matmul", dict(out=Pt2[0:64, n * 128:(n + 1) * 128], lhsT=V[:, n * 64:(n + 1) * 64],
                                       rhs=idf[:], start=True, stop=True), reads=[vs_, cs], writes=[pts2], sig=(n == NB - 1))
        em.op("dve", "tensor_tensor", dict(out=r3(KV[:, 1, :], 128), in0=r3(Pt2[0:64, 0:NB * 128], 128),
                                           in1=bc(c["beta"], n0, NB, 128), op=ALU.mult), reads=[pts2, cslot], writes=[kvs])
        yield
        for r in range(5):
            lastr = (r == 4)
            M2, m2s = rM.next()
            if not lastr:
                Pa, pas = rPc.next()
                for n in range(NB):
                    sl = slice(n * 64, (n + 1) * 64)
                    em.op("pe", "matmul", dict(out=Pa[0:64, sl], lhsT=fr(M[:, 1, sl]), rhs=fr(M[:, 0, sl]), start=True, stop=True),
                          reads=[ms], writes=[pas], sig=(n == NB - 1))
                em.op("act", "copy", dict(out=fr(M2[:, 0, :]), in_=Pa[0:64, 0:W]), reads=[pas], writes=[m2s])
            Pb, pbs = rPc.next()
            for n in range(NB):
                sl = slice(n * 64, (n + 1) * 64)
                em.op("pe", "matmul", dict(out=Pb[0:64, sl], lhsT=fr(M[:, 0, sl]), rhs=fr(M[:, 1, sl]), start=True, stop=True),
                      reads=[ms], writes=[pbs], sig=(n == NB - 1))
            em.op("dve", "tensor_copy", dict(out=fr(M2[:, 1, :]), in_=Pb[0:64, 0:W]), reads=[pbs], writes=[m2s])
            Pp, pps = rPc.next()
            for n in range(NB):
                sl = slice(n * 64, (n + 1) * 64)
                em.op("pe", "matmul", dict(out=Pp[0:64, sl], lhsT=fr(M2[:, 1, sl]), rhs=fr(Pm[:, sl]), start=True, stop=True),
                      reads=[m2s, pms], writes=[pps], sig=(n == NB - 1))
            Pm2, pms2 = rPm.next()
            em.op("dve", "tensor_tensor", dict(out=fr(Pm2[:]), in0=Pp[0:64, 0:W], in1=Pm[:], op=ALU.add),
                  reads=[pps, pms], writes=[pms2])
            M, ms = M2, m2s
            Pm, pms = Pm2, pms2
            yield
        Tt, tts = rTt.next()
        em.op("act", "copy", dict(out=Tt[:], in_=Pm[:]), reads=[pms], writes=[tts])
        Pu, pus = rPc.next()
        for n in range(NB):
            em.op("pe", "matmul", dict(out=Pu[0:64, n * 128:(n + 1) * 128], lhsT=Tt[:, n * 64:(n + 1) * 64],
                                       rhs=KV[:, 1, n * 128:(n + 1) * 128], start=True, stop=True),
                  reads=[tts, kvs], writes=[pus], sig=(n == NB - 1))
        em.op("act", "copy", dict(out=R["u"][:], in_=Pu[0:64, 0:NB * 128]), reads=[pus], writes=[rs])
        Pw, pws = rPc.next()
        for n in range(NB):
            em.op("pe", "matmul", dict(out=Pw[:, n * 64:(n + 1) * 64], lhsT=KV[:, 0, n * 128:(n + 1) * 128],
                                       rhs=Tt[:, n * 64:(n + 1) * 64], start=True, stop=True),
                  reads=[tts, kvs], writes=[pws], sig=(n == NB - 1))
        em.op("dve", "tensor_copy", dict(out=R["wT"][:], in_=Pw[:, 0:W]), reads=[pws], writes=[rs])
        yield

    def scan_step(si, n, R, Ob, obs):
        b, d = scans[si]
        rs = R["slot"]
        nn = n if d == 0 else NB - 1 - n
        sl = slice(nn * 64, (nn + 1) * 64)
        sle = slice(nn * 128, (nn + 1) * 128)
        pr = n % 2
        em.op("pe", "matmul", dict(out=ps_vn[:, si, :], lhsT=R["wT"][:, sl], rhs=S[si][:], start=True, stop=True),
              reads=[rs, Ss[si]], writes=[pvs[si]])
        em.op("dve", "tensor_tensor", dict(out=vnew[si][:], in0=R["u"][:, sle], in1=ps_vn[:, si, :], op=ALU.subtract),
              reads=[rs, pvs[si]], writes=[vns[si]])
        em.op("pe", "matmul", dict(out=ps_o[:, si, pr, :], lhsT=S[si][:], rhs=R["qdT"][:, sl], start=True, stop=False),
              reads=[rs, Ss[si]], writes=[pos[si][pr]], sig=False)
        em.op("pe", "matmul", dict(out=ps_o[:, si, pr, :], lhsT=vnew[si][:], rhs=R["attnT"][:, sl], start=False, stop=True),
              reads=[rs, vns[si]], writes=[pos[si][pr]], sig=False)
        em.op("pe", "matmul", dict(out=ps_ds[:, si, :], lhsT=R["kd"][:, sle], rhs=vnew[si][:], start=True, stop=True),
              reads=[rs, vns[si]], writes=[pds[si]])
        em.op("dve", "scalar_tensor_tensor", dict(out=S[si][:], in0=S[si][:], scalar=R["gl"][:, nn:nn + 1], in1=ps_ds[:, si, :],
                                                  op0=ALU.mult, op1=ALU.add), reads=[rs, pds[si]], writes=[Ss[si]])
        em.op("act", "copy", dict(out=Ob[:, sl], in_=ps_o[:, si, pr, :]), reads=[pos[si][pr]], writes=[obs])

    NSTAGE = 10
    for blk in range(NBLK + 1):
        gens = []
        if blk < NBLK:
            for si, (b, d) in enumerate(scans):
                bb = blk if d == 0 else NBLK - 1 - blk
                gens.append(precompute(si, bb, res[si][blk % 2]))
        def advance(k):
            for _ in range(k):
                for gi in range(len(gens)):
                    if gens[gi] is None:
                        continue
                    try:
                        next(gens[gi])
                    except StopIteration:
                        gens[gi] = None
        if blk == 0:
            advance(NSTAGE + 2)
            continue
        obufs = [rOb.next() for _ in range(NS)]
        per = -(-(NSTAGE + 1) // NB)
        for n in range(NB):
            for si in range(NS):
                scan_step(si, n, res[si][(blk - 1) % 2], obufs[si][0], obufs[si][1])
            advance(per)
        advance(NSTAGE + 2)
        for si, (b, d) in enumerate(scans):
            bb = (blk - 1) if d == 0 else NBLK - 1 - (blk - 1)
            g0 = b * T + bb * PB
            em.dma("sp", io["oT"][d, :, g0:g0 + PB], obufs[si][0][:], obufs[si][1], reads=[obufs[si][1]])
    fin = []
    for s in rOb.slots:
        fin.extend(s.r)
    em.wait_all("sp", fin)

    def p2c_A(b, blk):
        g0 = b * T + blk * PB
        F0, f0s = rY.next()
        F1, f1s = rY.next()
        F2, f2s = rY.next()
        em.dma("sp", F0[:], io["oT"][0, :, g0:g0 + PB], f0s, writes=[f0s])
        em.dma("sp", F1[:], io["oT"][1, :, g0:g0 + PB], f1s, writes=[f1s])
        em.dma("sp", F2[:], io["xin"](3, b, blk * PB, blk * PB + PB), f2s, writes=[f2s])
        em.op("dve", "tensor_tensor", dict(out=F0[:], in0=F0[:], in1=F1[:], op=ALU.add), reads=[f1s], writes=[f0s])
        em.op("pool", "tensor_tensor", dict(out=F1[:], in0=F0[:], in1=F0[:], op=ALU.mult), reads=[f0s], writes=[f1s])
        P, pss = rPc.next()
        em.op("pe", "matmul", dict(out=P[:, 0:W], lhsT=ones128[:], rhs=F1[:], start=True, stop=True), reads=[f1s, cs], writes=[pss])
        em.op("act", "activation", dict(out=F2[:], in_=F2[:], func=AF.Silu), writes=[f2s])
        return (g0, F0, f0s, F1, f1s, F2, f2s, P, pss)

    def p2c_B(ctx):
        g0, F0, f0s, F1, f1s, F2, f2s, P, pss = ctx
        em.op("act", "activation", dict(out=F1[:], in_=P[:, 0:W], func=AF.Sqrt, scale=1.0 / HD, bias=RMS_EPS), reads=[pss], writes=[f1s])
        em.op("dve", "reciprocal", dict(out=F1[:], in_=F1[:]), writes=[f1s])
        em.op("pool", "tensor_tensor", dict(out=F0[:], in0=F0[:], in1=F1[:], op=ALU.mult), reads=[f1s], writes=[f0s])
        O, os_ = rO.next()
        em.op("dve", "scalar_tensor_tensor", dict(out=O[:], in0=F0[:], scalar=par[:, 12:13], in1=F2[:], op0=ALU.mult, op1=ALU.mult),
              reads=[f0s, f2s, ps_], writes=[os_])
        em.dma("sp", io["gdnT"][:, g0:g0 + PB], O[:], os_, reads=[os_])

    prev = None
    for b in range(B):
        for blk in range(NBLK):
            ctx = p2c_A(b, blk)
            if prev is not None:
                p2c_B(prev)
            prev = ctx
    p2c_B(prev)
    fin = []
    for s_ in rO.slots:
        fin.extend(s_.r)
    em.wait_all("sp", fin)


def run_p2_program(xin, abc, cw, csc, nw, alog, dtb, B, T):
    NTOK = B * T
    nc = bass.Bass("TRN2", target_bir_lowering=False)
    with contextlib.ExitStack() as st:
        em = Em(nc, st)
        d_x = nc.dram_tensor("xin", [7, 128, NTOK], F32, kind="ExternalInput").ap()
        d_abc = nc.dram_tensor("abc", [NTOK, 4], F32, kind="ExternalInput").ap()
        d_cw = nc.dram_tensor("cw", [128, 9], F32, kind="ExternalInput").ap()
        d_csc = nc.dram_tensor("csc", [128, 3], F32, kind="ExternalInput").ap()
        d_nw = nc.dram_tensor("nw", [128, 1], F32, kind="ExternalInput").ap()
        d_al = nc.dram_tensor("alog", [64, 2], F32, kind="ExternalInput").ap()
        d_dt = nc.dram_tensor("dtb", [64, 2], F32, kind="ExternalInput").ap()
        d_g = nc.dram_tensor("gdnT", [128, NTOK], BF16, kind="ExternalOutput").ap()
        d_u = nc.dram_tensor("uscT", [128, NTOK], BF16, kind="ExternalOutput").ap()
        io = dict(
            xin=lambda kind, b, a, c: d_x[kind, :, b * T + a:b * T + c], abc=d_abc, cw=d_cw, csc=d_csc, nw=d_nw,
            alog=d_al, dtb=d_dt, gdnT=d_g, uscT=d_u,
            qn=nc.dram_tensor("s_qn", [128, NTOK], BF16, kind="Internal").ap(),
            kn=nc.dram_tensor("s_kn", [128, NTOK], BF16, kind="Internal").ap(),
            vv=nc.dram_tensor("s_vv", [128, NTOK], F32, kind="Internal").ap(),
            oT=nc.dram_tensor("s_oT", [2, 128, NTOK], F32, kind="Internal").ap())
        C = make_consts(em)
        build_p2(em, C, io, B, T)
        em.finalize()
    print("P2 insts", em.ninst, "sems", em.nsem, flush=True)
    n = len(xin)
    in_maps = [dict(xin=xin[c], abc=abc[c], cw=cw[c], csc=csc[c], nw=nw[c], alog=alog[c], dtb=dtb[c]) for c in range(n)]
    res = run_bass_kernel_spmd(nc, in_maps, core_ids=list(range(n)))
    return [r["gdnT"] for r in res.results], [r["uscT"] for r in res.results]


def p1_dest(io, g):
    if g < 4096:
        return io["send"][(g % 1024) // 128, g // 1024, :, :]
    g2 = g - 4128
    kind, h = 4 + g2 // 1024, (g2 % 1024) // 128
    if kind <= 6:
        return io["send"][h, kind, :, :]
    return io["gateT"][kind - 7, h * 128:(h + 1) * 128, :]


def build_p1(em, C, io, TOK):
    cs = C["slot"]
    NT = TOK // 128
    xT = sb(em, "p1xT", [128, 8, TOK], BF16)
    xTs = Slot("xT")
    rXl = Ring(em, "p1x", [128, 1024], F32, 2)
    rXb = Ring(em, "p1xb", [128, 1024], BF16, 2)
    rP = Ring(em, "p1P", [128, 512], F32, 6, psum=True)
    for t in range(NT):
        X, xs = rXl.next()
        em.dma("sp", X[:], io["x"][t * 128:(t + 1) * 128, :], xs, writes=[xs])
        Xb, xbs = rXb.next()
        em.op("act", "copy", dict(out=Xb[:], in_=X[:]), reads=[xs], writes=[xbs])
        for hf in range(2):
            P, pss = rP.next()
            for k in range(4):
                kk = hf * 4 + k
                em.op("pe", "matmul", dict(out=P[:, k * 128:(k + 1) * 128], lhsT=Xb[:, kk * 128:(kk + 1) * 128],
                                           rhs=C["idb"][:], start=True, stop=True), reads=[xbs, cs], writes=[pss], sig=(k == 3))
            em.op("dve", "tensor_copy", dict(out=xT[:, hf * 4:(hf + 1) * 4, t * 128:(t + 1) * 128], in_=r3(P[:, :], 128)),
                  reads=[pss], writes=[xTs])
    CW = 512
    rWf = Ring(em, "p1wf", [128, 8, CW], F32, 2)
    rWb = Ring(em, "p1wb", [128, 8, CW], BF16, 2)
    rSt = Ring(em, "p1st", [128, 512], F32, 4)
    wv = io["w_in"].rearrange("(kc p) c -> p kc c", p=128)
    slabs = [(c0, 512) for c0 in range(0, 4096, 512)] + [(4096, 32)] + [(c0, 512) for c0 in range(4128, 9248, 512)]
    TB = min(512, TOK)
    NTB = TOK // TB
    ev = 0
    for (c0, cw_) in slabs:
        Wf, wfs = rWf.next()
        em.dma("sp", Wf[:, :, 0:cw_], wv[:, :, c0:c0 + cw_], wfs, writes=[wfs])
        Wb, wbs = rWb.next()
        em.op("pool", "tensor_copy", dict(out=Wb[:, :, 0:cw_], in_=Wf[:, :, 0:cw_]), reads=[wfs], writes=[wbs])
        if cw_ == 32:
            for t in range(NT):
                P, pss = rP.next()
                for k in range(8):
                    em.op("pe", "matmul", dict(out=P[:, 0:32], lhsT=xT[:, k, t * 128:(t + 1) * 128], rhs=Wb[:, k, 0:32],
                                               start=(k == 0), stop=(k == 7)), reads=[xTs, wbs], writes=[pss], sig=(k == 7))
                St, sts = rSt.next()
                em.op("act", "copy", dict(out=St[:, 0:32], in_=P[:, 0:32]), reads=[pss], writes=[sts])
                em.dma("sp", io["abT"][t * 128:(t + 1) * 128, :], St[:, 0:32], sts, reads=[sts])
            continue
        for j in range(cw_ // 128):
            dst = p1_dest(io, c0 + j * 128)
            for tb in range(NTB):
                P, pss = rP.next()
                for k in range(8):
                    em.op("pe", "matmul", dict(out=P[:, 0:TB], lhsT=Wb[:, k, j * 128:(j + 1) * 128], rhs=xT[:, k, tb * TB:(tb + 1) * TB],
                                               start=(k == 0), stop=(k == 7)), reads=[xTs, wbs], writes=[pss], sig=(k == 7))
                St, sts = rSt.next()
                if ev % 2 == 0:
                    em.op("act", "copy", dict(out=St[:, 0:TB], in_=P[:, 0:TB]), reads=[pss], writes=[sts])
                else:
                    em.op("dve", "tensor_copy", dict(out=St[:, 0:TB], in_=P[:, 0:TB]), reads=[pss], writes=[sts])
                ev += 1
                em.dma("sp", dst[:, tb * TB:(tb + 1) * TB], St[:, 0:TB], sts, reads=[sts])
    fin = []
    for s in rSt.slots:
        fin.extend(s.r)
    em.wait_all("sp", fin)


def load_weight_bf16(em, Wb, src3, nk, ncols, rSt, slabc, engs=("pool", "act", "dve")):
    slots = []
    for i, c0 in enumerate(range(0, ncols, slabc)):
        Wf, wfs = rSt.next()
        ws = Slot(f"wslab{c0}")
        em.dma("sp", Wf[:, 0:nk, 0:slabc], src3[:, :, c0:c0 + slabc], wfs, writes=[wfs])
        eng = engs[i % len(engs)]
        if eng == "act":
            em.op("act", "copy", dict(out=Wb[:, :, c0:c0 + slabc], in_=Wf[:, 0:nk, 0:slabc]), reads=[wfs], writes=[ws])
        else:
            em.op(eng, "tensor_copy", dict(out=Wb[:, :, c0:c0 + slabc], in_=Wf[:, 0:nk, 0:slabc]), reads=[wfs], writes=[ws])
        slots.append(ws)
    return slots


def layer_norm(em, Rr, rs, gbc, bbc, cslot, rSm):
    Sm, sms = rSm.next()
    em.op("dve", "bn_stats", dict(out=Sm[:, 0:6], in_=Rr[:, 0:512]), reads=[rs], writes=[sms])
    em.op("dve", "bn_stats", dict(out=Sm[:, 6:12], in_=Rr[:, 512:1024]), reads=[rs], writes=[sms])
    em.op("dve", "bn_aggr", dict(out=Sm[:, 12:14], in_=Sm[:, 0:12]), writes=[sms])
    em.op("act", "activation", dict(out=Sm[:, 14:15], in_=Sm[:, 13:14], func=AF.Sqrt, bias=LN_EPS), writes=[sms])
    em.op("dve", "reciprocal", dict(out=Sm[:, 14:15], in_=Sm[:, 14:15]), writes=[sms])
    em.op("dve", "tensor_scalar", dict(out=Rr[:], in0=Rr[:], scalar1=Sm[:, 12:13], scalar2=Sm[:, 14:15],
                                       op0=ALU.subtract, op1=ALU.mult), reads=[sms], writes=[rs])
    em.op("pool", "tensor_tensor", dict(out=Rr[:], in0=Rr[:], in1=gbc[:], op=ALU.mult), reads=[cslot], writes=[rs])
    em.op("pool", "tensor_tensor", dict(out=Rr[:], in0=Rr[:], in1=bbc[:], op=ALU.add), reads=[cslot], writes=[rs])


def build_p3a(em, C, io, TOK):
    cs = C["slot"]
    rSt = Ring(em, "p3awf", [128, 8, 256], F32, 3)
    W = {}
    for nm in ("wg", "ws", "wo"):
        wt = sb(em, f"p3a_{nm}", [128, 8, 1024], BF16)
        W[nm] = (wt, load_weight_bf16(em, wt, io[nm].rearrange("(kc p) c -> p kc c", p=128), 8, 1024, rSt, 256))
    lnc = sb(em, "p3a_ln", [128, 2, 1024], F32)
    lns = Slot("ln")
    em.dma("sp", lnc[:, 0, :], io["lng"], lns, writes=[lns])
    em.dma("sp", lnc[:, 1, :], io["lnb"], lns, writes=[lns])
    TBA = min(256, TOK)
    rG = Ring(em, "p3aG", [128, 2, 8, TBA], BF16, 2)
    rGT = Ring(em, "p3aGT", [128, 2, 8, TBA], F32, 2)
    rT = Ring(em, "p3aT", [128, 2, TBA], F32, 2)
    rMx = Ring(em, "p3aMx", [128, 8, TBA], BF16, 2)
    rX = Ring(em, "p3aX", [128, 1024], F32, 3)
    rSm = Ring(em, "p3aSm", [128, 16], F32, 2)
    rP = Ring(em, "p3aP", [128, 512], F32, 6, psum=True)
    gv = io["gdnT"].rearrange("(j p) t -> p j t", p=128)
    uv = io["uscT"].rearrange("(j p) t -> p j t", p=128)
    for tb in range(TOK // TBA):
        tsl = slice(tb * TBA, (tb + 1) * TBA)
        G, gs = rG.next()
        em.dma("sp", G[:, 0, :, :], gv[:, :, tsl], gs, writes=[gs])
        em.dma("sp", G[:, 1, :, :], uv[:, :, tsl], gs, writes=[gs])
        GT, gts = rGT.next()
        for gi in range(2):
            em.dma("sp", GT[:, gi, :, :], io["gateT"][gi].rearrange("(j p) t -> p j t", p=128)[:, :, tsl], gts, writes=[gts])
        em.op("act", "activation", dict(out=GT[:], in_=GT[:], func=AF.Sigmoid), writes=[gts])
        Mx, mxs = rMx.next()
        for m in range(8):
            Ps = []
            for gi, nm in enumerate(("wg", "ws")):
                P, pss = rP.next()
                for j in range(8):
                    em.op("pe", "matmul", dict(out=P[:, 0:TBA], lhsT=W[nm][0][:, j, m * 128:(m + 1) * 128], rhs=G[:, gi, j, :],
                                               start=(j == 0), stop=(j == 7)), reads=[W[nm][1][m // 2], gs], writes=[pss], sig=(j == 7))
                Ps.append((P, pss))
            Tt, tts = rT.next()
            for gi in range(2):
                em.op("dve", "tensor_tensor", dict(out=Tt[:, gi, :], in0=GT[:, gi, m, :], in1=Ps[gi][0][:, 0:TBA], op=ALU.mult),
                      reads=[gts, Ps[gi][1]], writes=[tts])
            em.op("pool", "tensor_tensor", dict(out=Mx[:, m, :], in0=Tt[:, 0, :], in1=Tt[:, 1, :], op=ALU.add),
                  reads=[tts], writes=[mxs])
        for tile in range(TBA // 128):
            t0 = tb * TBA + tile * 128
            X, xs = rX.next()
            em.dma("sp", X[:], io["x"][t0:t0 + 128, :], xs, writes=[xs])
            for hf in range(2):
                P, pss = rP.next()
                for j in range(8):
                    em.op("pe", "matmul", dict(out=P[:, :], lhsT=Mx[:, j, tile * 128:(tile + 1) * 128],
                                               rhs=W["wo"][0][:, j, hf * 512:(hf + 1) * 512], start=(j == 0), stop=(j == 7)),
                          reads=[W["wo"][1][2 * hf], W["wo"][1][2 * hf + 1], mxs], writes=[pss], sig=(j == 7))
                em.op("dve", "scalar_tensor_tensor", dict(out=X[:, hf * 512:(hf + 1) * 512], in0=X[:, hf * 512:(hf + 1) * 512],
                                                          scalar=float(ALPHA), in1=P[:, :], op0=ALU.mult, op1=ALU.add),
                      reads=[pss], writes=[xs])
            layer_norm(em, X, xs, lnc[:, 0, :], lnc[:, 1, :], lns, rSm)
            em.dma("sp", io["x1"][t0:t0 + 128, :], X[:], xs, reads=[xs])
    fin = []
    for s in rX.slots:
        fin.extend(s.r)
    em.wait_all("sp", fin)


def build_p3b(em, C, io, TOK):
    cs = C["slot"]
    rSt = Ring(em, "p3bwf", [128, 8, 128], F32, 3)
    Wup = sb(em, "p3b_wup", [128, 8, 4096], BF16)
    Wdn = sb(em, "p3b_wdn", [128, 32, 1024], BF16)
    wus = load_weight_bf16(em, Wup, io["wup"].rearrange("(kc p) c -> p kc c", p=128), 8, 4096, rSt, 128)
    dv = io["wdn"].rearrange("(fc p) c -> p fc c", p=128)
    wds = []
    for fc in range(32):
        Wf, wfs = rSt.next()
        Wf2 = Wf[:].rearrange("p a b -> p (a b)")
        ws = Slot(f"wdn{fc}")
        em.dma("sp", Wf2, dv[:, fc, :], wfs, writes=[wfs])
        if fc % 3 == 1:
            em.op("act", "copy", dict(out=Wdn[:, fc, :], in_=Wf2), reads=[wfs], writes=[ws])
        else:
            em.op("pool" if fc % 3 == 0 else "dve", "tensor_copy", dict(out=Wdn[:, fc, :], in_=Wf2), reads=[wfs], writes=[ws])
        wds.append(ws)
    cst = sb(em, "p3b_c", [128, 3, 1024], F32)
    bup = sb(em, "p3b_bup", [128, 32], F32)
    cs2 = Slot("p3bc")
    em.dma("sp", cst[:, 0, :], io["lng"], cs2, writes=[cs2])
    em.dma("sp", cst[:, 1, :], io["lnb"], cs2, writes=[cs2])
    em.dma("sp", cst[:, 2, :], io["bdn"], cs2, writes=[cs2])
    em.dma("sp", bup[:], io["bup"], cs2, writes=[cs2])
    TBB = min(256, TOK)
    NTL = TBB // 128
    rX = Ring(em, "p3bX", [128, 1024], F32, 2 * NTL)
    rXb = Ring(em, "p3bXb", [128, 1024], BF16, 1)
    rXT = Ring(em, "p3bXT", [128, 8, TBB], BF16, 1)
    rH = Ring(em, "p3bH", [128, TBB], F32, 2)
    rHT = Ring(em, "p3bHT", [128, 32, TBB], BF16, 1)
    rSm = Ring(em, "p3bSm", [128, 16], F32, 2)
    rP = Ring(em, "p3bP", [128, 512], F32, 6, psum=True)
    for tb in range(TOK // TBB):
        XT, xts = rXT.next()
        tiles = []
        for tile in range(NTL):
            t0 = tb * TBB + tile * 128
            X, xs = rX.next()
            em.dma("sp", X[:], io["x1"][t0:t0 + 128, :], xs, writes=[xs])
            tiles.append((X, xs, t0))
            Xb, xbs = rXb.next()
            em.op("act", "copy", dict(out=Xb[:], in_=X[:]), reads=[xs], writes=[xbs])
            for hf in range(2):
                P, pss = rP.next()
                for k in range(4):
                    kk = hf * 4 + k
                    em.op("pe", "matmul", dict(out=P[:, k * 128:(k + 1) * 128], lhsT=Xb[:, kk * 128:(kk + 1) * 128],
                                               rhs=C["idb"][:], start=True, stop=True), reads=[xbs, cs], writes=[pss], sig=(k == 3))
                em.op("dve", "tensor_copy", dict(out=XT[:, hf * 4:(hf + 1) * 4, tile * 128:(tile + 1) * 128], in_=r3(P[:, :], 128)),
                      reads=[pss], writes=[xts])
        HT, hts = rHT.next()
        for f in range(32):
            P, pss = rP.next()
            for k in range(8):
                em.op("pe", "matmul", dict(out=P[:, 0:TBB], lhsT=Wup[:, k, f * 128:(f + 1) * 128], rhs=XT[:, k, :],
                                           start=(k == 0), stop=(k == 7)), reads=[wus[f], xts], writes=[pss], sig=(k == 7))
            H, hs = rH.next()
            em.op("act", "activation", dict(out=H[:], in_=P[:, 0:TBB], func=AF.Relu, bias=bup[:, f:f + 1]),
                  reads=[pss, cs2], writes=[hs])
            em.op("pool", "tensor_tensor", dict(out=HT[:, f, :], in0=H[:], in1=H[:], op=ALU.mult), reads=[hs], writes=[hts])
        for tile in range(NTL):
            X, xs, t0 = tiles[tile]
            for hf in range(2):
                P, pss = rP.next()
                for f in range(32):
                    em.op("pe", "matmul", dict(out=P[:, :], lhsT=HT[:, f, tile * 128:(tile + 1) * 128],
                                               rhs=Wdn[:, f, hf * 512:(hf + 1) * 512], start=(f == 0), stop=(f == 31)),
                          reads=[wds[f], hts], writes=[pss], sig=(f == 31))
                em.op("dve", "scalar_tensor_tensor", dict(out=X[:, hf * 512:(hf + 1) * 512], in0=X[:, hf * 512:(hf + 1) * 512],
                                                          scalar=float(ALPHA), in1=P[:, :], op0=ALU.mult, op1=ALU.add),
                      reads=[pss], writes=[xs])
            em.op("pool", "tensor_tensor", dict(out=X[:], in0=X[:], in1=cst[:, 2, :], op=ALU.add), reads=[cs2], writes=[xs])
            layer_norm(em, X, xs, cst[:, 0, :], cst[:, 1, :], cs2, rSm)
            em.dma("sp", io["x2"][t0:t0 + 128, :], X[:], xs, reads=[xs])
    fin = []
    for s in rX.slots:
        fin.extend(s.r)
    em.wait_all("sp", fin)


def _launch(build, ins, outs, internals, arrays, ncores=8):
    nc = bass.Bass("TRN2", target_bir_lowering=False)
    with contextlib.ExitStack() as st:
        em = Em(nc, st)
        io = {}
        for nm, (shape, dt) in ins.items():
            io[nm] = nc.dram_tensor(nm, list(shape), dt, kind="ExternalInput").ap()
        for nm, (shape, dt) in outs.items():
            io[nm] = nc.dram_tensor(nm, list(shape), dt, kind="ExternalOutput").ap()
        for nm, (shape, dt) in internals.items():
            io[nm] = nc.dram_tensor(nm, list(shape), dt, kind="Internal").ap()
        C = make_consts(em)
        build(em, C, io)
        em.finalize()
    if _DBG.get("trace"):
        res = run_bass_kernel_spmd(nc, arrays, core_ids=list(range(ncores)), trace=True)
        _DBG["ns"] = res.exec_time_ns
        _DBG["res"] = res
        _DBG["ninst"] = dict(em.ninst)
    else:
        res = run_bass_kernel_spmd(nc, arrays, core_ids=list(range(ncores)))
    return [{nm: r[nm] for nm in outs} for r in res.results]


def run_p1(x_sh, w_in_l, TOK):
    ins = dict(x=((TOK, 1024), F32), w_in=((1024, IN_COLS), F32))
    outs = dict(send=((8, 7, 128, TOK), F32), abT=((TOK, 32), F32), gateT=((2, 1024, TOK), F32))
    return _launch(lambda em, C, io: build_p1(em, C, io, TOK), ins, outs, {}, [dict(x=x_sh[c], w_in=w_in_l) for c in range(8)])


def run_p2(xin, abc, cw, csc, nw, alog, dtb, B, T):
    NTOK = B * T
    ins = dict(xin=((7, 128, NTOK), F32), abc=((NTOK, 4), F32), cw=((128, 9), F32), csc=((128, 3), F32),
               nw=((128, 1), F32), alog=((64, 2), F32), dtb=((64, 2), F32))
    outs = dict(gdnT=((128, NTOK), BF16), uscT=((128, NTOK), BF16))
    internals = dict(qn=((128, NTOK), BF16), kn=((128, NTOK), BF16), vv=((128, NTOK), F32), oT=((2, 128, NTOK), F32))

    def build(em, C, io):
        d_x = io["xin"]
        io["xin"] = lambda kind, b, a, c: d_x[kind, :, b * T + a:b * T + c]
        build_p2(em, C, io, B, T)
    arrays = [dict(xin=xin[c], abc=abc[c], cw=cw[c], csc=csc[c], nw=nw[c], alog=alog[c], dtb=dtb[c]) for c in range(8)]
    return _launch(build, ins, outs, internals, arrays)


def run_p3a(gdnT, uscT, gateT, x_sh, wg, ws, wo, lng, lnb, TOK):
    ins = dict(gdnT=((1024, TOK), BF16), uscT=((1024, TOK), BF16), gateT=((2, 1024, TOK), F32), x=((TOK, 1024), F32),
               wg=((1024, 1024), F32), ws=((1024, 1024), F32), wo=((1024, 1024), F32), lng=((128, 1024), F32), lnb=((128, 1024), F32))
    outs = dict(x1=((TOK, 1024), F32))
    arrays = [dict(gdnT=gdnT[c], uscT=uscT[c], gateT=gateT[c], x=x_sh[c], wg=wg, ws=ws, wo=wo, lng=lng, lnb=lnb) for c in range(8)]
    return _launch(lambda em, C, io: build_p3a(em, C, io, TOK), ins, outs, {}, arrays)


def run_p3b(x1, wup, bup, wdn, bdn, lng, lnb, TOK):
    ins = dict(x1=((TOK, 1024), F32), wup=((1024, 4096), F32), bup=((128, 32), F32), wdn=((4096, 1024), F32),
               bdn=((128, 1024), F32), lng=((128, 1024), F32), lnb=((128, 1024), F32))
    outs = dict(x2=((TOK, 1024), F32))
    arrays = [dict(x1=x1[c], wup=wup, bup=bup, wdn=wdn, bdn=bdn, lng=lng, lnb=lnb) for c in range(8)]
    return _launch(lambda em, C, io: build_p3b(em, C, io, TOK), ins, outs, {}, arrays)


def bcast128(v):
    return np.ascontiguousarray(np.broadcast_to(np.asarray(v, np.float32)[None, :], (128, v.shape[0])))


def forward_unfused(inp, B, T, depth):
    TOK = B * T // 8
    x = np.ascontiguousarray(inp["x"], dtype=np.float32).reshape(B * T, 1024)
    x_sh = [np.ascontiguousarray(x[c * TOK:(c + 1) * TOK]) for c in range(8)]
    for l in range(depth):
        o1 = run_p1(x_sh, np.ascontiguousarray(inp["w_in"][l]), TOK)
        xin = [np.ascontiguousarray(np.concatenate([o1[c]["send"][h] for c in range(8)], axis=2)) for h in range(8)]
        abT = np.concatenate([o1[c]["abT"] for c in range(8)], axis=0)
        abc = [np.ascontiguousarray(abT[:, [h, 8 + h, 16 + h, 24 + h]]) for h in range(8)]
        cq = inp["conv_qkv"][l]
        cw = [np.ascontiguousarray(np.concatenate([cq[:, k * 1024 + h * 128:k * 1024 + (h + 1) * 128].T for k in range(3)], axis=1))
              for h in range(8)]
        csc = [np.ascontiguousarray(inp["conv_sc"][l][:, h * 128:(h + 1) * 128].T) for h in range(8)]
        nw = [np.ascontiguousarray(inp["gdn_norm_w"][l].reshape(128, 1))] * 8
        alog = [np.ascontiguousarray(np.broadcast_to(inp["a_log"][l][:, h][None, :], (64, 2))) for h in range(8)]
        dtb = [np.ascontiguousarray(np.broadcast_to(inp["dt_bias"][l][:, h][None, :], (64, 2))) for h in range(8)]
        o2 = run_p2(xin, abc, cw, csc, nw, alog, dtb, B, T)
        gd = np.concatenate([o2[h]["gdnT"] for h in range(8)], axis=0)
        us = np.concatenate([o2[h]["uscT"] for h in range(8)], axis=0)
        gdn_sh = [np.ascontiguousarray(gd[:, c * TOK:(c + 1) * TOK]) for c in range(8)]
        usc_sh = [np.ascontiguousarray(us[:, c * TOK:(c + 1) * TOK]) for c in range(8)]
        gate_sh = [o1[c]["gateT"] for c in range(8)]
        o3 = run_p3a(gdn_sh, usc_sh, gate_sh, x_sh, np.ascontiguousarray(inp["w_o_gdn"][l]), np.ascontiguousarray(inp["w_o_sc"][l]),
                     np.ascontiguousarray(inp["w_out"][l]), bcast128(inp["ln1_g"][l]), bcast128(inp["ln1_b"][l]), TOK)
        x1 = [o3[c]["x1"] for c in range(8)]
        o4 = run_p3b(x1, np.ascontiguousarray(inp["w_up"][l]), np.ascontiguousarray(inp["b_up"][l].reshape(32, 128).T),
                     np.ascontiguousarray(inp["w_down"][l]), bcast128(inp["b_down"][l]), bcast128(inp["ln2_g"][l]),
                     bcast128(inp["ln2_b"][l]), TOK)
        x_sh = [o4[c]["x2"] for c in range(8)]
    return np.concatenate(x_sh, axis=0).reshape(B, T, 1024).astype(np.float32)


def kernel(**inputs):
    inp = {k: np.asarray(v) for k, v in inputs.items()}
    B, T = inp["x"].shape[0], inp["x"].shape[1]
    return forward_unfused(inp, B, T, inp["w_in"].shape[0])
```
